# Optimizing a Trainium2 kernel written in Bass

```python
import math
import jax, jax.numpy as jnp
from jax import lax
import numpy as np

D_MODEL = 1024
BATCH = 1
SEQ = 16384
DEPTH = 1
DEC_BATCH = 16
DEC_SEQ = 4096
PAST_LEN = 128

N_HEADS = 8
QK_NOPE_DIM = 128
QK_ROPE_DIM = 64
QK_HEAD_DIM = QK_NOPE_DIM + QK_ROPE_DIM
V_HEAD_DIM = 128
Q_LORA_RANK = 384
KV_LORA_RANK = 256
ROPE_BASE = 10000.0
ATTN_SCALE = QK_HEAD_DIM ** -0.5
Q_BLOCK = 128
MLA_OUT = N_HEADS * V_HEAD_DIM
CHUNK = 128
SGU_GROUPS = 8
SGU_WIDTH = D_MODEL
SGU_GROUP_DIM = SGU_WIDTH // SGU_GROUPS
D_FF = 4 * D_MODEL
NORM_EPS = 1e-6
IN_COLS = Q_LORA_RANK + KV_LORA_RANK + QK_ROPE_DIM + 2 * SGU_WIDTH + 2 * D_MODEL
SPLITS = (
    Q_LORA_RANK,
    Q_LORA_RANK + KV_LORA_RANK,
    Q_LORA_RANK + KV_LORA_RANK + QK_ROPE_DIM,
    Q_LORA_RANK + KV_LORA_RANK + QK_ROPE_DIM + 2 * SGU_WIDTH,
)

kernel_name = "hybrid_mla_sgu_gated_encoder"


def _rmsnorm(x, g):
    xf = x.astype(jnp.float32)
    y = xf * lax.rsqrt(jnp.mean(xf * xf, axis=-1, keepdims=True) + NORM_EPS)
    return (y * g.astype(jnp.float32)).astype(x.dtype)


def _rope_tables(s):
    pos = jnp.arange(s, dtype=jnp.float32)
    inv = ROPE_BASE ** (-jnp.arange(0, QK_ROPE_DIM, 2, dtype=jnp.float32) / QK_ROPE_DIM)
    ang = pos[:, None] * inv[None, :]
    ang = jnp.concatenate([ang, ang], axis=-1)
    return jnp.cos(ang), jnp.sin(ang)


def _apply_rope(x, cos, sin):
    xf = x.astype(jnp.float32)
    half = QK_ROPE_DIM // 2
    rot = jnp.concatenate([-xf[..., half:], xf[..., :half]], axis=-1)
    return (xf * cos + rot * sin).astype(x.dtype)


def _dense_attention(q, k, v):
    b, s, h, dk = q.shape
    nb = s // Q_BLOCK
    qb = q.reshape(b, nb, Q_BLOCK, h, dk).transpose(1, 0, 2, 3, 4)

    def one_block(q_blk):
        sc = jnp.einsum('bqhd,bkhd->bhqk', q_blk, k).astype(jnp.float32) * ATTN_SCALE
        p = jax.nn.softmax(sc, axis=-1).astype(v.dtype)
        return jnp.einsum('bhqk,bkhd->bqhd', p, v)

    o = lax.map(one_block, qb)
    return o.transpose(1, 0, 2, 3, 4).reshape(b, s, h, V_HEAD_DIM)


def _block(x, norm_mix_g, w_in, q_norm_g, w_uq, kv_norm_g, w_ukv, sgu_norm_g, w_s, b_s,
           w_o, norm_ffn_g, w_ff1, w_ff2):
    b, s, _ = x.shape
    hn = _rmsnorm(x, norm_mix_g)
    z = hn @ w_in
    c_q, c_kv, k_rope, uv, gates = jnp.split(z, SPLITS, axis=-1)

    cos, sin = _rope_tables(s)
    q = (_rmsnorm(c_q, q_norm_g) @ w_uq).reshape(b, s, N_HEADS, QK_HEAD_DIM)
    q_nope = q[..., :QK_NOPE_DIM]
    q_rope = _apply_rope(q[..., QK_NOPE_DIM:], cos[:, None, :], sin[:, None, :])
    kv = (_rmsnorm(c_kv, kv_norm_g) @ w_ukv).reshape(b, s, N_HEADS, QK_NOPE_DIM + V_HEAD_DIM)
    k_nope = kv[..., :QK_NOPE_DIM]
    v = kv[..., QK_NOPE_DIM:]
    k_rope = _apply_rope(k_rope, cos, sin)
    q_full = jnp.concatenate([q_nope, q_rope], axis=-1)
    k_full = jnp.concatenate(
        [k_nope, jnp.broadcast_to(k_rope[:, :, None, :], (b, s, N_HEADS, QK_ROPE_DIM))], axis=-1)
    o_a = _dense_attention(q_full, k_full, v).reshape(b, s, MLA_OUT)

    uv = jax.nn.gelu(uv)
    u, vs = jnp.split(uv, 2, axis=-1)
    vs = _rmsnorm(vs, sgu_norm_g).reshape(b, s // CHUNK, CHUNK, SGU_GROUPS, SGU_GROUP_DIM)
    vs = jnp.einsum('gpq,bnqgc->bnpgc', w_s, vs) + b_s.T[None, None, :, :, None]
    o_b = u * vs.reshape(b, s, SGU_WIDTH)

    g_a, g_b = jnp.split(gates, 2, axis=-1)
    merged = jax.nn.sigmoid(g_a) * o_a + jax.nn.sigmoid(g_b) * o_b
    x = x + merged @ w_o

    hf = _rmsnorm(x, norm_ffn_g)
    x = x + jnp.square(jax.nn.relu(hf @ w_ff1)) @ w_ff2
    return x


def setup_inputs(seed: int = 0) -> dict:
    key = jax.random.key(seed)
    ks = jax.random.split(key, 16)
    f32 = jnp.float32

    def nrm(k, shape, scale):
        return jax.random.normal(k, shape, f32) * scale

    def gain(k, shape):
        return 1.0 + 0.01 * jax.random.normal(k, shape, f32)

    return {
        "x_prompt": jax.random.normal(ks[0], (BATCH, SEQ, D_MODEL), f32),
        "x_sample": jax.random.normal(ks[1], (DEC_BATCH, DEC_SEQ, D_MODEL), f32),
        "norm_mix_g": gain(ks[2], (DEPTH, D_MODEL)),
        "w_in": nrm(ks[3], (DEPTH, D_MODEL, IN_COLS), D_MODEL ** -0.5),
        "q_norm_g": gain(ks[4], (DEPTH, Q_LORA_RANK)),
        "w_uq": nrm(ks[5], (DEPTH, Q_LORA_RANK, N_HEADS * QK_HEAD_DIM), Q_LORA_RANK ** -0.5),
        "kv_norm_g": gain(ks[6], (DEPTH, KV_LORA_RANK)),
        "w_ukv": nrm(ks[7], (DEPTH, KV_LORA_RANK, N_HEADS * (QK_NOPE_DIM + V_HEAD_DIM)), KV_LORA_RANK ** -0.5),
        "sgu_norm_g": gain(ks[8], (DEPTH, SGU_WIDTH)),
        "w_s": nrm(ks[9], (DEPTH, SGU_GROUPS, CHUNK, CHUNK), CHUNK ** -0.5),
        "b_s": 1.0 + 0.02 * jax.random.normal(ks[10], (DEPTH, SGU_GROUPS, CHUNK), f32),
        "w_o": nrm(ks[11], (DEPTH, D_MODEL, D_MODEL), D_MODEL ** -0.5),
        "norm_ffn_g": gain(ks[12], (DEPTH, D_MODEL)),
        "w_ff1": nrm(ks[13], (DEPTH, D_MODEL, D_FF), D_MODEL ** -0.5),
        "w_ff2": nrm(ks[14], (DEPTH, D_FF, D_MODEL), D_FF ** -0.5),
        "final_norm_g": gain(ks[15], (D_MODEL,)),
    }


def reference(x_prompt, x_sample, norm_mix_g, w_in, q_norm_g, w_uq, kv_norm_g, w_ukv,
              sgu_norm_g, w_s, b_s, w_o, norm_ffn_g, w_ff1, w_ff2, final_norm_g):
    def run(x):
        for l in range(DEPTH):
            x = _block(x, norm_mix_g[l], w_in[l], q_norm_g[l], w_uq[l], kv_norm_g[l], w_ukv[l],
                       sgu_norm_g[l], w_s[l], b_s[l], w_o[l], norm_ffn_g[l], w_ff1[l], w_ff2[l])
        return _rmsnorm(x, final_norm_g)

    y_prompt = run(x_prompt)
    y_sample = run(x_sample)
    return (y_prompt, y_sample)
```

```python
import numpy as np
from contextlib import ExitStack
import concourse.bass as bass
import concourse.mybir as mybir
from concourse.bass_utils import run_bass_kernel_spmd

F32 = mybir.dt.float32
BF16 = mybir.dt.bfloat16
ALU = mybir.AluOpType
AF = mybir.ActivationFunctionType

D = 1024
NH = 8
QL = 384
KVL = 256
RD = 64
DFF = 4096
INC = 4800
QKD = 192
SCALE = float(QKD ** -0.5)
EPS = 1e-6
ROPE_BASE = 10000.0
GC1 = 0.044715
GC2 = 1.5957691216057308
MT = 512
NQ = 1024
NR = 2048
KC = 1024
U0, V0, GA0, GB0 = 704, 1728, 2752, 3776


class DSem:
    def __init__(self, sem):
        self.sem = sem
        self.n = 0


class Eng:
    def __init__(self, name, sem):
        self.name = name
        self.sem = sem
        self.n = 0
        self.q = []
        self.waited = {}

    def _filter(self, waits):
        out = []
        stack = list(waits) if isinstance(waits, (list, tuple)) and not _is_tok(waits) else [waits]
        while stack:
            w = stack.pop()
            if w is None:
                continue
            if _is_tok(w):
                s, v = w
                k = id(s)
                if self.waited.get(k, 0) >= v:
                    continue
                self.waited[k] = v
                out.append((s, v))
            else:
                stack.extend(w)
        return out

    def emit(self, fn, waits=(), sig=True):
        ws = self._filter(waits)
        if sig:
            self.n += 1
        self.q.append((fn, ws, self.sem if sig else None, 1))
        return (self.sem, self.n)

    def dma(self, out, in_, dsem, waits=()):
        ws = self._filter(waits)
        dsem.n += 16
        self.q.append((_f_dma(out, in_), ws, dsem.sem, 16))
        return (dsem.sem, dsem.n)

    def wait_only(self, waits):
        ws = self._filter(waits)
        if ws:
            self.q.append((None, ws, None, 0))

    def tok(self):
        return (self.sem, self.n) if self.n > 0 else None

    def replay(self, e):
        for fn, ws, sem, inc in self.q:
            for s, v in ws:
                e.wait_ge(s, v)
            if fn is None:
                continue
            ins = fn(e)
            if sem is not None:
                ins.then_inc(sem, inc)


def _is_tok(w):
    return isinstance(w, tuple) and len(w) == 2 and isinstance(w[1], int)


def _f_dma(out, in_):
    return lambda e: e.dma_start(out=out, in_=in_)


def _f_mm(out, lhsT, rhs, start, stop):
    return lambda e: e.matmul(out, lhsT=lhsT, rhs=rhs, start=start, stop=stop)


def _f_tr(out, in_, ident):
    return lambda e: e.transpose(out=out, in_=in_, identity=ident)


def _f_act(out, in_, func, scale=1.0, bias=0.0):
    return lambda e: e.activation(out=out, in_=in_, func=func, scale=scale, bias=bias)


def _f_act_acc(out, in_, func, accum_out):
    return lambda e: e.activation(out=out, in_=in_, func=func, accum_out=accum_out)


def _f_copy(out, in_):
    return lambda e: e.tensor_copy(out=out, in_=in_)


def _f_tt(out, in0, in1, op):
    return lambda e: e.tensor_tensor(out=out, in0=in0, in1=in1, op=op)


def _f_ts(out, in0, s1, s2, op0, op1):
    return lambda e: e.tensor_scalar(out=out, in0=in0, scalar1=s1, scalar2=s2, op0=op0, op1=op1)


def _f_stt(out, in0, scalar, in1, op0, op1, accum_out=None):
    if accum_out is None:
        return lambda e: e.scalar_tensor_tensor(out=out, in0=in0, scalar=scalar, in1=in1, op0=op0, op1=op1)
    return lambda e: e.scalar_tensor_tensor(out=out, in0=in0, scalar=scalar, in1=in1, op0=op0, op1=op1,
                                            accum_out=accum_out)


def _f_recip(out, in_):
    return lambda e: e.reciprocal(out=out, in_=in_)


def _f_memset(ap, v):
    return lambda e: e.memset(ap, v)


class Cfg:
    def __init__(self, nseq, ss, sp, nqp):
        self.nseq = nseq
        self.ss = ss
        self.sp = sp
        self.nqp = nqp


FULL = Cfg(2, 4096, 16384, 2048)


def build(cfg):
    nc = bass.Bass("TRN2", target_bir_lowering=False)
    SMAX = max(cfg.ss, cfg.sp)

    def din(name, shape, dt=F32):
        return nc.dram_tensor(name, list(shape), dt, kind="ExternalInput").ap()

    def dscr(name, shape, dt=BF16):
        return nc.dram_tensor(name, list(shape), dt).ap()

    xs = din("xs", [cfg.nseq * cfg.ss, D])
    xp = din("xp", [cfg.sp, D])
    xq = din("xq", [cfg.nqp, D])
    cos_all = din("cos_all", [RD, SMAX])
    sin_all = din("sin_all", [RD, SMAX])
    cos_q = din("cos_q", [RD, cfg.nqp])
    sin_q = din("sin_q", [RD, cfg.nqp])
    norm_mix_g = din("norm_mix_g", [D])
    w_in = din("w_in", [D, INC])
    q_norm_g = din("q_norm_g", [QL])
    w_uq = din("w_uq", [QL, NH * QKD])
    kv_norm_g = din("kv_norm_g", [KVL])
    w_ukv = din("w_ukv", [KVL, NH * 256])
    sgu_norm_g = din("sgu_norm_g", [D])
    w_s = din("w_s", [8, 128, 128])
    b_s = din("b_s", [8 * 128])
    w_o = din("w_o", [D, D])
    norm_ffn_g = din("norm_ffn_g", [D])
    w_ff1 = din("w_ff1", [D, DFF])
    w_ff2 = din("w_ff2", [DFF, D])
    final_norm_g = din("final_norm_g", [D])
    ys = nc.dram_tensor("ys", [cfg.nseq * cfg.ss, D], F32, kind="ExternalOutput").ap()
    yq = nc.dram_tensor("yq", [cfg.nqp, D], F32, kind="ExternalOutput").ap()

    win_bf = dscr("win_bf", [D, INC])
    wlat_bf = dscr("wlat_bf", [D, 768])
    wuq_bf = dscr("wuq_bf", [QL, NH * QKD])
    wuqp_bf = dscr("wuqp_bf", [QL, NH * RD])
    wuqr_d = dscr("wuqr_d", [128, 3, NH, 128])
    wuqpr_d = dscr("wuqpr_d", [128, 3, NH, 128])
    wuk_bf = dscr("wuk_bf", [KVL, NH * 128])
    wuv_bf = dscr("wuv_bf", [KVL, NH * 128])
    wo_bf = dscr("wo_bf", [D, D])
    wff1_bf = dscr("wff1_bf", [D, DFF])
    wff2_bf = dscr("wff2_bf", [DFF, D])

    jobs = []
    for s in range(cfg.nseq):
        jobs.append(dict(name=f"s{s}", xk=xs[s * cfg.ss:(s + 1) * cfg.ss, :], S=cfg.ss, xqr=None, nq=cfg.ss,
                         y=ys[s * cfg.ss:(s + 1) * cfg.ss, :], cosk=cos_all, sink=sin_all,
                         cosq=cos_all, sinq=sin_all))
    jobs.append(dict(name="p", xk=xp, S=cfg.sp, xqr=xq, nq=cfg.nqp, y=yq, cosk=cos_all, sink=sin_all,
                     cosq=cos_q, sinq=sin_q))
    for jb in jobs:
        S = jb["S"]
        jb["kT"] = dscr("kT_" + jb["name"], [NH, 128, S])
        jb["krT"] = dscr("krT_" + jb["name"], [RD, S])
        jb["v"] = dscr("v_" + jb["name"], [NH, S // KC, 128, KC // 128, 128])
        jb["cq"] = dscr("cq_" + jb["name"], [3, 128, jb["nq"]])

    with ExitStack() as st:
        def sb(name, shape, dt):
            return st.enter_context(nc.sbuf_tensor(name, list(shape), dt))

        def mksem(name):
            return st.enter_context(nc.semaphore(name))

        PE = Eng("pe", mksem("s_pe"))
        ACT = Eng("act", mksem("s_act"))
        DVE = Eng("dve", mksem("s_dve"))
        POOL = Eng("pool", mksem("s_pool"))
        SP = Eng("sp", mksem("s_sp"))
        ENGS = [PE, ACT, DVE, POOL, SP]
        dsem_pool = {"hw": [], "sw": []}
        dsem_idx = {"hw": 0, "sw": 0}

        def new_dsem(kind="hw"):
            pool = dsem_pool[kind]
            if dsem_idx[kind] == len(pool):
                pool.append(DSem(mksem(f"d{kind}{len(pool)}")))
            d = pool[dsem_idx[kind]]
            dsem_idx[kind] += 1
            return d

        ps = st.enter_context(nc.psum_tensor("ps", [128, 8, 512], F32))

        def bank(b):
            return ps[:, b, :]

        def bank_bf(b):
            return ps[:, b, :].bitcast(BF16)

        ident = sb("ident", [128, 128], BF16)
        identf = sb("identf", [128, 128], F32)
        ones_bf = sb("ones_bf", [128, 128], BF16)
        ones_f = sb("ones_f", [128, 128], F32)
        mhalf = sb("mhalf", [128, 4], F32)
        eps_col = sb("eps_col", [128, 1], F32)
        g_mix = sb("g_mix", [128, D], F32)
        g_ffn = sb("g_ffn", [128, D], F32)
        g_fin = sb("g_fin", [128, D], F32)
        g_sgu = sb("g_sgu", [128, D], F32)
        bs_bc = sb("bs_bc", [128, 8, 128], F32)
        wsT = sb("wsT", [128, 8, 128], BF16)
        gq_col = sb("gq_col", [128, 3], F32)
        gkv_col = sb("gkv_col", [128, 2], F32)
        ss4 = sb("ss4", [128, 4], F32)
        t4 = sb("t4", [128, 4], F32)
        rstd4 = sb("rstd4", [128, 4], F32)
        oaT = sb("oaT", [128, NH, NR], BF16)
        ARENA = 147 * 1024 // 2
        arena = sb("arena", [128, ARENA], BF16)

        XT_OFF = ARENA * 2 - 2 * 4 * D * 4
        x_sem_p = [DSem(mksem("xs0")), DSem(mksem("xs1"))]
        y_sem_p = [DSem(mksem("ys0")), DSem(mksem("ys1"))]
        xt_free_p = [None, None]
        x_pref = {}

        class Bump:
            def __init__(self):
                self.off = 0

            def get(self, shape, dt, parts=128):
                n = 1
                for s_ in shape[1:]:
                    n *= s_
                nb = n * (4 if dt == F32 else 2)
                nb = (nb + 63) // 64 * 64
                a = self.off // 2
                self.off += nb
                assert self.off <= ARENA * 2, ("arena overflow", self.off)
                v = arena[0:shape[0], a:a + nb // 2]
                if dt == F32:
                    v = v.bitcast(F32)
                v = v[:, 0:n]
                if len(shape) == 3:
                    v = v.rearrange("p (a b) -> p a b", b=shape[2])
                elif len(shape) == 4:
                    v = v.rearrange("p (a b c) -> p a b c", b=shape[2], c=shape[3])
                return v

        bank_free = {b: None for b in range(8)}
        bank_order = list(range(8))
        bank_busy = set()

        def balloc(allowed=None):
            for b in bank_order:
                if b in bank_busy:
                    continue
                if allowed is not None and b not in allowed:
                    continue
                bank_order.remove(b)
                bank_order.append(b)
                bank_busy.add(b)
                return b, bank_free[b]
            raise RuntimeError("no free psum bank")

        def brelease(b, tok):
            bank_free[b] = tok
            bank_busy.discard(b)

        def barrier(extra=()):
            toks = [e.tok() for e in ENGS] + list(extra)
            for e in ENGS:
                e.wait_only(toks)
            dsem_idx["hw"] = 0
            dsem_idx["sw"] = 0

        def mm_group(out, pairs, waits=(), first_start=True, last_stop=True):
            n = len(pairs)
            tok = None
            for i, (l, r) in enumerate(pairs):
                tok = PE.emit(_f_mm(out, l, r, (i == 0) and first_start, (i == n - 1) and last_stop),
                              waits=waits if i == 0 else (), sig=(i == n - 1))
            return tok

        pend_store = []
        c_tok = []
        bp = Bump()
        NCS = 4
        stf = [bp.get([128, INC], F32) for _ in range(NCS)]
        stb = [bp.get([128, INC], BF16) for _ in range(NCS)]
        wsf = bp.get([128, 8, 128], F32)
        ld_sem = [new_dsem() for _ in range(NCS)]
        stq_sem = [new_dsem("sw") for _ in range(NCS)]
        csem = new_dsem()

        c_tok.append(SP.dma(g_mix[:], norm_mix_g.partition_broadcast(128), csem))
        c_tok.append(SP.dma(g_ffn[:], norm_ffn_g.partition_broadcast(128), csem))
        c_tok.append(SP.dma(g_fin[:], final_norm_g.partition_broadcast(128), csem))
        c_tok.append(SP.dma(g_sgu[:], sgu_norm_g.partition_broadcast(128), csem))
        c_tok.append(SP.dma(bs_bc[:].rearrange("p a b -> p (a b)"), b_s.partition_broadcast(128), csem))
        for k in range(3):
            c_tok.append(SP.dma(gq_col[:, k:k + 1], q_norm_g[k * 128:(k + 1) * 128].rearrange("(p o) -> p o", o=1),
                                csem))
        for k in range(2):
            c_tok.append(SP.dma(gkv_col[:, k:k + 1], kv_norm_g[k * 128:(k + 1) * 128].rearrange("(p o) -> p o", o=1),
                                csem))
        c_tok.append(SP.dma(wsf, w_s.rearrange("g p q -> p g q"), csem))
        const_ready = c_tok[-1]

        i0 = POOL.emit(_f_memset(identf[:], 0.0))
        i1 = POOL.emit(lambda e: e.affine_select(out=identf[:], in_=identf[:], compare_op=ALU.not_equal, fill=1.0,
                                                 base=0, pattern=[[-1, 128]], channel_multiplier=1), waits=[i0])
        i2 = POOL.emit(_f_copy(ident[:], identf[:]), waits=[i1])
        POOL.emit(_f_memset(ones_bf[:], 1.0))
        POOL.emit(_f_memset(ones_f[:], 1.0))
        POOL.emit(_f_memset(mhalf[:], -0.5))
        POOL.emit(_f_memset(eps_col[:], EPS))
        pool_consts = POOL.emit(_f_memset(ss4[:], 0.0))
        for half in range(2):
            b, bw = balloc()
            tk = None
            for gg in range(4):
                g = half * 4 + gg
                tk = PE.emit(_f_tr(bank(b)[:, gg * 128:(gg + 1) * 128], wsf[:, g, :], identf[:]),
                             waits=[const_ready, i1, bw])
            tk2 = ACT.emit(_f_act(wsT[:, half * 4:(half + 1) * 4, :].rearrange("p a b -> p (a b)"), bank(b),
                                  AF.Copy), waits=[tk])
            brelease(b, tk2)
        ws_ready = ACT.tok()

        conv_i = [0]
        slot_free = [None] * NCS
        slot_conv = [None] * NCS
        conv_engs = [DVE, ACT]

        def convert(src, ncols, stores, c3=None):
            i = conv_i[0]
            conv_i[0] += 1
            s = i % NCS
            dstv = stf[s][:, 0:ncols]
            if c3 is not None:
                dstv = dstv.rearrange("p (a c) -> p a c", c=c3)
            tl = SP.dma(dstv, src, ld_sem[s], waits=[slot_conv[s]])
            eng = conv_engs[i % 2]
            if eng is ACT:
                tcv = ACT.emit(_f_act(stb[s][:, 0:ncols], stf[s][:, 0:ncols], AF.Copy), waits=[tl, slot_free[s]])
            else:
                tcv = eng.emit(_f_copy(stb[s][:, 0:ncols], stf[s][:, 0:ncols]), waits=[tl, slot_free[s]])
            slot_conv[s] = tcv
            tks = []
            for dst, vf in stores:
                tks.append(POOL.dma(dst, vf(stb[s]), stq_sem[s], waits=[tcv]))
            slot_free[s] = tks[-1]
            pend_store.append(tks[-1])

        for kc in range(8):
            r = slice(kc * 128, (kc + 1) * 128)
            convert(w_in[r, :], INC, [
                (win_bf[r, :], lambda t: t[:, 0:INC]),
                (wlat_bf[r, 0:704], lambda t: t[:, 0:704]),
                (wlat_bf[r, 704:736], lambda t: t[:, 672:704]),
                (wlat_bf[r, 736:768], lambda t: t[:, 640:672]),
            ])
        for kc in range(3):
            r = slice(kc * 128, (kc + 1) * 128)
            hv = lambda t: t[:, 0:NH * QKD].rearrange("p (h d) -> p h d", d=QKD)
            dup = []
            for half in range(2):
                o_ = half * 64
                dup.append((wuqr_d[:, kc, :, o_:o_ + 64], lambda t: hv(t)[:, :, 128:192]))
                dup.append((wuqpr_d[:, kc, :, o_:o_ + 32], lambda t: hv(t)[:, :, 160:192]))
                dup.append((wuqpr_d[:, kc, :, o_ + 32:o_ + 64], lambda t: hv(t)[:, :, 128:160]))
            convert(w_uq[r, :], NH * QKD, dup + [
                (wuq_bf[r, :], lambda t: t[:, 0:NH * QKD]),
                (wuqp_bf[r, :].rearrange("p (h d) -> p h d", d=RD)[:, :, 0:32],
                 lambda t: t[:, 0:NH * QKD].rearrange("p (h d) -> p h d", d=QKD)[:, :, 160:192]),
                (wuqp_bf[r, :].rearrange("p (h d) -> p h d", d=RD)[:, :, 32:64],
                 lambda t: t[:, 0:NH * QKD].rearrange("p (h d) -> p h d", d=QKD)[:, :, 128:160]),
            ])
        for kc in range(2):
            r = slice(kc * 128, (kc + 1) * 128)
            convert(w_ukv[r, :], NH * 256, [
                (wuk_bf[r, :].rearrange("p (h d) -> p h d", d=128),
                 lambda t: t[:, 0:NH * 256].rearrange("p (h d) -> p h d", d=256)[:, :, 0:128]),
                (wuv_bf[r, :].rearrange("p (h d) -> p h d", d=128),
                 lambda t: t[:, 0:NH * 256].rearrange("p (h d) -> p h d", d=256)[:, :, 128:256]),
            ])
        for kc in range(8):
            r = slice(kc * 128, (kc + 1) * 128)
            convert(w_o[r, :], D, [(wo_bf[r, :], lambda t: t[:, 0:D])])
        for kc in range(8):
            r = slice(kc * 128, (kc + 1) * 128)
            convert(w_ff1[r, :], DFF, [(wff1_bf[r, :], lambda t: t[:, 0:DFF])])
        for k4 in range(8):
            r = slice(k4 * 512, (k4 + 1) * 512)
            convert(w_ff2[r, :].rearrange("(a p) c -> p a c", p=128), 4 * D,
                    [(wff2_bf[r, :].rearrange("(a p) c -> p a c", p=128),
                      lambda t: t[:, 0:4 * D].rearrange("p (a c) -> p a c", c=D))], c3=D)
        wconv_done = list(slot_free)
        barrier(wconv_done + [const_ready])

        def norm_transpose(xt, x_ready, g_bc, xn2, hnT, st_tok):
            junk = st_tok["junk"]
            tks = []
            for j in range(4):
                tks.append(ACT.emit(_f_act_acc(junk, xt[:, j, :], AF.Square, ss4[:, j:j + 1]),
                                    waits=[x_ready, st_tok.get("ss_free")]))
            t1 = DVE.emit(_f_ts(t4[:], ss4[:], 1.0 / D, EPS, ALU.mult, ALU.add), waits=[tks[-1]])
            st_tok["ss_free"] = t1
            t2 = POOL.emit(_f_tt(rstd4[:], t4[:], mhalf[:, 0:4], ALU.pow), waits=[t1])
            done = []
            for j in range(4):
                s = j % 2
                tn = DVE.emit(_f_stt(xn2[s], xt[:, j, :], rstd4[:, j:j + 1], g_bc[:], ALU.mult, ALU.mult),
                              waits=[t2, st_tok["xn_free"][s]])
                b, bw = balloc()
                tk = None
                for kc in range(8):
                    tk = PE.emit(_f_tr(bank_bf(b)[:, kc * 128:(kc + 1) * 128], xn2[s][:, kc * 128:(kc + 1) * 128],
                                       ident[:]), waits=[tn, bw] if kc == 0 else (), sig=(kc == 7))
                st_tok["xn_free"][s] = tk
                te = ACT.emit(_f_act(hnT[:, :, j * 128:(j + 1) * 128],
                                     bank_bf(b).rearrange("p (k t) -> p k t", t=128), AF.Copy),
                              waits=[tk, st_tok.get("hnT_free")])
                brelease(b, te)
                done.append(te)
            return done[-1]

        def phase1(xsrc, n_mt, do_kv, do_q, cosk, sink, jb, qcol0):
            bp = Bump()
            wlat = bp.get([128, 8, 768], BF16)
            wuk = bp.get([128, 2, 1024], BF16)
            wuv = bp.get([128, 2, 1024], BF16)
            xt2 = [bp.get([128, 4, D], F32), bp.get([128, 4, D], F32)]
            xn2 = [bp.get([128, D], BF16), bp.get([128, D], BF16)]
            junk = bp.get([128, D], BF16)
            hnT2 = [bp.get([128, 8, MT], BF16), bp.get([128, 8, MT], BF16)]
            kst2 = [bp.get([128, NH, MT], BF16), bp.get([128, NH, MT], BF16)]
            vst2 = [bp.get([128, NH, 4, 128], BF16), bp.get([128, NH, 4, 128], BF16)]
            krst2 = [bp.get([64, MT], BF16, parts=64), bp.get([64, MT], BF16, parts=64)]
            ckvn2 = [bp.get([128, 2, MT], BF16), bp.get([128, 2, MT], BF16)]
            cqst2 = [bp.get([128, 3, MT], BF16), bp.get([128, 3, MT], BF16)]
            sqb = [bp.get([128, MT], BF16) for _ in range(3)]
            tf = [bp.get([128, MT], F32) for _ in range(4)]
            cs2 = [bp.get([64, 2, MT], F32), bp.get([64, 2, MT], F32)]

            wsem = new_dsem()
            SP.dma(wlat, wlat_bf.rearrange("(kc p) c -> p kc c", p=128), wsem)
            SP.dma(wuk, wuk_bf.rearrange("(kc p) c -> p kc c", p=128), wsem)
            w_ready = SP.dma(wuv, wuv_bf.rearrange("(kc p) c -> p kc c", p=128), wsem)

            x_sem = [new_dsem(), new_dsem()]
            cs_sem = [new_dsem(), new_dsem()]
            st_sem = [new_dsem("sw"), new_dsem("sw")]
            stt_ = {"junk": junk, "xn_free": [None, None]}
            xt_free = [None, None]
            cs_free = [None, None]
            hn_free = [None, None]
            stage_free = [None, None]
            ckvn_free = [None, None]
            stores = []
            A = {}

            def load(i):
                s = i % 2
                r0 = i * MT
                tx = SP.dma(xt2[s], xsrc[r0:r0 + MT, :].rearrange("(j p) d -> p j d", p=128), x_sem[s],
                            waits=[xt_free[s]])
                tc_ = None
                if do_kv:
                    SP.dma(cs2[s][:, 0, :], cosk[:, r0:r0 + MT], cs_sem[s], waits=[cs_free[s]])
                    tc_ = SP.dma(cs2[s][:, 1, :], sink[:, r0:r0 + MT], cs_sem[s])
                A[i] = dict(tx=tx, tcs=tc_)

            def stageA(i):
                s = i % 2
                a = A[i]
                stt_["hnT_free"] = hn_free[s]
                th = norm_transpose(xt2[s], a["tx"], g_mix, xn2, hnT2[s], stt_)
                xt_free[s] = DVE.tok()
                yield
                hnT = hnT2[s]
                last_pe = None
                if do_kv:
                    cb = []
                    for m in range(2):
                        b, bw = balloc()
                        tk = mm_group(bank(b), [(wlat[:, kc, QL + m * 128:QL + (m + 1) * 128], hnT[:, kc, :])
                                                for kc in range(8)], waits=[th, bw, w_ready])
                        cb.append((b, tk))
                    rb = []
                    for m in range(2):
                        b, bw = balloc()
                        tk = mm_group(bank(b)[0:64, :], [(wlat[:, kc, 640 + m * 64:640 + (m + 1) * 64], hnT[:, kc, :])
                                                          for kc in range(8)], waits=[bw])
                        rb.append((b, tk))
                    tsq = []
                    for m in range(2):
                        tsq.append(ACT.emit(_f_act(sqb[m], bank(cb[m][0]), AF.Square), waits=[cb[m][1]]))
                    yield
                    b, bw = balloc()
                    tss = mm_group(bank(b), [(ones_bf[:], sqb[m]) for m in range(2)], waits=[tsq[-1], bw])
                    tt = ACT.emit(_f_act(tf[0], bank(b), AF.Ln, scale=1.0 / KVL, bias=eps_col[:, 0:1]), waits=[tss])
                    brelease(b, tt)
                    tr_ = ACT.emit(_f_act(tf[1], tf[0], AF.Exp, scale=-0.5), waits=[tt, stt_.get("rstd_free")])
                    tcn = None
                    for m in range(2):
                        tcn = DVE.emit(_f_stt(ckvn2[s][:, m, :], bank(cb[m][0]), gkv_col[:, m:m + 1], tf[1],
                                              ALU.mult, ALU.mult), waits=[tr_, ckvn_free[s]])
                        brelease(cb[m][0], tcn)
                    a["ckvn"] = tcn
                    stt_["rstd_free"] = tcn
                    r1 = DVE.emit(_f_tt(tf[2][0:64, :], bank(rb[0][0])[0:64, :], cs2[s][:, 0, :], ALU.mult),
                                  waits=[rb[0][1], a["tcs"], stt_.get("kr_pool")])
                    brelease(rb[0][0], r1)
                    r2 = DVE.emit(_f_tt(tf[3][0:64, :], bank(rb[1][0])[0:64, :], cs2[s][:, 1, :], ALU.mult),
                                  waits=[rb[1][1]])
                    brelease(rb[1][0], r2)
                    cs_free[s] = r2
                    r3 = POOL.emit(_f_tt(krst2[s], tf[2][0:64, :], tf[3][0:64, :], ALU.add),
                                   waits=[r1, r2, stage_free[s]])
                    a["kr"] = r3
                    stt_["kr_pool"] = r3
                    last_pe = tss
                if do_q:
                    qb = []
                    for m in range(3):
                        b, bw = balloc()
                        tk = mm_group(bank(b), [(wlat[:, kc, m * 128:(m + 1) * 128], hnT[:, kc, :])
                                                for kc in range(8)], waits=[th, bw, w_ready])
                        qb.append((b, tk))
                    tsq = []
                    for m in range(3):
                        tsq.append(ACT.emit(_f_act(sqb[m], bank(qb[m][0]), AF.Square), waits=[qb[m][1], last_pe]))
                    b, bw = balloc()
                    tss = mm_group(bank(b), [(ones_bf[:], sqb[m]) for m in range(3)], waits=[tsq[-1], bw])
                    tt = ACT.emit(_f_act(tf[0], bank(b), AF.Ln, scale=1.0 / QL, bias=eps_col[:, 0:1]), waits=[tss])
                    brelease(b, tt)
                    tr_ = ACT.emit(_f_act(tf[1], tf[0], AF.Exp, scale=-0.5), waits=[tt, stt_.get("rstd_free")])
                    tcq = None
                    for m in range(3):
                        tcq = DVE.emit(_f_stt(cqst2[s][:, m, :], bank(qb[m][0]), gq_col[:, m:m + 1], tf[1],
                                              ALU.mult, ALU.mult), waits=[tr_, stage_free[s]])
                        brelease(qb[m][0], tcq)
                    a["cq"] = tcq
                    stt_["rstd_free"] = tcq
                    last_pe = tss
                hn_free[s] = PE.tok()

            def stageB(i):
                s = i % 2
                a = A[i]
                r0 = i * MT
                stks = []
                if do_kv:
                    ck = ckvn2[s]
                    evs = [ACT, DVE]
                    te = None
                    for h in range(NH):
                        b, bw = balloc()
                        tk = mm_group(bank(b), [(wuk[:, m, h * 128:(h + 1) * 128], ck[:, m, :]) for m in range(2)],
                                      waits=[a["ckvn"], bw])
                        if h % 2 == 0:
                            te = ACT.emit(_f_act(kst2[s][:, h, :], bank(b), AF.Copy), waits=[tk, stage_free[s]])
                        else:
                            te = DVE.emit(_f_copy(kst2[s][:, h, :], bank(b)), waits=[tk, stage_free[s]])
                        brelease(b, te)
                    tkA, tkD = ACT.tok(), DVE.tok()
                    stks.append(POOL.dma(jb["kT"][:, :, r0:r0 + MT].rearrange("h d t -> d h t"), kst2[s], st_sem[s],
                                         waits=[tkA, tkD]))
                    stks.append(POOL.dma(jb["krT"][:, r0:r0 + MT], krst2[s], st_sem[s], waits=[a["kr"]]))
                    yield
                    for j in range(4):
                        for half in range(2):
                            b, bw = balloc()
                            tk = mm_group(bank(b), [(ck[:, m, j * 128:(j + 1) * 128],
                                                     wuv[:, m, half * 512:(half + 1) * 512]) for m in range(2)],
                                          waits=[bw])
                            dst = vst2[s][:, half * 4:(half + 1) * 4, j, :]
                            src = bank(b).rearrange("p (h d) -> p h d", d=128)
                            if (j + half) % 2 == 0:
                                te = ACT.emit(_f_act(dst, src, AF.Copy), waits=[tk])
                            else:
                                te = DVE.emit(_f_copy(dst, src), waits=[tk])
                            brelease(b, te)
                    tkA, tkD = ACT.tok(), DVE.tok()
                    ckvn_free[s] = PE.tok()
                    c = r0 // KC
                    kb0 = (r0 % KC) // 128
                    stks.append(POOL.dma(jb["v"][:, c, :, kb0:kb0 + 4, :].rearrange("h p k d -> p h k d"), vst2[s],
                                         st_sem[s], waits=[tkA, tkD]))
                if do_q:
                    stks.append(POOL.dma(jb["cq"][:, :, qcol0 + r0:qcol0 + r0 + MT].rearrange("m p t -> p m t"),
                                         cqst2[s], st_sem[s], waits=[a["cq"]]))
                stage_free[s] = stks[-1]
                stores.append(stks[-1])
                yield

            def drive(gens):
                gens = [g for g in gens if g is not None]
                while gens:
                    for g in list(gens):
                        try:
                            next(g)
                        except StopIteration:
                            gens.remove(g)

            load(0)
            if n_mt > 1:
                load(1)
            drive([stageA(0)])
            for i in range(n_mt):
                ga = stageA(i + 1) if i + 1 < n_mt else None
                if i + 2 < n_mt:
                    load(i + 2)
                drive([ga, stageB(i)])
            barrier(stores[-2:])
            return stores[-2:]

        def phase2(jb, q0, npass):
            S = jb["S"]
            NQT = npass * NQ
            VH = [(ps_, h_) for ps_ in range(npass) for h_ in range(NH)]
            bp = Bump()
            wuq = bp.get([128, 3, NH * QKD], BF16)
            wuq_r = bp.get([128, 3, NH, 128], BF16)
            wuqp_r = bp.get([128, 3, NH, 128], BF16)
            cqn = bp.get([128, 3, NQT], BF16)
            csq = bp.get([128, 2, NQT], F32)
            qn2 = [bp.get([128, NQ], BF16), bp.get([128, NQ], BF16)]
            qr2 = [bp.get([128, NQ], BF16), bp.get([128, NQ], BF16)]
            NKV = 4
            kn = [bp.get([128, KC], BF16) for _ in range(NKV)]
            kr = [bp.get([128, KC // 2], BF16) for _ in range(NKV)]
            vv = [bp.get([128, KC // 128, 128], BF16) for _ in range(NKV)]
            NPB = 8
            pT = [bp.get([128, 512], BF16) for _ in range(NPB)]
            rec = [bp.get([128, 512], F32) for _ in range(2)]
            ocp = [[bp.get([128, 512], F32) for _ in range(2)] for _ in range(2)]
            accD = [[bp.get([128, 512], F32) for _ in range(2)] for _ in range(2)]
            pair = [bp.get([128, 512], BF16) for _ in range(2)]
            qtmp = [bp.get([128, 512], F32) for _ in range(2)]
            assert bp.off <= XT_OFF, bp.off

            wsem = new_dsem()
            SP.dma(wuq, wuq_bf.rearrange("(kc p) c -> p kc c", p=128), wsem)
            SP.dma(wuq_r.rearrange("p a b c -> p (a b c)"), wuqr_d.rearrange("p a b c -> p (a b c)"), wsem)
            SP.dma(wuqp_r.rearrange("p a b c -> p (a b c)"), wuqpr_d.rearrange("p a b c -> p (a b c)"), wsem)
            for half in range(2):
                SP.dma(csq[half * 64:(half + 1) * 64, 0, :], jb["cosq"][:, q0:q0 + NQT], wsem)
                SP.dma(csq[half * 64:(half + 1) * 64, 1, :], jb["sinq"][:, q0:q0 + NQT], wsem)
            w_ready = SP.dma(cqn, jb["cq"][:, :, q0:q0 + NQT].rearrange("m p t -> p m t"), wsem)

            kv_sem = [new_dsem() for _ in range(NKV)]
            kv_free = [None] * NKV
            nchunk = S // KC
            seq = [(h_, c) for (ps_, h_) in VH for c in range(nchunk)]
            kv_tok = {}

            def kv_load(idx):
                h, c = seq[idx]
                s = idx % NKV
                SP.dma(kn[s], jb["kT"][h, :, c * KC:(c + 1) * KC], kv_sem[s], waits=[kv_free[s]])
                krv = jb["krT"][:, c * KC:(c + 1) * KC].rearrange("r (j two t) -> r two j t", two=2, t=128)
                SP.dma(kr[s][0:64, :].rearrange("p (j t) -> p j t", t=128), krv[:, 0], kv_sem[s])
                SP.dma(kr[s][64:128, :].rearrange("p (j t) -> p j t", t=128), krv[:, 1], kv_sem[s])
                kv_tok[idx] = SP.dma(vv[s], jb["v"][h, c], kv_sem[s])

            for idx in range(min(NKV - 1, len(seq))):
                kv_load(idx)

            SB = [0, 1, 2, 3, 4, 5]
            OB = [6, 7]
            for b in range(8):
                bank_busy.discard(b)
            pT_free = [None] * NPB
            q_free = [None, None]
            o_free = [None, None]
            fin_pending = []
            fin_tok = [None, None]
            tile_ctr = [0]
            NQS = NQ // 512
            NPR = KC // 256

            def qproj_steps(vi_):
                ps_, h = VH[vi_]
                s = vi_ % 2
                res = {"toks": []}
                steps = []

                def mk(qs, kind):
                    cols = slice(qs * 512, (qs + 1) * 512)
                    gcols = slice(ps_ * NQ + qs * 512, ps_ * NQ + (qs + 1) * 512)

                    def nope():
                        b, bw = balloc(SB)
                        tk = mm_group(bank(b), [(wuq[:, m, h * QKD:h * QKD + 128], cqn[:, m, gcols])
                                                for m in range(3)], waits=[w_ready, bw])
                        te = ACT.emit(_f_act(qn2[s][:, cols], bank(b), AF.Copy), waits=[tk, q_free[s]])
                        brelease(b, te)
                        res["toks"].append(te)

                    def ropea():
                        b1, bw1 = balloc(SB)
                        tk1 = mm_group(bank(b1), [(wuq_r[:, m, h, :], cqn[:, m, gcols]) for m in range(3)],
                                       waits=[w_ready, bw1])
                        r1 = DVE.emit(_f_tt(qtmp[0], bank(b1), csq[:, 0, gcols], ALU.mult),
                                      waits=[tk1, qproj.last_pool])
                        brelease(b1, r1)
                        res["r1"] = r1

                    def ropeb():
                        b2, bw2 = balloc(SB)
                        tk2 = mm_group(bank(b2), [(wuqp_r[:, m, h, :], cqn[:, m, gcols]) for m in range(3)],
                                       waits=[w_ready, bw2])
                        r2 = DVE.emit(_f_tt(qtmp[1], bank(b2), csq[:, 1, gcols], ALU.mult),
                                      waits=[tk2, qproj.last_pool])
                        brelease(b2, r2)
                        tq = POOL.emit(_f_tt(qr2[s][:, cols], qtmp[0], qtmp[1], ALU.add),
                                       waits=[res["r1"], r2, q_free[s]])
                        qproj.last_pool = tq
                        res["toks"].append(tq)

                    return {"nope": nope, "ropea": ropea, "ropeb": ropeb}[kind]

                for qs in range(NQS):
                    for kind in ("nope", "ropea", "ropeb"):
                        steps.append(mk(qs, kind))
                return steps, res

            def qproj(vi_):
                steps, res = qproj_steps(vi_)
                for f_ in steps:
                    f_()
                return res["toks"]

            qproj.last_pool = None
            qtoks = {0: qproj(0)}

            def finalize(hh, oc0, aD, tD, oc, tO):
                for qs in range(NQS):
                    ocols = slice(oc0 + qs * 512, oc0 + (qs + 1) * 512)
                    b, bw = balloc(SB)
                    tr_ = PE.emit(_f_mm(bank(b), ones_f[:], aD[qs], True, True), waits=[tD[qs], bw])
                    t1 = DVE.emit(_f_recip(rec[qs], bank(b)), waits=[tr_])
                    brelease(b, t1)
                    fin_tok[hp_of[(hh, oc0)]] = DVE.emit(_f_tt(oaT[:, hh, ocols], oc[qs], rec[qs], ALU.mult),
                                               waits=[t1, tO[qs]])

            hp_of = {}
            for vi, (ps, h) in enumerate(VH):
                s = vi % 2
                hp = vi % 2
                oc0 = ps * NQ
                hp_of[(h, oc0)] = hp
                qt = qtoks[vi]
                items = [(c, pr, qs) for c in range(nchunk) for pr in range(NPR) for qs in range(NQS)]
                nit = len(items)
                s_tok = {}
                accD_tok = [None] * NQS
                lastpv = [None] * NQS
                qsteps, qres = [], None

                def issue_S(ii):
                    c, pr, qs = items[ii]
                    li = vi * nchunk + c
                    ks = li % NKV
                    cols = slice(qs * 512, (qs + 1) * 512)
                    bA, bwA = balloc(SB)
                    bB, bwB = balloc(SB)
                    kA, kB = 2 * pr, 2 * pr + 1
                    PE.emit(_f_mm(bank(bA), kn[ks][:, kA * 128:(kA + 1) * 128], qn2[s][:, cols], True, False),
                            waits=[kv_tok[li], qt, bwA], sig=False)
                    PE.emit(_f_mm(bank(bB), kn[ks][:, kB * 128:(kB + 1) * 128], qn2[s][:, cols], True, False),
                            waits=[bwB], sig=False)
                    PE.emit(_f_mm(bank(bA), kr[ks][0:64, pr * 128:(pr + 1) * 128], qr2[s][0:64, cols], False, True),
                            sig=False)
                    tk = PE.emit(_f_mm(bank(bB), kr[ks][64:128, pr * 128:(pr + 1) * 128], qr2[s][64:128, cols],
                                       False, True))
                    s_tok[ii] = (bA, bB, tk)

                issue_S(0)
                for ii in range(nit):
                    c, pr, qs = items[ii]
                    li = vi * nchunk + c
                    ks = li % NKV
                    if ii + 1 < nit:
                        issue_S(ii + 1)
                    if ii == 5 and fin_pending:
                        finalize(*fin_pending.pop())
                    if ii == nit // 2 and vi + 1 < len(VH):
                        qsteps, qres = qproj_steps(vi + 1)
                    if vi + 1 < len(VH) and ii >= nit // 2 and qsteps:
                        qsteps.pop(0)()
                        if not qsteps:
                            qtoks[vi + 1] = qres["toks"]
                    bA, bB, tk = s_tok.pop(ii)
                    first = (c == 0 and pr == 0)
                    last = (c == nchunk - 1) and (pr == NPR - 1)
                    pbs, tes, tps = [], [], []
                    for t_, (b, kb) in enumerate(((bA, 2 * pr), (bB, 2 * pr + 1))):
                        g = tile_ctr[0]
                        tile_ctr[0] += 1
                        pb = g % NPB
                        te = ACT.emit(_f_act(pT[pb], bank(b), AF.Exp, scale=SCALE), waits=[tk, pT_free[pb]])
                        brelease(b, te)
                        st_ = first and t_ == 0
                        en_ = last and t_ == 1
                        tp = PE.emit(_f_mm(bank(OB[qs]), vv[ks][:, kb, :], pT[pb], st_, en_),
                                     waits=[te, o_free[qs]] if st_ else [te])
                        pbs.append(pb)
                        tes.append(te)
                        tps.append(tp)
                    lastpv[qs] = tps[1]
                    tpair = DVE.emit(_f_tt(pair[qs], pT[pbs[0]], pT[pbs[1]], ALU.add),
                                     waits=[tes[0], tes[1], accD_tok[qs]])
                    if accD_tok[qs] is None:
                        ta = DVE.emit(_f_copy(accD[hp][qs], pair[qs]), waits=[tpair, fin_tok[hp]])
                    else:
                        ta = DVE.emit(_f_tt(accD[hp][qs], accD[hp][qs], pair[qs], ALU.add), waits=[tpair])
                    accD_tok[qs] = ta
                    pT_free[pbs[0]] = [tps[0], tpair]
                    pT_free[pbs[1]] = [tps[1], tpair]
                    if pr == NPR - 1 and qs == NQS - 1:
                        kv_free[ks] = tps[1]
                        nxt = li + NKV - 1
                        if nxt < len(seq) and nxt not in kv_tok:
                            kv_load(nxt)
                while qsteps:
                    qsteps.pop(0)()
                    if not qsteps:
                        qtoks[vi + 1] = qres["toks"]
                tO = []
                for qs in range(NQS):
                    c2 = ACT.emit(_f_act(ocp[hp][qs], bank(OB[qs]), AF.Copy), waits=[lastpv[qs], fin_tok[hp]])
                    o_free[qs] = c2
                    tO.append(c2)
                fin_pending.append((h, oc0, accD[hp], accD_tok, ocp[hp], tO))
                if vi == len(VH) - 1:
                    while fin_pending:
                        finalize(*fin_pending.pop(0))
                q_free[s] = PE.tok()
            barrier()

        def phase3(jb, q0, nr):
            xsrc = jb["xqr"] if jb["xqr"] is not None else jb["xk"]
            bp = Bump()
            bp.off = XT_OFF
            xt2 = [bp.get([128, 4, D], F32), bp.get([128, 4, D], F32)]
            bp.off = 0
            xn2 = [bp.get([128, D], BF16), bp.get([128, D], BF16)]
            junk = bp.get([128, D], BF16)
            hnTs = [bp.get([128, 8, MT], BF16), bp.get([128, 8, MT], BF16)]
            NW = 5
            wr = [bp.get([128, 8, 512], BF16) for _ in range(NW)]
            tf = [bp.get([128, 512], F32) for _ in range(6)]
            ma4 = bp.get([128, 4, 512], F32)
            u_off = bp.off
            vg = bp.get([128, D], F32)
            vsn = bp.get([128, 4, D], BF16)
            uT = bp.get([128, 8, MT], BF16)
            end1 = bp.off
            bp.off = u_off
            h1T = bp.get([128, 32, MT], BF16)
            bp.off = max(bp.off, end1)
            assert bp.off <= XT_OFF, bp.off

            n_mt = nr // MT
            x_sem = x_sem_p
            y_sem = y_sem_p
            w_sem = [new_dsem() for _ in range(NW)]
            w_free = [None] * NW
            xt_free = xt_free_p
            stt_ = {"junk": junk, "xn_free": [None, None], "hnT_free": None}
            hn_free = [None, None]
            ytoks = []

            def piece_list():
                L = []
                for i in range(2):
                    L.append(("v", win_bf[:, V0 + i * 512:V0 + (i + 1) * 512]))
                for i in range(2):
                    L.append(("u", win_bf[:, U0 + i * 512:U0 + (i + 1) * 512]))
                for i in range(2):
                    L.append(("ga", win_bf[:, GA0 + i * 512:GA0 + (i + 1) * 512]))
                    L.append(("gb", win_bf[:, GB0 + i * 512:GB0 + (i + 1) * 512]))
                for i in range(2):
                    L.append(("wo", wo_bf[:, i * 512:(i + 1) * 512]))
                for i in range(8):
                    L.append(("f1", wff1_bf[:, i * 512:(i + 1) * 512]))
                for half in range(2):
                    for g4 in range(4):
                        L.append(("f2", wff2_bf[g4 * 1024:(g4 + 1) * 1024, half * 512:(half + 1) * 512]))
                return L

            pieces = []
            for mt in range(n_mt):
                pieces += piece_list()
            w_tok = {}
            w_next = [0]

            def w_issue():
                i = w_next[0]
                if i >= len(pieces):
                    return
                s = i % NW
                w_tok[i] = SP.dma(wr[s], pieces[i][1].rearrange("(kc p) c -> p kc c", p=128), w_sem[s],
                                  waits=[w_free[s]])
                w_next[0] += 1

            def w_done(i, tok):
                w_free[i % NW] = tok
                w_issue()

            def xload(mt):
                s = mt % 2
                r0 = q0 + mt * MT
                return SP.dma(xt2[s], xsrc[r0:r0 + MT, :].rearrange("(j p) d -> p j d", p=128), x_sem[s],
                              waits=[xt_free[s]])

            xtok = {}
            if (jb["name"], q0) in x_pref:
                xtok = x_pref.pop((jb["name"], q0))
            if 0 not in xtok:
                xtok[0] = xload(0)
            for _ in range(NW - 1):
                w_issue()
            if n_mt > 1 and 1 not in xtok:
                xtok[1] = xload(1)
            w_issue()
            th_next = None

            def rms_rows(src3, g_bc, dst_fn, wait):
                tks = []
                for j in range(4):
                    tks.append(ACT.emit(_f_act_acc(junk, src3(j), AF.Square, ss4[:, j:j + 1]),
                                        waits=[wait, stt_.get("ss_free")]))
                t1 = DVE.emit(_f_ts(t4[:], ss4[:], 1.0 / D, EPS, ALU.mult, ALU.add), waits=[tks[-1]])
                stt_["ss_free"] = t1
                t2 = POOL.emit(_f_tt(rstd4[:], t4[:], mhalf[:, 0:4], ALU.pow), waits=[t1])
                tk = None
                for j in range(4):
                    tk = DVE.emit(_f_stt(dst_fn(j), src3(j), rstd4[:, j:j + 1], g_bc[:], ALU.mult, ALU.mult),
                                  waits=[t2])
                return tk

            tf_free = [None] * 6
            ma4_free = [None] * 4

            def gelu_from_psum(b, tk, out_ap, k):
                a1 = ACT.emit(_f_act(tf[k], bank(b), AF.Square), waits=[tk, tf_free[k]])
                d1 = DVE.emit(_f_ts(tf[k], tf[k], GC1, 1.0, ALU.mult, ALU.add), waits=[a1])
                d2 = DVE.emit(_f_tt(tf[k], tf[k], bank(b), ALU.mult), waits=[d1])
                a2 = ACT.emit(_f_act(tf[k + 1], tf[k], AF.Sigmoid, scale=GC2), waits=[d2, tf_free[k + 1]])
                d3 = DVE.emit(_f_tt(out_ap, tf[k + 1], bank(b), ALU.mult), waits=[a2])
                tf_free[k] = a2
                tf_free[k + 1] = d3
                return d3

            pi = [0]
            for mt in range(n_mt):
                s = mt % 2
                xt = xt2[s]
                qc = slice(mt * MT, (mt + 1) * MT)
                hnT = hnTs[mt % 2]
                if th_next is None:
                    stt_["hnT_free"] = hn_free[mt % 2]
                    th = norm_transpose(xt, xtok[mt], g_mix, xn2, hnT, stt_)
                else:
                    th = th_next
                    th_next = None
                pv0, pv1 = pi[0], pi[0] + 1
                pi[0] += 2
                last = None
                for j in range(4):
                    for half in range(2):
                        p_ = pv0 + half
                        b, bw = balloc()
                        tk = mm_group(bank(b), [(hnT[:, kc, j * 128:(j + 1) * 128], wr[p_ % NW][:, kc, :])
                                                for kc in range(8)], waits=[th, w_tok[p_], bw])
                        last = tk
                        d3 = ACT.emit(_f_act(vg[:, half * 512:(half + 1) * 512], bank(b), AF.Gelu_apprx_tanh),
                                      waits=[tk, stt_.get("vg_free")])
                        brelease(b, d3)
                    tsq = ACT.emit(_f_act_acc(junk, vg[:], AF.Square, ss4[:, 0:1]), waits=[d3, stt_.get("ss_free")])
                    t1 = DVE.emit(_f_ts(t4[:, 0:1], ss4[:, 0:1], 1.0 / D, EPS, ALU.mult, ALU.add), waits=[tsq])
                    stt_["ss_free"] = t1
                    t2 = POOL.emit(_f_tt(rstd4[:, 0:1], t4[:, 0:1], mhalf[:, 0:1], ALU.pow), waits=[t1])
                    tvs = DVE.emit(_f_stt(vsn[:, j, :], vg[:], rstd4[:, 0:1], g_sgu[:], ALU.mult, ALU.mult),
                                   waits=[t2])
                    stt_["vg_free"] = tvs
                w_done(pv0, last)
                w_done(pv1, last)
                for i in range(2):
                    p_ = pi[0]
                    pi[0] += 1
                    for mm in range(4):
                        m = i * 4 + mm
                        b, bw = balloc()
                        tk = mm_group(bank(b), [(wr[p_ % NW][:, kc, mm * 128:(mm + 1) * 128], hnT[:, kc, :])
                                                for kc in range(8)], waits=[w_tok[p_], bw])
                        d3 = ACT.emit(_f_act(uT[:, m, :], bank(b), AF.Gelu_apprx_tanh), waits=[tk])
                        brelease(b, d3)
                    w_done(p_, tk)
                tu = d3
                for g in range(8):
                    b, bw = balloc()
                    tk = None
                    for j in range(4):
                        tk = PE.emit(_f_mm(bank(b)[:, j * 128:(j + 1) * 128], vsn[:, j, g * 128:(g + 1) * 128],
                                           wsT[:, g, :], True, True), waits=[tvs, bw] if j == 0 else ())
                    k = 4 + (g % 2)
                    d1 = DVE.emit(_f_tt(tf[k].rearrange("p (j t) -> p j t", t=128),
                                        bank(b).rearrange("p (j t) -> p j t", t=128),
                                        bs_bc[:, g:g + 1, :].broadcast_to([128, 4, 128]), ALU.add),
                                  waits=[tk, tf_free[k]])
                    brelease(b, d1)
                    tob = DVE.emit(_f_tt(uT[:, g, :], uT[:, g, :], tf[k], ALU.mult), waits=[d1, tu])
                    tf_free[k] = tob
                for i in range(2):
                    pa, pb_ = pi[0], pi[0] + 1
                    pi[0] += 2
                    for mm in range(4):
                        m = i * 4 + mm
                        b, bw = balloc()
                        tk = mm_group(bank(b), [(wr[pa % NW][:, kc, mm * 128:(mm + 1) * 128], hnT[:, kc, :])
                                                for kc in range(8)], waits=[w_tok[pa], bw])
                        k = mm % 2
                        a1 = ACT.emit(_f_act(tf[k], bank(b), AF.Sigmoid), waits=[tk, tf_free[k]])
                        brelease(b, a1)
                        d1 = POOL.emit(_f_tt(ma4[:, mm, :], tf[k], oaT[:, m, qc], ALU.mult),
                                       waits=[a1, ma4_free[mm]])
                        tf_free[k] = d1
                        ma_tok = d1
                    w_done(pa, tk)
                    for mm in range(4):
                        m = i * 4 + mm
                        b, bw = balloc()
                        tk = mm_group(bank(b), [(wr[pb_ % NW][:, kc, mm * 128:(mm + 1) * 128], hnT[:, kc, :])
                                                for kc in range(8)], waits=[w_tok[pb_], bw])
                        k = 2 + mm % 2
                        a1 = ACT.emit(_f_act(tf[k], bank(b), AF.Sigmoid), waits=[tk, tf_free[k]])
                        brelease(b, a1)
                        d1 = DVE.emit(_f_tt(tf[k], tf[k], uT[:, m, :], ALU.mult), waits=[a1, tob])
                        tmg = DVE.emit(_f_tt(uT[:, m, :], tf[k], ma4[:, mm, :], ALU.add), waits=[d1, ma_tok])
                        tf_free[k] = tmg
                        ma4_free[mm] = tmg
                    w_done(pb_, tk)
                hn_free[mt % 2] = PE.tok()
                for half in range(2):
                    p_ = pi[0]
                    pi[0] += 1
                    for j in range(4):
                        b, bw = balloc()
                        tk = mm_group(bank(b), [(uT[:, kc, j * 128:(j + 1) * 128], wr[p_ % NW][:, kc, :])
                                                for kc in range(8)], waits=[tmg, w_tok[p_], bw])
                        tx1 = DVE.emit(_f_tt(xt[:, j, half * 512:(half + 1) * 512],
                                             xt[:, j, half * 512:(half + 1) * 512], bank(b), ALU.add), waits=[tk])
                        brelease(b, tx1)
                    w_done(p_, tk)
                stt_["hnT_free"] = hn_free[mt % 2]
                th2 = norm_transpose(xt, tx1, g_ffn, xn2, hnT, stt_)
                for i in range(8):
                    p_ = pi[0]
                    pi[0] += 1
                    for mm in range(4):
                        m = i * 4 + mm
                        b, bw = balloc()
                        tk = mm_group(bank(b), [(wr[p_ % NW][:, kc, mm * 128:(mm + 1) * 128], hnT[:, kc, :])
                                                for kc in range(8)], waits=[th2, w_tok[p_], bw])
                        k = m % 4
                        a1 = ACT.emit(_f_act(tf[k], bank(b), AF.Relu), waits=[tk, tf_free[k]])
                        brelease(b, a1)
                        th1 = POOL.emit(_f_tt(h1T[:, m, :], tf[k], tf[k], ALU.mult), waits=[a1])
                        tf_free[k] = th1
                    w_done(p_, tk)
                hn_free[mt % 2] = PE.tok()
                if mt + 2 < n_mt and (mt + 2) not in xtok:
                    pass
                if mt + 1 < n_mt:
                    stt_["hnT_free"] = hn_free[(mt + 1) % 2]
                    th_next = norm_transpose(xt2[(mt + 1) % 2], xtok[mt + 1], g_mix, xn2, hnTs[(mt + 1) % 2], stt_)
                for half in range(2):
                    bks = []
                    for j in range(4):
                        b, bw = balloc()
                        bks.append((b, bw))
                    tk = None
                    for g4 in range(4):
                        p_ = pi[0]
                        pi[0] += 1
                        for j in range(4):
                            b, bw = bks[j]
                            for kc in range(8):
                                tk = PE.emit(_f_mm(bank(b), h1T[:, g4 * 8 + kc, j * 128:(j + 1) * 128],
                                                   wr[p_ % NW][:, kc, :], g4 == 0 and kc == 0, g4 == 3 and kc == 7),
                                             waits=[th1, w_tok[p_], bw] if kc == 0 else (),
                                             sig=(kc == 7))
                            if g4 == 3:
                                tx2 = DVE.emit(_f_tt(xt[:, j, half * 512:(half + 1) * 512],
                                                     xt[:, j, half * 512:(half + 1) * 512], bank(b), ALU.add),
                                               waits=[tk])
                                brelease(b, tx2)
                        w_done(p_, tk)
                tfin = rms_rows(lambda j: xt[:, j, :], g_fin, lambda j: xt[:, j, :], tx2)
                r0 = q0 + mt * MT
                ty = POOL.dma(jb["y"][r0:r0 + MT, :].rearrange("(j p) d -> p j d", p=128), xt, y_sem[s], waits=[tfin])
                xt_free[s] = ty
                ytoks.append(ty)
                if mt + 2 < n_mt:
                    xtok[mt + 2] = xload(mt + 2)
            barrier()
            return ytoks

        def prefetch_x(jb, q0, nr):
            xsrc = jb["xqr"] if jb["xqr"] is not None else jb["xk"]
            toks = {}
            for mt in range(min(2, nr // MT)):
                s_ = mt % 2
                r0 = q0 + mt * MT
                toks[mt] = SP.dma(arena_xt(s_), xsrc[r0:r0 + MT, :].rearrange("(j p) d -> p j d", p=128),
                                  x_sem_p[s_], waits=[xt_free_p[s_]])
            x_pref[(jb["name"], q0)] = toks

        def arena_xt(s_):
            bpx = Bump()
            bpx.off = XT_OFF + s_ * 4 * D * 4
            return bpx.get([128, 4, D], F32)

        all_y = []
        pending_y = []
        for jb in jobs:
            S = jb["S"]
            if pending_y:
                barrier(pending_y)
                pending_y = []
            if jb["xqr"] is None:
                phase1(jb["xk"], S // MT, True, True, jb["cosk"], jb["sink"], jb, 0)
            else:
                phase1(jb["xk"], S // MT, True, False, jb["cosk"], jb["sink"], jb, 0)
                phase1(jb["xqr"], jb["nq"] // MT, False, True, None, None, jb, 0)
            nr = min(NR, jb["nq"])
            for q0 in range(0, jb["nq"], nr):
                prefetch_x(jb, q0, nr)
                phase2(jb, q0, nr // NQ)
                ys_ = phase3(jb, q0, nr)
                all_y += ys_
                pending_y = ys_[-2:]
        barrier(all_y)

        with nc.Block() as block:
            @block.sync
            def _(e):
                SP.replay(e)

            @block.tensor
            def _(e):
                PE.replay(e)

            @block.vector
            def _(e):
                DVE.replay(e)

            @block.scalar
            def _(e):
                ACT.replay(e)

            @block.gpsimd
            def _(e):
                POOL.replay(e)
    return nc


def _rope_tables(n):
    pos = np.arange(n, dtype=np.float32)
    inv = (np.float32(ROPE_BASE) ** (-np.arange(0, RD, 2, dtype=np.float32) / np.float32(RD))).astype(np.float32)
    ang = pos[:, None] * inv[None, :]
    ang = np.concatenate([ang, ang], axis=-1)
    cos = np.cos(ang).astype(np.float32)
    sin = np.sin(ang).astype(np.float32)
    sgn = np.concatenate([-np.ones(RD // 2, np.float32), np.ones(RD // 2, np.float32)])
    return np.ascontiguousarray(cos.T), np.ascontiguousarray((sin * sgn[None, :]).T)


def run(cfg, n_cores, x_prompt, x_sample, w):
    nc = build(cfg)
    smax = max(cfg.ss, cfg.sp)
    cosT, sinT = _rope_tables(smax)
    in_maps = []
    for c in range(n_cores):
        q0 = c * cfg.nqp
        m = {
            "xs": np.ascontiguousarray(x_sample[c * cfg.nseq:(c + 1) * cfg.nseq].reshape(cfg.nseq * cfg.ss, D)),
            "xp": x_prompt,
            "xq": np.ascontiguousarray(x_prompt[q0:q0 + cfg.nqp]),
            "cos_all": cosT, "sin_all": sinT,
            "cos_q": np.ascontiguousarray(cosT[:, q0:q0 + cfg.nqp]),
            "sin_q": np.ascontiguousarray(sinT[:, q0:q0 + cfg.nqp]),
        }
        m.update(w)
        in_maps.append(m)
    res = run_bass_kernel_spmd(nc, in_maps, core_ids=list(range(n_cores)))
    ysam = np.stack([r["ys"].reshape(cfg.nseq, cfg.ss, D) for r in res.results], 0).reshape(-1, cfg.ss, D)
    yp = np.concatenate([r["yq"] for r in res.results], 0)
    return yp, ysam


def _weights(norm_mix_g, w_in, q_norm_g, w_uq, kv_norm_g, w_ukv, sgu_norm_g, w_s, b_s, w_o, norm_ffn_g,
             w_ff1, w_ff2, final_norm_g):
    f = lambda a: np.ascontiguousarray(np.asarray(a, dtype=np.float32))
    return {
        "norm_mix_g": f(norm_mix_g[0]), "w_in": f(w_in[0]), "q_norm_g": f(q_norm_g[0]), "w_uq": f(w_uq[0]),
        "kv_norm_g": f(kv_norm_g[0]), "w_ukv": f(w_ukv[0]), "sgu_norm_g": f(sgu_norm_g[0]), "w_s": f(w_s[0]),
        "b_s": f(b_s[0]).reshape(-1), "w_o": f(w_o[0]), "norm_ffn_g": f(norm_ffn_g[0]), "w_ff1": f(w_ff1[0]),
        "w_ff2": f(w_ff2[0]), "final_norm_g": f(final_norm_g),
    }


def kernel(x_prompt, x_sample, norm_mix_g, w_in, q_norm_g, w_uq, kv_norm_g, w_ukv, sgu_norm_g, w_s, b_s, w_o,
           norm_ffn_g, w_ff1, w_ff2, final_norm_g):
    w = _weights(norm_mix_g, w_in, q_norm_g, w_uq, kv_norm_g, w_ukv, sgu_norm_g, w_s, b_s, w_o, norm_ffn_g,
                 w_ff1, w_ff2, final_norm_g)
    xp = np.ascontiguousarray(np.asarray(x_prompt, dtype=np.float32)[0])
    xsam = np.ascontiguousarray(np.asarray(x_sample, dtype=np.float32))
    yp, ysam = run(FULL, 8, xp, xsam, w)
    return (yp.reshape(1, FULL.sp, D).astype(np.float32), ysam.reshape(16, FULL.ss, D).astype(np.float32))
```

```python
import numpy as np
from contextlib import ExitStack
import concourse.bass as bass
import concourse.mybir as mybir
from concourse.bass_utils import run_bass_kernel_spmd

F32 = mybir.dt.float32
BF16 = mybir.dt.bfloat16
ALU = mybir.AluOpType
AF = mybir.ActivationFunctionType

D = 1024
NH = 8
QL = 384
KVL = 256
RD = 64
DFF = 4096
INC = 4800
QKD = 192
SCALE = float(QKD ** -0.5)
EPS = 1e-6
ROPE_BASE = 10000.0
GC1 = 0.044715
GC2 = 1.5957691216057308
MT = 512
NQ = 1024
NR = 2048
KC = 1024
U0, V0, GA0, GB0 = 704, 1728, 2752, 3776


class DSem:
    def __init__(self, sem):
        self.sem = sem
        self.n = 0


class Eng:
    def __init__(self, name, sem):
        self.name = name
        self.sem = sem
        self.n = 0
        self.q = []
        self.waited = {}

    def _filter(self, waits):
        out = []
        stack = list(waits) if isinstance(waits, (list, tuple)) and not _is_tok(waits) else [waits]
        while stack:
            w = stack.pop()
            if w is None:
                continue
            if _is_tok(w):
                s, v = w
                k = id(s)
                if self.waited.get(k, 0) >= v:
                    continue
                self.waited[k] = v
                out.append((s, v))
            else:
                stack.extend(w)
        return out

    def emit(self, fn, waits=(), sig=True):
        ws = self._filter(waits)
        if sig:
            self.n += 1
        self.q.append((fn, ws, self.sem if sig else None, 1))
        return (self.sem, self.n)

    def dma(self, out, in_, dsem, waits=()):
        ws = self._filter(waits)
        dsem.n += 16
        self.q.append((_f_dma(out, in_), ws, dsem.sem, 16))
        return (dsem.sem, dsem.n)

    def wait_only(self, waits):
        ws = self._filter(waits)
        if ws:
            self.q.append((None, ws, None, 0))

    def tok(self):
        return (self.sem, self.n) if self.n > 0 else None

    def replay(self, e):
        for fn, ws, sem, inc in self.q:
            for s, v in ws:
                e.wait_ge(s, v)
            if fn is None:
                continue
            ins = fn(e)
            if sem is not None:
                ins.then_inc(sem, inc)


def _is_tok(w):
    return isinstance(w, tuple) and len(w) == 2 and isinstance(w[1], int)


def _f_dma(out, in_):
    return lambda e: e.dma_start(out=out, in_=in_)


def _f_mm(out, lhsT, rhs, start, stop):
    return lambda e: e.matmul(out, lhsT=lhsT, rhs=rhs, start=start, stop=stop)


def _f_tr(out, in_, ident):
    return lambda e: e.transpose(out=out, in_=in_, identity=ident)


def _f_act(out, in_, func, scale=1.0, bias=0.0):
    return lambda e: e.activation(out=out, in_=in_, func=func, scale=scale, bias=bias)


def _f_act_acc(out, in_, func, accum_out):
    return lambda e: e.activation(out=out, in_=in_, func=func, accum_out=accum_out)


def _f_copy(out, in_):
    return lambda e: e.tensor_copy(out=out, in_=in_)


def _f_tt(out, in0, in1, op):
    return lambda e: e.tensor_tensor(out=out, in0=in0, in1=in1, op=op)


def _f_ts(out, in0, s1, s2, op0, op1):
    return lambda e: e.tensor_scalar(out=out, in0=in0, scalar1=s1, scalar2=s2, op0=op0, op1=op1)


def _f_stt(out, in0, scalar, in1, op0, op1, accum_out=None):
    if accum_out is None:
        return lambda e: e.scalar_tensor_tensor(out=out, in0=in0, scalar=scalar, in1=in1, op0=op0, op1=op1)
    return lambda e: e.scalar_tensor_tensor(out=out, in0=in0, scalar=scalar, in1=in1, op0=op0, op1=op1,
                                            accum_out=accum_out)


def _f_recip(out, in_):
    return lambda e: e.reciprocal(out=out, in_=in_)


def _f_memset(ap, v):
    return lambda e: e.memset(ap, v)


class Cfg:
    def __init__(self, nseq, ss, sp, nqp):
        self.nseq = nseq
        self.ss = ss
        self.sp = sp
        self.nqp = nqp


FULL = Cfg(2, 4096, 16384, 2048)


def build(cfg):
    nc = bass.Bass("TRN2", target_bir_lowering=False)
    SMAX = max(cfg.ss, cfg.sp)

    def din(name, shape, dt=F32):
        return nc.dram_tensor(name, list(shape), dt, kind="ExternalInput").ap()

    def dscr(name, shape, dt=BF16):
        return nc.dram_tensor(name, list(shape), dt).ap()

    xs = din("xs", [cfg.nseq * cfg.ss, D])
    xp = din("xp", [cfg.sp, D])
    xq = din("xq", [cfg.nqp, D])
    cos_all = din("cos_all", [RD, SMAX])
    sin_all = din("sin_all", [RD, SMAX])
    cos_q = din("cos_q", [RD, cfg.nqp])
    sin_q = din("sin_q", [RD, cfg.nqp])
    norm_mix_g = din("norm_mix_g", [D])
    w_in = din("w_in", [D, INC])
    q_norm_g = din("q_norm_g", [QL])
    w_uq = din("w_uq", [QL, NH * QKD])
    kv_norm_g = din("kv_norm_g", [KVL])
    w_ukv = din("w_ukv", [KVL, NH * 256])
    sgu_norm_g = din("sgu_norm_g", [D])
    w_s = din("w_s", [8, 128, 128])
    b_s = din("b_s", [8 * 128])
    w_o = din("w_o", [D, D])
    norm_ffn_g = din("norm_ffn_g", [D])
    w_ff1 = din("w_ff1", [D, DFF])
    w_ff2 = din("w_ff2", [DFF, D])
    final_norm_g = din("final_norm_g", [D])
    ys = nc.dram_tensor("ys", [cfg.nseq * cfg.ss, D], F32, kind="ExternalOutput").ap()
    yq = nc.dram_tensor("yq", [cfg.nqp, D], F32, kind="ExternalOutput").ap()

    win_bf = dscr("win_bf", [D, INC])
    wlat_bf = dscr("wlat_bf", [D, 768])
    wuq_bf = dscr("wuq_bf", [QL, NH * QKD])
    wuqp_bf = dscr("wuqp_bf", [QL, NH * RD])
    wuqr_d = dscr("wuqr_d", [128, 3, NH, 128])
    wuqpr_d = dscr("wuqpr_d", [128, 3, NH, 128])
    wuk_bf = dscr("wuk_bf", [KVL, NH * 128])
    wuv_bf = dscr("wuv_bf", [KVL, NH * 128])
    wo_bf = dscr("wo_bf", [D, D])
    wff1_bf = dscr("wff1_bf", [D, DFF])
    wff2_bf = dscr("wff2_bf", [DFF, D])

    jobs = []
    for s in range(cfg.nseq):
        jobs.append(dict(name=f"s{s}", xk=xs[s * cfg.ss:(s + 1) * cfg.ss, :], S=cfg.ss, xqr=None, nq=cfg.ss,
                         y=ys[s * cfg.ss:(s + 1) * cfg.ss, :], cosk=cos_all, sink=sin_all,
                         cosq=cos_all, sinq=sin_all))
    jobs.append(dict(name="p", xk=xp, S=cfg.sp, xqr=xq, nq=cfg.nqp, y=yq, cosk=cos_all, sink=sin_all,
                     cosq=cos_q, sinq=sin_q))
    for jb in jobs:
        S = jb["S"]
        jb["kT"] = dscr("kT_" + jb["name"], [NH, 128, S])
        jb["krT"] = dscr("krT_" + jb["name"], [RD, S])
        jb["v"] = dscr("v_" + jb["name"], [NH, S // KC, 128, KC // 128, 128])
        jb["cq"] = dscr("cq_" + jb["name"], [3, 128, jb["nq"]])

    with ExitStack() as st:
        def sb(name, shape, dt):
            return st.enter_context(nc.sbuf_tensor(name, list(shape), dt))

        def mksem(name):
            return st.enter_context(nc.semaphore(name))

        PE = Eng("pe", mksem("s_pe"))
        ACT = Eng("act", mksem("s_act"))
        DVE = Eng("dve", mksem("s_dve"))
        POOL = Eng("pool", mksem("s_pool"))
        SP = Eng("sp", mksem("s_sp"))
        ENGS = [PE, ACT, DVE, POOL, SP]
        dsem_pool = {"hw": [], "sw": []}
        dsem_idx = {"hw": 0, "sw": 0}

        def new_dsem(kind="hw"):
            pool = dsem_pool[kind]
            if dsem_idx[kind] == len(pool):
                pool.append(DSem(mksem(f"d{kind}{len(pool)}")))
            d = pool[dsem_idx[kind]]
            dsem_idx[kind] += 1
            return d

        ps = st.enter_context(nc.psum_tensor("ps", [128, 8, 512], F32))

        def bank(b):
            return ps[:, b, :]

        def bank2(b):
            return ps[:, b:b + 2, :]

        def bank_bf(b):
            return ps[:, b, :].bitcast(BF16)

        ident = sb("ident", [128, 128], BF16)
        identf = sb("identf", [128, 128], F32)
        ones_bf = sb("ones_bf", [128, 128], BF16)
        ones_f = sb("ones_f", [128, 128], F32)
        mhalf = sb("mhalf", [128, 4], F32)
        eps_col = sb("eps_col", [128, 1], F32)
        g_mix = sb("g_mix", [128, D], F32)
        g_ffn = sb("g_ffn", [128, D], F32)
        g_fin = sb("g_fin", [128, D], F32)
        g_sgu = sb("g_sgu", [128, D], F32)
        bs_bc = sb("bs_bc", [128, 8, 128], F32)
        wsT = sb("wsT", [128, 8, 128], BF16)
        gq_col = sb("gq_col", [128, 3], F32)
        gkv_col = sb("gkv_col", [128, 2], F32)
        ss4 = sb("ss4", [128, 4], F32)
        t4 = sb("t4", [128, 4], F32)
        rstd4 = sb("rstd4", [128, 4], F32)
        oaT = sb("oaT", [128, NH, NR], BF16)
        ARENA = 147 * 1024 // 2
        arena = sb("arena", [128, ARENA], BF16)

        XT_OFF = ARENA * 2 - 2 * 4 * D * 4
        x_sem_p = [DSem(mksem("xs0")), DSem(mksem("xs1"))]
        y_sem_p = [DSem(mksem("ys0")), DSem(mksem("ys1"))]
        xt_free_p = [None, None]
        x_pref = {}

        class Bump:
            def __init__(self):
                self.off = 0

            def get(self, shape, dt, parts=128):
                n = 1
                for s_ in shape[1:]:
                    n *= s_
                nb = n * (4 if dt == F32 else 2)
                nb = (nb + 63) // 64 * 64
                a = self.off // 2
                self.off += nb
                assert self.off <= ARENA * 2, ("arena overflow", self.off)
                v = arena[0:shape[0], a:a + nb // 2]
                if dt == F32:
                    v = v.bitcast(F32)
                v = v[:, 0:n]
                if len(shape) == 3:
                    v = v.rearrange("p (a b) -> p a b", b=shape[2])
                elif len(shape) == 4:
                    v = v.rearrange("p (a b c) -> p a b c", b=shape[2], c=shape[3])
                return v

        bank_free = {b: None for b in range(8)}
        bank_order = list(range(8))
        bank_busy = set()

        def balloc(allowed=None):
            for b in bank_order:
                if b in bank_busy:
                    continue
                if allowed is not None and b not in allowed:
                    continue
                bank_order.remove(b)
                bank_order.append(b)
                bank_busy.add(b)
                return b, bank_free[b]
            raise RuntimeError("no free psum bank")

        def brelease(b, tok):
            bank_free[b] = tok
            bank_busy.discard(b)

        def barrier(extra=()):
            toks = [e.tok() for e in ENGS] + list(extra)
            for e in ENGS:
                e.wait_only(toks)
            dsem_idx["hw"] = 0
            dsem_idx["sw"] = 0

        def mm_group(out, pairs, waits=(), first_start=True, last_stop=True):
            n = len(pairs)
            tok = None
            for i, (l, r) in enumerate(pairs):
                tok = PE.emit(_f_mm(out, l, r, (i == 0) and first_start, (i == n - 1) and last_stop),
                              waits=waits if i == 0 else (), sig=(i == n - 1))
            return tok

        pend_store = []
        c_tok = []
        bp = Bump()
        NCS = 4
        stf = [bp.get([128, INC], F32) for _ in range(NCS)]
        stb = [bp.get([128, INC], BF16) for _ in range(NCS)]
        wsf = bp.get([128, 8, 128], F32)
        ld_sem = [new_dsem() for _ in range(NCS)]
        stq_sem = [new_dsem("sw") for _ in range(NCS)]
        csem = new_dsem()

        c_tok.append(SP.dma(g_mix[:], norm_mix_g.partition_broadcast(128), csem))
        c_tok.append(SP.dma(g_ffn[:], norm_ffn_g.partition_broadcast(128), csem))
        c_tok.append(SP.dma(g_fin[:], final_norm_g.partition_broadcast(128), csem))
        c_tok.append(SP.dma(g_sgu[:], sgu_norm_g.partition_broadcast(128), csem))
        c_tok.append(SP.dma(bs_bc[:].rearrange("p a b -> p (a b)"), b_s.partition_broadcast(128), csem))
        for k in range(3):
            c_tok.append(SP.dma(gq_col[:, k:k + 1], q_norm_g[k * 128:(k + 1) * 128].rearrange("(p o) -> p o", o=1),
                                csem))
        for k in range(2):
            c_tok.append(SP.dma(gkv_col[:, k:k + 1], kv_norm_g[k * 128:(k + 1) * 128].rearrange("(p o) -> p o", o=1),
                                csem))
        c_tok.append(SP.dma(wsf, w_s.rearrange("g p q -> p g q"), csem))
        const_ready = c_tok[-1]

        i0 = POOL.emit(_f_memset(identf[:], 0.0))
        i1 = POOL.emit(lambda e: e.affine_select(out=identf[:], in_=identf[:], compare_op=ALU.not_equal, fill=1.0,
                                                 base=0, pattern=[[-1, 128]], channel_multiplier=1), waits=[i0])
        i2 = POOL.emit(_f_copy(ident[:], identf[:]), waits=[i1])
        POOL.emit(_f_memset(ones_bf[:], 1.0))
        POOL.emit(_f_memset(ones_f[:], 1.0))
        POOL.emit(_f_memset(mhalf[:], -0.5))
        POOL.emit(_f_memset(eps_col[:], EPS))
        pool_consts = POOL.emit(_f_memset(ss4[:], 0.0))
        for half in range(2):
            b, bw = balloc()
            tk = None
            for gg in range(4):
                g = half * 4 + gg
                tk = PE.emit(_f_tr(bank(b)[:, gg * 128:(gg + 1) * 128], wsf[:, g, :], identf[:]),
                             waits=[const_ready, i1, bw])
            tk2 = ACT.emit(_f_act(wsT[:, half * 4:(half + 1) * 4, :].rearrange("p a b -> p (a b)"), bank(b),
                                  AF.Copy), waits=[tk])
            brelease(b, tk2)
        ws_ready = ACT.tok()

        conv_i = [0]
        slot_free = [None] * NCS
        slot_conv = [None] * NCS
        conv_engs = [DVE, ACT]

        def convert(src, ncols, stores, c3=None):
            i = conv_i[0]
            conv_i[0] += 1
            s = i % NCS
            dstv = stf[s][:, 0:ncols]
            if c3 is not None:
                dstv = dstv.rearrange("p (a c) -> p a c", c=c3)
            tl = SP.dma(dstv, src, ld_sem[s], waits=[slot_conv[s]])
            eng = conv_engs[i % 2]
            if eng is ACT:
                tcv = ACT.emit(_f_act(stb[s][:, 0:ncols], stf[s][:, 0:ncols], AF.Copy), waits=[tl, slot_free[s]])
            else:
                tcv = eng.emit(_f_copy(stb[s][:, 0:ncols], stf[s][:, 0:ncols]), waits=[tl, slot_free[s]])
            slot_conv[s] = tcv
            tks = []
            for dst, vf in stores:
                tks.append(POOL.dma(dst, vf(stb[s]), stq_sem[s], waits=[tcv]))
            slot_free[s] = tks[-1]
            pend_store.append(tks[-1])

        for kc in range(8):
            r = slice(kc * 128, (kc + 1) * 128)
            convert(w_in[r, :], INC, [
                (win_bf[r, :], lambda t: t[:, 0:INC]),
                (wlat_bf[r, 0:704], lambda t: t[:, 0:704]),
                (wlat_bf[r, 704:736], lambda t: t[:, 672:704]),
                (wlat_bf[r, 736:768], lambda t: t[:, 640:672]),
            ])
        for kc in range(3):
            r = slice(kc * 128, (kc + 1) * 128)
            hv = lambda t: t[:, 0:NH * QKD].rearrange("p (h d) -> p h d", d=QKD)
            dup = []
            for half in range(2):
                o_ = half * 64
                dup.append((wuqr_d[:, kc, :, o_:o_ + 64], lambda t: hv(t)[:, :, 128:192]))
                dup.append((wuqpr_d[:, kc, :, o_:o_ + 32], lambda t: hv(t)[:, :, 160:192]))
                dup.append((wuqpr_d[:, kc, :, o_ + 32:o_ + 64], lambda t: hv(t)[:, :, 128:160]))
            convert(w_uq[r, :], NH * QKD, dup + [
                (wuq_bf[r, :], lambda t: t[:, 0:NH * QKD]),
                (wuqp_bf[r, :].rearrange("p (h d) -> p h d", d=RD)[:, :, 0:32],
                 lambda t: t[:, 0:NH * QKD].rearrange("p (h d) -> p h d", d=QKD)[:, :, 160:192]),
                (wuqp_bf[r, :].rearrange("p (h d) -> p h d", d=RD)[:, :, 32:64],
                 lambda t: t[:, 0:NH * QKD].rearrange("p (h d) -> p h d", d=QKD)[:, :, 128:160]),
            ])
        for kc in range(2):
            r = slice(kc * 128, (kc + 1) * 128)
            convert(w_ukv[r, :], NH * 256, [
                (wuk_bf[r, :].rearrange("p (h d) -> p h d", d=128),
                 lambda t: t[:, 0:NH * 256].rearrange("p (h d) -> p h d", d=256)[:, :, 0:128]),
                (wuv_bf[r, :].rearrange("p (h d) -> p h d", d=128),
                 lambda t: t[:, 0:NH * 256].rearrange("p (h d) -> p h d", d=256)[:, :, 128:256]),
            ])
        for kc in range(8):
            r = slice(kc * 128, (kc + 1) * 128)
            convert(w_o[r, :], D, [(wo_bf[r, :], lambda t: t[:, 0:D])])
        for kc in range(8):
            r = slice(kc * 128, (kc + 1) * 128)
            convert(w_ff1[r, :], DFF, [(wff1_bf[r, :], lambda t: t[:, 0:DFF])])
        for k4 in range(8):
            r = slice(k4 * 512, (k4 + 1) * 512)
            convert(w_ff2[r, :].rearrange("(a p) c -> p a c", p=128), 4 * D,
                    [(wff2_bf[r, :].rearrange("(a p) c -> p a c", p=128),
                      lambda t: t[:, 0:4 * D].rearrange("p (a c) -> p a c", c=D))], c3=D)
        wconv_done = list(slot_free)
        barrier(wconv_done + [const_ready])

        def norm_transpose(xt, x_ready, g_bc, xn2, hnT, st_tok):
            junk = st_tok["junk"]
            tks = []
            for j in range(4):
                tks.append(ACT.emit(_f_act_acc(junk, xt[:, j, :], AF.Square, ss4[:, j:j + 1]),
                                    waits=[x_ready, st_tok.get("ss_free")]))
            t1 = DVE.emit(_f_ts(t4[:], ss4[:], 1.0 / D, EPS, ALU.mult, ALU.add), waits=[tks[-1]])
            st_tok["ss_free"] = t1
            t2 = POOL.emit(_f_tt(rstd4[:], t4[:], mhalf[:, 0:4], ALU.pow), waits=[t1])
            done = []
            for j in range(4):
                s = j % 2
                tn = DVE.emit(_f_stt(xn2[s], xt[:, j, :], rstd4[:, j:j + 1], g_bc[:], ALU.mult, ALU.mult),
                              waits=[t2, st_tok["xn_free"][s]])
                b, bw = balloc()
                tk = None
                for kc in range(8):
                    tk = PE.emit(_f_tr(bank_bf(b)[:, kc * 128:(kc + 1) * 128], xn2[s][:, kc * 128:(kc + 1) * 128],
                                       ident[:]), waits=[tn, bw] if kc == 0 else (), sig=(kc == 7))
                st_tok["xn_free"][s] = tk
                te = ACT.emit(_f_act(hnT[:, :, j * 128:(j + 1) * 128],
                                     bank_bf(b).rearrange("p (k t) -> p k t", t=128), AF.Copy),
                              waits=[tk, st_tok.get("hnT_free")])
                brelease(b, te)
                done.append(te)
            return done[-1]

        def phase1(xsrc, n_mt, do_kv, do_q, cosk, sink, jb, qcol0):
            bp = Bump()
            wlat = bp.get([128, 8, 768], BF16)
            wuk = bp.get([128, 2, 1024], BF16)
            wuv = bp.get([128, 2, 1024], BF16)
            xt2 = [bp.get([128, 4, D], F32), bp.get([128, 4, D], F32)]
            xn2 = [bp.get([128, D], BF16), bp.get([128, D], BF16)]
            junk = bp.get([128, D], BF16)
            hnT2 = [bp.get([128, 8, MT], BF16), bp.get([128, 8, MT], BF16)]
            kst2 = [bp.get([128, NH, MT], BF16), bp.get([128, NH, MT], BF16)]
            vst2 = [bp.get([128, NH, 4, 128], BF16), bp.get([128, NH, 4, 128], BF16)]
            krst2 = [bp.get([64, MT], BF16, parts=64), bp.get([64, MT], BF16, parts=64)]
            ckvn2 = [bp.get([128, 2, MT], BF16), bp.get([128, 2, MT], BF16)]
            cqst2 = [bp.get([128, 3, MT], BF16), bp.get([128, 3, MT], BF16)]
            sqb = [bp.get([128, MT], BF16) for _ in range(3)]
            tf = [bp.get([128, MT], F32) for _ in range(4)]
            cs2 = [bp.get([64, 2, MT], F32), bp.get([64, 2, MT], F32)]

            wsem = new_dsem()
            SP.dma(wlat, wlat_bf.rearrange("(kc p) c -> p kc c", p=128), wsem)
            SP.dma(wuk, wuk_bf.rearrange("(kc p) c -> p kc c", p=128), wsem)
            w_ready = SP.dma(wuv, wuv_bf.rearrange("(kc p) c -> p kc c", p=128), wsem)

            x_sem = [new_dsem(), new_dsem()]
            cs_sem = [new_dsem(), new_dsem()]
            st_sem = [new_dsem("sw"), new_dsem("sw")]
            stt_ = {"junk": junk, "xn_free": [None, None]}
            xt_free = [None, None]
            cs_free = [None, None]
            hn_free = [None, None]
            stage_free = [None, None]
            ckvn_free = [None, None]
            stores = []
            A = {}

            def load(i):
                s = i % 2
                r0 = i * MT
                tx = SP.dma(xt2[s], xsrc[r0:r0 + MT, :].rearrange("(j p) d -> p j d", p=128), x_sem[s],
                            waits=[xt_free[s]])
                tc_ = None
                if do_kv:
                    SP.dma(cs2[s][:, 0, :], cosk[:, r0:r0 + MT], cs_sem[s], waits=[cs_free[s]])
                    tc_ = SP.dma(cs2[s][:, 1, :], sink[:, r0:r0 + MT], cs_sem[s])
                A[i] = dict(tx=tx, tcs=tc_)

            def stageA(i):
                s = i % 2
                a = A[i]
                stt_["hnT_free"] = hn_free[s]
                th = norm_transpose(xt2[s], a["tx"], g_mix, xn2, hnT2[s], stt_)
                xt_free[s] = DVE.tok()
                yield
                hnT = hnT2[s]
                last_pe = None
                if do_kv:
                    cb = []
                    for m in range(2):
                        b, bw = balloc()
                        tk = mm_group(bank(b), [(wlat[:, kc, QL + m * 128:QL + (m + 1) * 128], hnT[:, kc, :])
                                                for kc in range(8)], waits=[th, bw, w_ready])
                        cb.append((b, tk))
                    rb = []
                    for m in range(2):
                        b, bw = balloc()
                        tk = mm_group(bank(b)[0:64, :], [(wlat[:, kc, 640 + m * 64:640 + (m + 1) * 64], hnT[:, kc, :])
                                                          for kc in range(8)], waits=[bw])
                        rb.append((b, tk))
                    tsq = []
                    for m in range(2):
                        tsq.append(ACT.emit(_f_act(sqb[m], bank(cb[m][0]), AF.Square), waits=[cb[m][1]]))
                    yield
                    b, bw = balloc()
                    tss = mm_group(bank(b), [(ones_bf[:], sqb[m]) for m in range(2)], waits=[tsq[-1], bw])
                    tt = ACT.emit(_f_act(tf[0], bank(b), AF.Ln, scale=1.0 / KVL, bias=eps_col[:, 0:1]), waits=[tss])
                    brelease(b, tt)
                    tr_ = ACT.emit(_f_act(tf[1], tf[0], AF.Exp, scale=-0.5), waits=[tt, stt_.get("rstd_free")])
                    tcn = None
                    for m in range(2):
                        tcn = DVE.emit(_f_stt(ckvn2[s][:, m, :], bank(cb[m][0]), gkv_col[:, m:m + 1], tf[1],
                                              ALU.mult, ALU.mult), waits=[tr_, ckvn_free[s]])
                        brelease(cb[m][0], tcn)
                    a["ckvn"] = tcn
                    stt_["rstd_free"] = tcn
                    r1 = DVE.emit(_f_tt(tf[2][0:64, :], bank(rb[0][0])[0:64, :], cs2[s][:, 0, :], ALU.mult),
                                  waits=[rb[0][1], a["tcs"], stt_.get("kr_pool")])
                    brelease(rb[0][0], r1)
                    r2 = DVE.emit(_f_tt(tf[3][0:64, :], bank(rb[1][0])[0:64, :], cs2[s][:, 1, :], ALU.mult),
                                  waits=[rb[1][1]])
                    brelease(rb[1][0], r2)
                    cs_free[s] = r2
                    r3 = POOL.emit(_f_tt(krst2[s], tf[2][0:64, :], tf[3][0:64, :], ALU.add),
                                   waits=[r1, r2, stage_free[s]])
                    a["kr"] = r3
                    stt_["kr_pool"] = r3
                    last_pe = tss
                if do_q:
                    qb = []
                    for m in range(3):
                        b, bw = balloc()
                        tk = mm_group(bank(b), [(wlat[:, kc, m * 128:(m + 1) * 128], hnT[:, kc, :])
                                                for kc in range(8)], waits=[th, bw, w_ready])
                        qb.append((b, tk))
                    tsq = []
                    for m in range(3):
                        tsq.append(ACT.emit(_f_act(sqb[m], bank(qb[m][0]), AF.Square), waits=[qb[m][1], last_pe]))
                    b, bw = balloc()
                    tss = mm_group(bank(b), [(ones_bf[:], sqb[m]) for m in range(3)], waits=[tsq[-1], bw])
                    tt = ACT.emit(_f_act(tf[0], bank(b), AF.Ln, scale=1.0 / QL, bias=eps_col[:, 0:1]), waits=[tss])
                    brelease(b, tt)
                    tr_ = ACT.emit(_f_act(tf[1], tf[0], AF.Exp, scale=-0.5), waits=[tt, stt_.get("rstd_free")])
                    tcq = None
                    for m in range(3):
                        tcq = DVE.emit(_f_stt(cqst2[s][:, m, :], bank(qb[m][0]), gq_col[:, m:m + 1], tf[1],
                                              ALU.mult, ALU.mult), waits=[tr_, stage_free[s]])
                        brelease(qb[m][0], tcq)
                    a["cq"] = tcq
                    stt_["rstd_free"] = tcq
                    last_pe = tss
                hn_free[s] = PE.tok()

            def stageB(i):
                s = i % 2
                a = A[i]
                r0 = i * MT
                stks = []
                if do_kv:
                    ck = ckvn2[s]
                    evs = [ACT, DVE]
                    te = None
                    for h in range(NH):
                        b, bw = balloc()
                        tk = mm_group(bank(b), [(wuk[:, m, h * 128:(h + 1) * 128], ck[:, m, :]) for m in range(2)],
                                      waits=[a["ckvn"], bw])
                        if h % 2 == 0:
                            te = ACT.emit(_f_act(kst2[s][:, h, :], bank(b), AF.Copy), waits=[tk, stage_free[s]])
                        else:
                            te = DVE.emit(_f_copy(kst2[s][:, h, :], bank(b)), waits=[tk, stage_free[s]])
                        brelease(b, te)
                    tkA, tkD = ACT.tok(), DVE.tok()
                    stks.append(POOL.dma(jb["kT"][:, :, r0:r0 + MT].rearrange("h d t -> d h t"), kst2[s], st_sem[s],
                                         waits=[tkA, tkD]))
                    stks.append(POOL.dma(jb["krT"][:, r0:r0 + MT], krst2[s], st_sem[s], waits=[a["kr"]]))
                    yield
                    for j in range(4):
                        for half in range(2):
                            b, bw = balloc()
                            tk = mm_group(bank(b), [(ck[:, m, j * 128:(j + 1) * 128],
                                                     wuv[:, m, half * 512:(half + 1) * 512]) for m in range(2)],
                                          waits=[bw])
                            dst = vst2[s][:, half * 4:(half + 1) * 4, j, :]
                            src = bank(b).rearrange("p (h d) -> p h d", d=128)
                            if (j + half) % 2 == 0:
                                te = ACT.emit(_f_act(dst, src, AF.Copy), waits=[tk])
                            else:
                                te = DVE.emit(_f_copy(dst, src), waits=[tk])
                            brelease(b, te)
                    tkA, tkD = ACT.tok(), DVE.tok()
                    ckvn_free[s] = PE.tok()
                    c = r0 // KC
                    kb0 = (r0 % KC) // 128
                    stks.append(POOL.dma(jb["v"][:, c, :, kb0:kb0 + 4, :].rearrange("h p k d -> p h k d"), vst2[s],
                                         st_sem[s], waits=[tkA, tkD]))
                if do_q:
                    stks.append(POOL.dma(jb["cq"][:, :, qcol0 + r0:qcol0 + r0 + MT].rearrange("m p t -> p m t"),
                                         cqst2[s], st_sem[s], waits=[a["cq"]]))
                stage_free[s] = stks[-1]
                stores.append(stks[-1])
                yield

            def drive(gens):
                gens = [g for g in gens if g is not None]
                while gens:
                    for g in list(gens):
                        try:
                            next(g)
                        except StopIteration:
                            gens.remove(g)

            load(0)
            if n_mt > 1:
                load(1)
            drive([stageA(0)])
            for i in range(n_mt):
                ga = stageA(i + 1) if i + 1 < n_mt else None
                if i + 2 < n_mt:
                    load(i + 2)
                drive([ga, stageB(i)])
            barrier(stores[-2:])
            return stores[-2:]

        def phase2(jb, q0, npass):
            S = jb["S"]
            NQT = npass * NQ
            VH = [(ps_, h_) for ps_ in range(npass) for h_ in range(NH)]
            bp = Bump()
            wuq = bp.get([128, 3, NH * QKD], BF16)
            wuq_r = bp.get([128, 3, NH, 128], BF16)
            wuqp_r = bp.get([128, 3, NH, 128], BF16)
            cqn = bp.get([128, 3, NQT], BF16)
            csq = bp.get([128, 2, NQT], F32)
            qn2 = [bp.get([128, NQ], BF16), bp.get([128, NQ], BF16)]
            qr2 = [bp.get([128, NQ], BF16), bp.get([128, NQ], BF16)]
            NKV = 4
            kn = [bp.get([128, KC], BF16) for _ in range(NKV)]
            kr = [bp.get([128, KC // 2], BF16) for _ in range(NKV)]
            vv = [bp.get([128, KC // 128, 128], BF16) for _ in range(NKV)]
            NPP = 4
            pT2 = [bp.get([128, 2, 512], BF16) for _ in range(NPP)]
            rec = [bp.get([128, 512], F32) for _ in range(2)]
            ocp = [[bp.get([128, 512], F32) for _ in range(2)] for _ in range(2)]
            accD = [[bp.get([128, 512], F32) for _ in range(2)] for _ in range(2)]
            pair = [bp.get([128, 512], BF16) for _ in range(2)]
            qtmp = [bp.get([128, 512], F32) for _ in range(2)]
            assert bp.off <= XT_OFF, bp.off

            wsem = new_dsem()
            SP.dma(wuq, wuq_bf.rearrange("(kc p) c -> p kc c", p=128), wsem)
            SP.dma(wuq_r.rearrange("p a b c -> p (a b c)"), wuqr_d.rearrange("p a b c -> p (a b c)"), wsem)
            SP.dma(wuqp_r.rearrange("p a b c -> p (a b c)"), wuqpr_d.rearrange("p a b c -> p (a b c)"), wsem)
            for half in range(2):
                SP.dma(csq[half * 64:(half + 1) * 64, 0, :], jb["cosq"][:, q0:q0 + NQT], wsem)
                SP.dma(csq[half * 64:(half + 1) * 64, 1, :], jb["sinq"][:, q0:q0 + NQT], wsem)
            w_ready = SP.dma(cqn, jb["cq"][:, :, q0:q0 + NQT].rearrange("m p t -> p m t"), wsem)

            kv_sem = [new_dsem() for _ in range(NKV)]
            kv_free = [None] * NKV
            nchunk = S // KC
            seq = [(h_, c) for (ps_, h_) in VH for c in range(nchunk)]
            kv_tok = {}

            def kv_load(idx):
                h, c = seq[idx]
                s = idx % NKV
                SP.dma(kn[s], jb["kT"][h, :, c * KC:(c + 1) * KC], kv_sem[s], waits=[kv_free[s]])
                krv = jb["krT"][:, c * KC:(c + 1) * KC].rearrange("r (j two t) -> r two j t", two=2, t=128)
                SP.dma(kr[s][0:64, :].rearrange("p (j t) -> p j t", t=128), krv[:, 0], kv_sem[s])
                SP.dma(kr[s][64:128, :].rearrange("p (j t) -> p j t", t=128), krv[:, 1], kv_sem[s])
                kv_tok[idx] = SP.dma(vv[s], jb["v"][h, c], kv_sem[s])

            for idx in range(min(NKV - 1, len(seq))):
                kv_load(idx)

            SB = [0, 1, 2, 3, 4, 5]
            OB = [6, 7]
            for b in range(8):
                bank_busy.discard(b)
            pT_free = [None] * NPP
            PAIRS = [(0, 1), (2, 3), (4, 5)]
            pair_order = list(PAIRS)

            def balloc_pair():
                for pq in pair_order:
                    if pq[0] in bank_busy or pq[1] in bank_busy:
                        continue
                    pair_order.remove(pq)
                    pair_order.append(pq)
                    for b_ in pq:
                        bank_busy.add(b_)
                        bank_order.remove(b_)
                        bank_order.append(b_)
                    return pq[0], pq[1], [bank_free[pq[0]], bank_free[pq[1]]]
                raise RuntimeError("no free psum bank pair")
            q_free = [None, None]
            o_free = [None, None]
            fin_pending = []
            fin_tok = [None, None]
            tile_ctr = [0]
            NQS = NQ // 512
            NPR = KC // 256

            def qproj_steps(vi_):
                ps_, h = VH[vi_]
                s = vi_ % 2
                res = {"toks": []}
                steps = []

                def mk(qs, kind):
                    cols = slice(qs * 512, (qs + 1) * 512)
                    gcols = slice(ps_ * NQ + qs * 512, ps_ * NQ + (qs + 1) * 512)

                    def nope():
                        b, bw = balloc(SB)
                        tk = mm_group(bank(b), [(wuq[:, m, h * QKD:h * QKD + 128], cqn[:, m, gcols])
                                                for m in range(3)], waits=[w_ready, bw])
                        te = ACT.emit(_f_act(qn2[s][:, cols], bank(b), AF.Copy), waits=[tk, q_free[s]])
                        brelease(b, te)
                        res["toks"].append(te)

                    def ropea():
                        b1, bw1 = balloc(SB)
                        tk1 = mm_group(bank(b1), [(wuq_r[:, m, h, :], cqn[:, m, gcols]) for m in range(3)],
                                       waits=[w_ready, bw1])
                        r1 = DVE.emit(_f_tt(qtmp[0], bank(b1), csq[:, 0, gcols], ALU.mult),
                                      waits=[tk1, qproj.last_pool])
                        brelease(b1, r1)
                        res["r1"] = r1

                    def ropeb():
                        b2, bw2 = balloc(SB)
                        tk2 = mm_group(bank(b2), [(wuqp_r[:, m, h, :], cqn[:, m, gcols]) for m in range(3)],
                                       waits=[w_ready, bw2])
                        r2 = DVE.emit(_f_tt(qtmp[1], bank(b2), csq[:, 1, gcols], ALU.mult),
                                      waits=[tk2, qproj.last_pool])
                        brelease(b2, r2)
                        tq = POOL.emit(_f_tt(qr2[s][:, cols], qtmp[0], qtmp[1], ALU.add),
                                       waits=[res["r1"], r2, q_free[s]])
                        qproj.last_pool = tq
                        res["toks"].append(tq)

                    return {"nope": nope, "ropea": ropea, "ropeb": ropeb}[kind]

                for qs in range(NQS):
                    for kind in ("nope", "ropea", "ropeb"):
                        steps.append(mk(qs, kind))
                return steps, res

            def qproj(vi_):
                steps, res = qproj_steps(vi_)
                for f_ in steps:
                    f_()
                return res["toks"]

            qproj.last_pool = None
            qtoks = {0: qproj(0)}

            def finalize(hh, oc0, aD, tD, oc, tO):
                for qs in range(NQS):
                    ocols = slice(oc0 + qs * 512, oc0 + (qs + 1) * 512)
                    b, bw = balloc(SB)
                    tr_ = PE.emit(_f_mm(bank(b), ones_f[:], aD[qs], True, True), waits=[tD[qs], bw])
                    t1 = DVE.emit(_f_recip(rec[qs], bank(b)), waits=[tr_])
                    brelease(b, t1)
                    fin_tok[hp_of[(hh, oc0)]] = DVE.emit(_f_tt(oaT[:, hh, ocols], oc[qs], rec[qs], ALU.mult),
                                               waits=[t1, tO[qs]])

            hp_of = {}
            for vi, (ps, h) in enumerate(VH):
                s = vi % 2
                hp = vi % 2
                oc0 = ps * NQ
                hp_of[(h, oc0)] = hp
                qt = qtoks[vi]
                items = [(c, pr, qs) for c in range(nchunk) for pr in range(NPR) for qs in range(NQS)]
                nit = len(items)
                s_tok = {}
                accD_tok = [None] * NQS
                lastpv = [None] * NQS
                qsteps, qres = [], None

                def issue_S(ii):
                    c, pr, qs = items[ii]
                    li = vi * nchunk + c
                    ks = li % NKV
                    cols = slice(qs * 512, (qs + 1) * 512)
                    bA, bB, bwAB = balloc_pair()
                    bwA, bwB = bwAB, None
                    kA, kB = 2 * pr, 2 * pr + 1
                    PE.emit(_f_mm(bank(bA), kn[ks][:, kA * 128:(kA + 1) * 128], qn2[s][:, cols], True, False),
                            waits=[kv_tok[li], qt, bwA], sig=False)
                    PE.emit(_f_mm(bank(bB), kn[ks][:, kB * 128:(kB + 1) * 128], qn2[s][:, cols], True, False),
                            waits=[bwB], sig=False)
                    PE.emit(_f_mm(bank(bA), kr[ks][0:64, pr * 128:(pr + 1) * 128], qr2[s][0:64, cols], False, True),
                            sig=False)
                    tk = PE.emit(_f_mm(bank(bB), kr[ks][64:128, pr * 128:(pr + 1) * 128], qr2[s][64:128, cols],
                                       False, True))
                    s_tok[ii] = (bA, bB, tk)

                issue_S(0)
                for ii in range(nit):
                    c, pr, qs = items[ii]
                    li = vi * nchunk + c
                    ks = li % NKV
                    if ii + 1 < nit:
                        issue_S(ii + 1)
                    if ii == 5 and fin_pending:
                        finalize(*fin_pending.pop())
                    if ii == nit // 2 and vi + 1 < len(VH):
                        qsteps, qres = qproj_steps(vi + 1)
                    if vi + 1 < len(VH) and ii >= nit // 2 and qsteps:
                        qsteps.pop(0)()
                        if not qsteps:
                            qtoks[vi + 1] = qres["toks"]
                    bA, bB, tk = s_tok.pop(ii)
                    first = (c == 0 and pr == 0)
                    last = (c == nchunk - 1) and (pr == NPR - 1)
                    g = tile_ctr[0]
                    tile_ctr[0] += 1
                    pp = g % NPP
                    te = ACT.emit(_f_act(pT2[pp], bank2(bA), AF.Exp, scale=SCALE), waits=[tk, pT_free[pp]])
                    brelease(bA, te)
                    brelease(bB, te)
                    tps = []
                    for t_, kb in enumerate((2 * pr, 2 * pr + 1)):
                        st_ = first and t_ == 0
                        en_ = last and t_ == 1
                        tp = PE.emit(_f_mm(bank(OB[qs]), vv[ks][:, kb, :], pT2[pp][:, t_, :], st_, en_),
                                     waits=[te, o_free[qs]] if st_ else [te])
                        tps.append(tp)
                    lastpv[qs] = tps[1]
                    tpair = DVE.emit(_f_tt(pair[qs], pT2[pp][:, 0, :], pT2[pp][:, 1, :], ALU.add),
                                     waits=[te, accD_tok[qs]])
                    if accD_tok[qs] is None:
                        ta = DVE.emit(_f_copy(accD[hp][qs], pair[qs]), waits=[tpair, fin_tok[hp]])
                    else:
                        ta = DVE.emit(_f_tt(accD[hp][qs], accD[hp][qs], pair[qs], ALU.add), waits=[tpair])
                    accD_tok[qs] = ta
                    pT_free[pp] = [tps[1], tpair]
                    if pr == NPR - 1 and qs == NQS - 1:
                        kv_free[ks] = tps[1]
                        nxt = li + NKV - 1
                        if nxt < len(seq) and nxt not in kv_tok:
                            kv_load(nxt)
                while qsteps:
                    qsteps.pop(0)()
                    if not qsteps:
                        qtoks[vi + 1] = qres["toks"]
                tO = []
                for qs in range(NQS):
                    c2 = ACT.emit(_f_act(ocp[hp][qs], bank(OB[qs]), AF.Copy), waits=[lastpv[qs], fin_tok[hp]])
                    o_free[qs] = c2
                    tO.append(c2)
                fin_pending.append((h, oc0, accD[hp], accD_tok, ocp[hp], tO))
                if vi == len(VH) - 1:
                    while fin_pending:
                        finalize(*fin_pending.pop(0))
                q_free[s] = PE.tok()
            barrier()

        def phase3(jb, q0, nr):
            xsrc = jb["xqr"] if jb["xqr"] is not None else jb["xk"]
            bp = Bump()
            bp.off = XT_OFF
            xt2 = [bp.get([128, 4, D], F32), bp.get([128, 4, D], F32)]
            bp.off = 0
            xn2 = [bp.get([128, D], BF16), bp.get([128, D], BF16)]
            junk = bp.get([128, D], BF16)
            hnTs = [bp.get([128, 8, MT], BF16), bp.get([128, 8, MT], BF16)]
            NW = 5
            wr = [bp.get([128, 8, 512], BF16) for _ in range(NW)]
            tf = [bp.get([128, 512], F32) for _ in range(6)]
            ma4 = bp.get([128, 4, 512], F32)
            u_off = bp.off
            vg = bp.get([128, D], F32)
            vsn = bp.get([128, 4, D], BF16)
            uT = bp.get([128, 8, MT], BF16)
            end1 = bp.off
            bp.off = u_off
            h1T = bp.get([128, 32, MT], BF16)
            bp.off = max(bp.off, end1)
            assert bp.off <= XT_OFF, bp.off

            n_mt = nr // MT
            x_sem = x_sem_p
            y_sem = y_sem_p
            w_sem = [new_dsem() for _ in range(NW)]
            w_free = [None] * NW
            xt_free = xt_free_p
            stt_ = {"junk": junk, "xn_free": [None, None], "hnT_free": None}
            hn_free = [None, None]
            ytoks = []

            def piece_list():
                L = []
                for i in range(2):
                    L.append(("v", win_bf[:, V0 + i * 512:V0 + (i + 1) * 512]))
                for i in range(2):
                    L.append(("u", win_bf[:, U0 + i * 512:U0 + (i + 1) * 512]))
                for i in range(2):
                    L.append(("ga", win_bf[:, GA0 + i * 512:GA0 + (i + 1) * 512]))
                    L.append(("gb", win_bf[:, GB0 + i * 512:GB0 + (i + 1) * 512]))
                for i in range(2):
                    L.append(("wo", wo_bf[:, i * 512:(i + 1) * 512]))
                for i in range(8):
                    L.append(("f1", wff1_bf[:, i * 512:(i + 1) * 512]))
                for half in range(2):
                    for g4 in range(4):
                        L.append(("f2", wff2_bf[g4 * 1024:(g4 + 1) * 1024, half * 512:(half + 1) * 512]))
                return L

            pieces = []
            for mt in range(n_mt):
                pieces += piece_list()
            w_tok = {}
            w_next = [0]

            def w_issue():
                i = w_next[0]
                if i >= len(pieces):
                    return
                s = i % NW
                w_tok[i] = SP.dma(wr[s], pieces[i][1].rearrange("(kc p) c -> p kc c", p=128), w_sem[s],
                                  waits=[w_free[s]])
                w_next[0] += 1

            def w_done(i, tok):
                w_free[i % NW] = tok
                w_issue()

            def xload(mt):
                s = mt % 2
                r0 = q0 + mt * MT
                return SP.dma(xt2[s], xsrc[r0:r0 + MT, :].rearrange("(j p) d -> p j d", p=128), x_sem[s],
                              waits=[xt_free[s]])

            xtok = {}
            if (jb["name"], q0) in x_pref:
                xtok = x_pref.pop((jb["name"], q0))
            if 0 not in xtok:
                xtok[0] = xload(0)
            for _ in range(NW - 1):
                w_issue()
            if n_mt > 1 and 1 not in xtok:
                xtok[1] = xload(1)
            w_issue()
            th_next = None

            def rms_rows(src3, g_bc, dst_fn, wait):
                tks = []
                for j in range(4):
                    tks.append(ACT.emit(_f_act_acc(junk, src3(j), AF.Square, ss4[:, j:j + 1]),
                                        waits=[wait, stt_.get("ss_free")]))
                t1 = DVE.emit(_f_ts(t4[:], ss4[:], 1.0 / D, EPS, ALU.mult, ALU.add), waits=[tks[-1]])
                stt_["ss_free"] = t1
                t2 = POOL.emit(_f_tt(rstd4[:], t4[:], mhalf[:, 0:4], ALU.pow), waits=[t1])
                tk = None
                for j in range(4):
                    tk = DVE.emit(_f_stt(dst_fn(j), src3(j), rstd4[:, j:j + 1], g_bc[:], ALU.mult, ALU.mult),
                                  waits=[t2])
                return tk

            tf_free = [None] * 6
            ma4_free = [None] * 4

            def gelu_from_psum(b, tk, out_ap, k):
                a1 = ACT.emit(_f_act(tf[k], bank(b), AF.Square), waits=[tk, tf_free[k]])
                d1 = DVE.emit(_f_ts(tf[k], tf[k], GC1, 1.0, ALU.mult, ALU.add), waits=[a1])
                d2 = DVE.emit(_f_tt(tf[k], tf[k], bank(b), ALU.mult), waits=[d1])
                a2 = ACT.emit(_f_act(tf[k + 1], tf[k], AF.Sigmoid, scale=GC2), waits=[d2, tf_free[k + 1]])
                d3 = DVE.emit(_f_tt(out_ap, tf[k + 1], bank(b), ALU.mult), waits=[a2])
                tf_free[k] = a2
                tf_free[k + 1] = d3
                return d3

            pi = [0]
            for mt in range(n_mt):
                s = mt % 2
                xt = xt2[s]
                qc = slice(mt * MT, (mt + 1) * MT)
                hnT = hnTs[mt % 2]
                if th_next is None:
                    stt_["hnT_free"] = hn_free[mt % 2]
                    th = norm_transpose(xt, xtok[mt], g_mix, xn2, hnT, stt_)
                else:
                    th = th_next
                    th_next = None
                pv0, pv1 = pi[0], pi[0] + 1
                pi[0] += 2
                last = None
                for j in range(4):
                    for half in range(2):
                        p_ = pv0 + half
                        b, bw = balloc()
                        tk = mm_group(bank(b), [(hnT[:, kc, j * 128:(j + 1) * 128], wr[p_ % NW][:, kc, :])
                                                for kc in range(8)], waits=[th, w_tok[p_], bw])
                        last = tk
                        d3 = ACT.emit(_f_act(vg[:, half * 512:(half + 1) * 512], bank(b), AF.Gelu_apprx_tanh),
                                      waits=[tk, stt_.get("vg_free")])
                        brelease(b, d3)
                    tsq = ACT.emit(_f_act_acc(junk, vg[:], AF.Square, ss4[:, 0:1]), waits=[d3, stt_.get("ss_free")])
                    t1 = DVE.emit(_f_ts(t4[:, 0:1], ss4[:, 0:1], 1.0 / D, EPS, ALU.mult, ALU.add), waits=[tsq])
                    stt_["ss_free"] = t1
                    t2 = POOL.emit(_f_tt(rstd4[:, 0:1], t4[:, 0:1], mhalf[:, 0:1], ALU.pow), waits=[t1])
                    tvs = DVE.emit(_f_stt(vsn[:, j, :], vg[:], rstd4[:, 0:1], g_sgu[:], ALU.mult, ALU.mult),
                                   waits=[t2])
                    stt_["vg_free"] = tvs
                w_done(pv0, last)
                w_done(pv1, last)
                for i in range(2):
                    p_ = pi[0]
                    pi[0] += 1
                    for mm in range(4):
                        m = i * 4 + mm
                        b, bw = balloc()
                        tk = mm_group(bank(b), [(wr[p_ % NW][:, kc, mm * 128:(mm + 1) * 128], hnT[:, kc, :])
                                                for kc in range(8)], waits=[w_tok[p_], bw])
                        d3 = ACT.emit(_f_act(uT[:, m, :], bank(b), AF.Gelu_apprx_tanh), waits=[tk])
                        brelease(b, d3)
                    w_done(p_, tk)
                tu = d3
                for g in range(8):
                    b, bw = balloc()
                    tk = None
                    for j in range(4):
                        tk = PE.emit(_f_mm(bank(b)[:, j * 128:(j + 1) * 128], vsn[:, j, g * 128:(g + 1) * 128],
                                           wsT[:, g, :], True, True), waits=[tvs, bw] if j == 0 else ())
                    k = 4 + (g % 2)
                    d1 = DVE.emit(_f_tt(tf[k].rearrange("p (j t) -> p j t", t=128),
                                        bank(b).rearrange("p (j t) -> p j t", t=128),
                                        bs_bc[:, g:g + 1, :].broadcast_to([128, 4, 128]), ALU.add),
                                  waits=[tk, tf_free[k]])
                    brelease(b, d1)
                    tob = DVE.emit(_f_tt(uT[:, g, :], uT[:, g, :], tf[k], ALU.mult), waits=[d1, tu])
                    tf_free[k] = tob
                for i in range(2):
                    pa, pb_ = pi[0], pi[0] + 1
                    pi[0] += 2
                    for mm in range(4):
                        m = i * 4 + mm
                        b, bw = balloc()
                        tk = mm_group(bank(b), [(wr[pa % NW][:, kc, mm * 128:(mm + 1) * 128], hnT[:, kc, :])
                                                for kc in range(8)], waits=[w_tok[pa], bw])
                        k = mm % 2
                        a1 = ACT.emit(_f_act(tf[k], bank(b), AF.Sigmoid), waits=[tk, tf_free[k]])
                        brelease(b, a1)
                        d1 = POOL.emit(_f_tt(ma4[:, mm, :], tf[k], oaT[:, m, qc], ALU.mult),
                                       waits=[a1, ma4_free[mm]])
                        tf_free[k] = d1
                        ma_tok = d1
                    w_done(pa, tk)
                    for mm in range(4):
                        m = i * 4 + mm
                        b, bw = balloc()
                        tk = mm_group(bank(b), [(wr[pb_ % NW][:, kc, mm * 128:(mm + 1) * 128], hnT[:, kc, :])
                                                for kc in range(8)], waits=[w_tok[pb_], bw])
                        k = 2 + mm % 2
                        a1 = ACT.emit(_f_act(tf[k], bank(b), AF.Sigmoid), waits=[tk, tf_free[k]])
                        brelease(b, a1)
                        d1 = DVE.emit(_f_tt(tf[k], tf[k], uT[:, m, :], ALU.mult), waits=[a1, tob])
                        tmg = DVE.emit(_f_tt(uT[:, m, :], tf[k], ma4[:, mm, :], ALU.add), waits=[d1, ma_tok])
                        tf_free[k] = tmg
                        ma4_free[mm] = tmg
                    w_done(pb_, tk)
                hn_free[mt % 2] = PE.tok()
                for half in range(2):
                    p_ = pi[0]
                    pi[0] += 1
                    for j in range(4):
                        b, bw = balloc()
                        tk = mm_group(bank(b), [(uT[:, kc, j * 128:(j + 1) * 128], wr[p_ % NW][:, kc, :])
                                                for kc in range(8)], waits=[tmg, w_tok[p_], bw])
                        tx1 = DVE.emit(_f_tt(xt[:, j, half * 512:(half + 1) * 512],
                                             xt[:, j, half * 512:(half + 1) * 512], bank(b), ALU.add), waits=[tk])
                        brelease(b, tx1)
                    w_done(p_, tk)
                stt_["hnT_free"] = hn_free[mt % 2]
                th2 = norm_transpose(xt, tx1, g_ffn, xn2, hnT, stt_)
                for i in range(8):
                    p_ = pi[0]
                    pi[0] += 1
                    for mm in range(4):
                        m = i * 4 + mm
                        b, bw = balloc()
                        tk = mm_group(bank(b), [(wr[p_ % NW][:, kc, mm * 128:(mm + 1) * 128], hnT[:, kc, :])
                                                for kc in range(8)], waits=[th2, w_tok[p_], bw])
                        k = m % 4
                        a1 = ACT.emit(_f_act(tf[k], bank(b), AF.Relu), waits=[tk, tf_free[k]])
                        brelease(b, a1)
                        th1 = POOL.emit(_f_tt(h1T[:, m, :], tf[k], tf[k], ALU.mult), waits=[a1])
                        tf_free[k] = th1
                    w_done(p_, tk)
                hn_free[mt % 2] = PE.tok()
                if mt + 2 < n_mt and (mt + 2) not in xtok:
                    pass
                if mt + 1 < n_mt:
                    stt_["hnT_free"] = hn_free[(mt + 1) % 2]
                    th_next = norm_transpose(xt2[(mt + 1) % 2], xtok[mt + 1], g_mix, xn2, hnTs[(mt + 1) % 2], stt_)
                for half in range(2):
                    bks = []
                    for j in range(4):
                        b, bw = balloc()
                        bks.append((b, bw))
                    tk = None
                    for g4 in range(4):
                        p_ = pi[0]
                        pi[0] += 1
                        for j in range(4):
                            b, bw = bks[j]
                            for kc in range(8):
                                tk = PE.emit(_f_mm(bank(b), h1T[:, g4 * 8 + kc, j * 128:(j + 1) * 128],
                                                   wr[p_ % NW][:, kc, :], g4 == 0 and kc == 0, g4 == 3 and kc == 7),
                                             waits=[th1, w_tok[p_], bw] if kc == 0 else (),
                                             sig=(kc == 7))
                            if g4 == 3:
                                tx2 = DVE.emit(_f_tt(xt[:, j, half * 512:(half + 1) * 512],
                                                     xt[:, j, half * 512:(half + 1) * 512], bank(b), ALU.add),
                                               waits=[tk])
                                brelease(b, tx2)
                        w_done(p_, tk)
                tfin = rms_rows(lambda j: xt[:, j, :], g_fin, lambda j: xt[:, j, :], tx2)
                r0 = q0 + mt * MT
                ty = POOL.dma(jb["y"][r0:r0 + MT, :].rearrange("(j p) d -> p j d", p=128), xt, y_sem[s], waits=[tfin])
                xt_free[s] = ty
                ytoks.append(ty)
                if mt + 2 < n_mt:
                    xtok[mt + 2] = xload(mt + 2)
            barrier()
            return ytoks

        def prefetch_x(jb, q0, nr):
            xsrc = jb["xqr"] if jb["xqr"] is not None else jb["xk"]
            toks = {}
            for mt in range(min(2, nr // MT)):
                s_ = mt % 2
                r0 = q0 + mt * MT
                toks[mt] = SP.dma(arena_xt(s_), xsrc[r0:r0 + MT, :].rearrange("(j p) d -> p j d", p=128),
                                  x_sem_p[s_], waits=[xt_free_p[s_]])
            x_pref[(jb["name"], q0)] = toks

        def arena_xt(s_):
            bpx = Bump()
            bpx.off = XT_OFF + s_ * 4 * D * 4
            return bpx.get([128, 4, D], F32)

        all_y = []
        pending_y = []
        for jb in jobs:
            S = jb["S"]
            if pending_y:
                barrier(pending_y)
                pending_y = []
            if jb["xqr"] is None:
                phase1(jb["xk"], S // MT, True, True, jb["cosk"], jb["sink"], jb, 0)
            else:
                phase1(jb["xk"], S // MT, True, False, jb["cosk"], jb["sink"], jb, 0)
                phase1(jb["xqr"], jb["nq"] // MT, False, True, None, None, jb, 0)
            nr = min(NR, jb["nq"])
            for q0 in range(0, jb["nq"], nr):
                prefetch_x(jb, q0, nr)
                phase2(jb, q0, nr // NQ)
                ys_ = phase3(jb, q0, nr)
                all_y += ys_
                pending_y = ys_[-2:]
        barrier(all_y)

        with nc.Block() as block:
            @block.sync
            def _(e):
                SP.replay(e)

            @block.tensor
            def _(e):
                PE.replay(e)

            @block.vector
            def _(e):
                DVE.replay(e)

            @block.scalar
            def _(e):
                ACT.replay(e)

            @block.gpsimd
            def _(e):
                POOL.replay(e)
    return nc


def _rope_tables(n):
    pos = np.arange(n, dtype=np.float32)
    inv = (np.float32(ROPE_BASE) ** (-np.arange(0, RD, 2, dtype=np.float32) / np.float32(RD))).astype(np.float32)
    ang = pos[:, None] * inv[None, :]
    ang = np.concatenate([ang, ang], axis=-1)
    cos = np.cos(ang).astype(np.float32)
    sin = np.sin(ang).astype(np.float32)
    sgn = np.concatenate([-np.ones(RD // 2, np.float32), np.ones(RD // 2, np.float32)])
    return np.ascontiguousarray(cos.T), np.ascontiguousarray((sin * sgn[None, :]).T)


def run(cfg, n_cores, x_prompt, x_sample, w):
    nc = build(cfg)
    smax = max(cfg.ss, cfg.sp)
    cosT, sinT = _rope_tables(smax)
    in_maps = []
    for c in range(n_cores):
        q0 = c * cfg.nqp
        m = {
            "xs": np.ascontiguousarray(x_sample[c * cfg.nseq:(c + 1) * cfg.nseq].reshape(cfg.nseq * cfg.ss, D)),
            "xp": x_prompt,
            "xq": np.ascontiguousarray(x_prompt[q0:q0 + cfg.nqp]),
            "cos_all": cosT, "sin_all": sinT,
            "cos_q": np.ascontiguousarray(cosT[:, q0:q0 + cfg.nqp]),
            "sin_q": np.ascontiguousarray(sinT[:, q0:q0 + cfg.nqp]),
        }
        m.update(w)
        in_maps.append(m)
    res = run_bass_kernel_spmd(nc, in_maps, core_ids=list(range(n_cores)))
    ysam = np.stack([r["ys"].reshape(cfg.nseq, cfg.ss, D) for r in res.results], 0).reshape(-1, cfg.ss, D)
    yp = np.concatenate([r["yq"] for r in res.results], 0)
    return yp, ysam


def _weights(norm_mix_g, w_in, q_norm_g, w_uq, kv_norm_g, w_ukv, sgu_norm_g, w_s, b_s, w_o, norm_ffn_g,
             w_ff1, w_ff2, final_norm_g):
    f = lambda a: np.ascontiguousarray(np.asarray(a, dtype=np.float32))
    return {
        "norm_mix_g": f(norm_mix_g[0]), "w_in": f(w_in[0]), "q_norm_g": f(q_norm_g[0]), "w_uq": f(w_uq[0]),
        "kv_norm_g": f(kv_norm_g[0]), "w_ukv": f(w_ukv[0]), "sgu_norm_g": f(sgu_norm_g[0]), "w_s": f(w_s[0]),
        "b_s": f(b_s[0]).reshape(-1), "w_o": f(w_o[0]), "norm_ffn_g": f(norm_ffn_g[0]), "w_ff1": f(w_ff1[0]),
        "w_ff2": f(w_ff2[0]), "final_norm_g": f(final_norm_g),
    }


def kernel(x_prompt, x_sample, norm_mix_g, w_in, q_norm_g, w_uq, kv_norm_g, w_ukv, sgu_norm_g, w_s, b_s, w_o,
           norm_ffn_g, w_ff1, w_ff2, final_norm_g):
    w = _weights(norm_mix_g, w_in, q_norm_g, w_uq, kv_norm_g, w_ukv, sgu_norm_g, w_s, b_s, w_o, norm_ffn_g,
                 w_ff1, w_ff2, final_norm_g)
    xp = np.ascontiguousarray(np.asarray(x_prompt, dtype=np.float32)[0])
    xsam = np.ascontiguousarray(np.asarray(x_sample, dtype=np.float32))
    yp, ysam = run(FULL, 8, xp, xsam, w)
    return (yp.reshape(1, FULL.sp, D).astype(np.float32), ysam.reshape(16, FULL.ss, D).astype(np.float32))
```

```python
import numpy as np
from contextlib import ExitStack
import concourse.bass as bass
import concourse.mybir as mybir
from concourse.bass_utils import run_bass_kernel_spmd

F32 = mybir.dt.float32
BF16 = mybir.dt.bfloat16
ALU = mybir.AluOpType
AF = mybir.ActivationFunctionType

D = 1024
NH = 8
QL = 384
KVL = 256
RD = 64
DFF = 4096
INC = 4800
QKD = 192
SCALE = float(QKD ** -0.5)
EPS = 1e-6
ROPE_BASE = 10000.0
GC1 = 0.044715
GC2 = 1.5957691216057308
MT = 512
NQ = 1024
NR = 2048
KC = 1024
U0, V0, GA0, GB0 = 704, 1728, 2752, 3776


class DSem:
    def __init__(self, sem):
        self.sem = sem
        self.n = 0


class Eng:
    def __init__(self, name, sem):
        self.name = name
        self.sem = sem
        self.n = 0
        self.q = []
        self.waited = {}

    def _filter(self, waits):
        out = []
        stack = list(waits) if isinstance(waits, (list, tuple)) and not _is_tok(waits) else [waits]
        while stack:
            w = stack.pop()
            if w is None:
                continue
            if _is_tok(w):
                s, v = w
                k = id(s)
                if self.waited.get(k, 0) >= v:
                    continue
                self.waited[k] = v
                out.append((s, v))
            else:
                stack.extend(w)
        return out

    def emit(self, fn, waits=(), sig=True):
        ws = self._filter(waits)
        if sig:
            self.n += 1
        self.q.append((fn, ws, self.sem if sig else None, 1))
        return (self.sem, self.n)

    def dma(self, out, in_, dsem, waits=()):
        ws = self._filter(waits)
        dsem.n += 16
        self.q.append((_f_dma(out, in_), ws, dsem.sem, 16))
        return (dsem.sem, dsem.n)

    def wait_only(self, waits):
        ws = self._filter(waits)
        if ws:
            self.q.append((None, ws, None, 0))

    def tok(self):
        return (self.sem, self.n) if self.n > 0 else None

    def replay(self, e):
        for fn, ws, sem, inc in self.q:
            for s, v in ws:
                e.wait_ge(s, v)
            if fn is None:
                continue
            ins = fn(e)
            if sem is not None:
                ins.then_inc(sem, inc)


def _is_tok(w):
    return isinstance(w, tuple) and len(w) == 2 and isinstance(w[1], int)


def _f_dma(out, in_):
    return lambda e: e.dma_start(out=out, in_=in_)


def _f_mm(out, lhsT, rhs, start, stop):
    return lambda e: e.matmul(out, lhsT=lhsT, rhs=rhs, start=start, stop=stop)


def _f_tr(out, in_, ident):
    return lambda e: e.transpose(out=out, in_=in_, identity=ident)


def _f_act(out, in_, func, scale=1.0, bias=0.0):
    return lambda e: e.activation(out=out, in_=in_, func=func, scale=scale, bias=bias)


def _f_act_acc(out, in_, func, accum_out):
    return lambda e: e.activation(out=out, in_=in_, func=func, accum_out=accum_out)


def _f_copy(out, in_):
    return lambda e: e.tensor_copy(out=out, in_=in_)


def _f_tt(out, in0, in1, op):
    return lambda e: e.tensor_tensor(out=out, in0=in0, in1=in1, op=op)


def _f_ts(out, in0, s1, s2, op0, op1):
    return lambda e: e.tensor_scalar(out=out, in0=in0, scalar1=s1, scalar2=s2, op0=op0, op1=op1)


def _f_stt(out, in0, scalar, in1, op0, op1, accum_out=None):
    if accum_out is None:
        return lambda e: e.scalar_tensor_tensor(out=out, in0=in0, scalar=scalar, in1=in1, op0=op0, op1=op1)
    return lambda e: e.scalar_tensor_tensor(out=out, in0=in0, scalar=scalar, in1=in1, op0=op0, op1=op1,
                                            accum_out=accum_out)


def _f_recip(out, in_):
    return lambda e: e.reciprocal(out=out, in_=in_)


def _f_memset(ap, v):
    return lambda e: e.memset(ap, v)


class Cfg:
    def __init__(self, nseq, ss, sp, nqp):
        self.nseq = nseq
        self.ss = ss
        self.sp = sp
        self.nqp = nqp


FULL = Cfg(2, 4096, 16384, 2048)


def build(cfg):
    nc = bass.Bass("TRN2", target_bir_lowering=False)
    SMAX = max(cfg.ss, cfg.sp)

    def din(name, shape, dt=F32):
        return nc.dram_tensor(name, list(shape), dt, kind="ExternalInput").ap()

    def dscr(name, shape, dt=BF16):
        return nc.dram_tensor(name, list(shape), dt).ap()

    xs = din("xs", [cfg.nseq * cfg.ss, D])
    xp = din("xp", [cfg.sp, D])
    xq = din("xq", [cfg.nqp, D])
    cos_all = din("cos_all", [RD, SMAX])
    sin_all = din("sin_all", [RD, SMAX])
    cos_q = din("cos_q", [RD, cfg.nqp])
    sin_q = din("sin_q", [RD, cfg.nqp])
    norm_mix_g = din("norm_mix_g", [D])
    w_in = din("w_in", [D, INC])
    q_norm_g = din("q_norm_g", [QL])
    w_uq = din("w_uq", [QL, NH * QKD])
    kv_norm_g = din("kv_norm_g", [KVL])
    w_ukv = din("w_ukv", [KVL, NH * 256])
    sgu_norm_g = din("sgu_norm_g", [D])
    w_s = din("w_s", [8, 128, 128])
    b_s = din("b_s", [8 * 128])
    w_o = din("w_o", [D, D])
    norm_ffn_g = din("norm_ffn_g", [D])
    w_ff1 = din("w_ff1", [D, DFF])
    w_ff2 = din("w_ff2", [DFF, D])
    final_norm_g = din("final_norm_g", [D])
    ys = nc.dram_tensor("ys", [cfg.nseq * cfg.ss, D], F32, kind="ExternalOutput").ap()
    yq = nc.dram_tensor("yq", [cfg.nqp, D], F32, kind="ExternalOutput").ap()

    win_bf = dscr("win_bf", [D, INC])
    wlat_bf = dscr("wlat_bf", [D, 768])
    wuq_bf = dscr("wuq_bf", [QL, NH * QKD])
    wuqp_bf = dscr("wuqp_bf", [QL, NH * RD])
    wuqr_d = dscr("wuqr_d", [128, 3, NH, 128])
    wuqpr_d = dscr("wuqpr_d", [128, 3, NH, 128])
    wuk_bf = dscr("wuk_bf", [KVL, NH * 128])
    wuv_bf = dscr("wuv_bf", [KVL, NH * 128])
    wo_bf = dscr("wo_bf", [D, D])
    wff1_bf = dscr("wff1_bf", [D, DFF])
    wff2_bf = dscr("wff2_bf", [DFF, D])

    jobs = []
    for s in range(cfg.nseq):
        jobs.append(dict(name=f"s{s}", xk=xs[s * cfg.ss:(s + 1) * cfg.ss, :], S=cfg.ss, xqr=None, nq=cfg.ss,
                         y=ys[s * cfg.ss:(s + 1) * cfg.ss, :], cosk=cos_all, sink=sin_all,
                         cosq=cos_all, sinq=sin_all))
    jobs.append(dict(name="p", xk=xp, S=cfg.sp, xqr=xq, nq=cfg.nqp, y=yq, cosk=cos_all, sink=sin_all,
                     cosq=cos_q, sinq=sin_q))
    for jb in jobs:
        S = jb["S"]
        jb["kT"] = dscr("kT_" + jb["name"], [NH, 128, S])
        jb["krT"] = dscr("krT_" + jb["name"], [RD, S])
        jb["v"] = dscr("v_" + jb["name"], [NH, S // KC, 128, KC // 128, 128])
        jb["cq"] = dscr("cq_" + jb["name"], [3, 128, jb["nq"]])
        jb["hn"] = dscr("hn_" + jb["name"], [jb["nq"] // MT, 128, 8, MT])

    with ExitStack() as st:
        def sb(name, shape, dt):
            return st.enter_context(nc.sbuf_tensor(name, list(shape), dt))

        def mksem(name):
            return st.enter_context(nc.semaphore(name))

        PE = Eng("pe", mksem("s_pe"))
        ACT = Eng("act", mksem("s_act"))
        DVE = Eng("dve", mksem("s_dve"))
        POOL = Eng("pool", mksem("s_pool"))
        SP = Eng("sp", mksem("s_sp"))
        ENGS = [PE, ACT, DVE, POOL, SP]
        dsem_pool = {"hw": [], "sw": []}
        dsem_idx = {"hw": 0, "sw": 0}

        def new_dsem(kind="hw"):
            pool = dsem_pool[kind]
            if dsem_idx[kind] == len(pool):
                pool.append(DSem(mksem(f"d{kind}{len(pool)}")))
            d = pool[dsem_idx[kind]]
            dsem_idx[kind] += 1
            return d

        ps = st.enter_context(nc.psum_tensor("ps", [128, 8, 512], F32))

        def bank(b):
            return ps[:, b, :]

        def bank_bf(b):
            return ps[:, b, :].bitcast(BF16)

        ident = sb("ident", [128, 128], BF16)
        identf = sb("identf", [128, 128], F32)
        ones_bf = sb("ones_bf", [128, 128], BF16)
        ones_f = sb("ones_f", [128, 128], F32)
        mhalf = sb("mhalf", [128, 4], F32)
        eps_col = sb("eps_col", [128, 1], F32)
        g_mix = sb("g_mix", [128, D], F32)
        g_ffn = sb("g_ffn", [128, D], F32)
        g_fin = sb("g_fin", [128, D], F32)
        g_sgu = sb("g_sgu", [128, D], F32)
        bs_bc = sb("bs_bc", [128, 8, 128], F32)
        wsT = sb("wsT", [128, 8, 128], BF16)
        gq_col = sb("gq_col", [128, 3], F32)
        gkv_col = sb("gkv_col", [128, 2], F32)
        ss4 = sb("ss4", [128, 4], F32)
        t4 = sb("t4", [128, 4], F32)
        rstd4 = sb("rstd4", [128, 4], F32)
        oaT = sb("oaT", [128, NH, NR], BF16)
        ARENA = 147 * 1024 // 2
        arena = sb("arena", [128, ARENA], BF16)

        XT_OFF = ARENA * 2 - 2 * 4 * D * 4
        x_sem_p = [DSem(mksem("xs0")), DSem(mksem("xs1"))]
        y_sem_p = [DSem(mksem("ys0")), DSem(mksem("ys1"))]
        xt_free_p = [None, None]
        x_pref = {}

        class Bump:
            def __init__(self):
                self.off = 0

            def get(self, shape, dt, parts=128):
                n = 1
                for s_ in shape[1:]:
                    n *= s_
                nb = n * (4 if dt == F32 else 2)
                nb = (nb + 63) // 64 * 64
                a = self.off // 2
                self.off += nb
                assert self.off <= ARENA * 2, ("arena overflow", self.off)
                v = arena[0:shape[0], a:a + nb // 2]
                if dt == F32:
                    v = v.bitcast(F32)
                v = v[:, 0:n]
                if len(shape) == 3:
                    v = v.rearrange("p (a b) -> p a b", b=shape[2])
                elif len(shape) == 4:
                    v = v.rearrange("p (a b c) -> p a b c", b=shape[2], c=shape[3])
                return v

        bank_free = {b: None for b in range(8)}
        bank_order = list(range(8))
        bank_busy = set()

        def balloc(allowed=None):
            for b in bank_order:
                if b in bank_busy:
                    continue
                if allowed is not None and b not in allowed:
                    continue
                bank_order.remove(b)
                bank_order.append(b)
                bank_busy.add(b)
                return b, bank_free[b]
            raise RuntimeError("no free psum bank")

        def brelease(b, tok):
            bank_free[b] = tok
            bank_busy.discard(b)

        def barrier(extra=()):
            toks = [e.tok() for e in ENGS] + list(extra)
            for e in ENGS:
                e.wait_only(toks)
            dsem_idx["hw"] = 0
            dsem_idx["sw"] = 0

        def mm_group(out, pairs, waits=(), first_start=True, last_stop=True):
            n = len(pairs)
            tok = None
            for i, (l, r) in enumerate(pairs):
                tok = PE.emit(_f_mm(out, l, r, (i == 0) and first_start, (i == n - 1) and last_stop),
                              waits=waits if i == 0 else (), sig=(i == n - 1))
            return tok

        pend_store = []
        c_tok = []
        bp = Bump()
        NCS = 4
        stf = [bp.get([128, INC], F32) for _ in range(NCS)]
        stb = [bp.get([128, INC], BF16) for _ in range(NCS)]
        wsf = bp.get([128, 8, 128], F32)
        ld_sem = [new_dsem() for _ in range(NCS)]
        stq_sem = [new_dsem("sw") for _ in range(NCS)]
        csem = new_dsem()

        c_tok.append(SP.dma(g_mix[:], norm_mix_g.partition_broadcast(128), csem))
        c_tok.append(SP.dma(g_ffn[:], norm_ffn_g.partition_broadcast(128), csem))
        c_tok.append(SP.dma(g_fin[:], final_norm_g.partition_broadcast(128), csem))
        c_tok.append(SP.dma(g_sgu[:], sgu_norm_g.partition_broadcast(128), csem))
        c_tok.append(SP.dma(bs_bc[:].rearrange("p a b -> p (a b)"), b_s.partition_broadcast(128), csem))
        for k in range(3):
            c_tok.append(SP.dma(gq_col[:, k:k + 1], q_norm_g[k * 128:(k + 1) * 128].rearrange("(p o) -> p o", o=1),
                                csem))
        for k in range(2):
            c_tok.append(SP.dma(gkv_col[:, k:k + 1], kv_norm_g[k * 128:(k + 1) * 128].rearrange("(p o) -> p o", o=1),
                                csem))
        c_tok.append(SP.dma(wsf, w_s.rearrange("g p q -> p g q"), csem))
        const_ready = c_tok[-1]

        i0 = POOL.emit(_f_memset(identf[:], 0.0))
        i1 = POOL.emit(lambda e: e.affine_select(out=identf[:], in_=identf[:], compare_op=ALU.not_equal, fill=1.0,
                                                 base=0, pattern=[[-1, 128]], channel_multiplier=1), waits=[i0])
        i2 = POOL.emit(_f_copy(ident[:], identf[:]), waits=[i1])
        POOL.emit(_f_memset(ones_bf[:], 1.0))
        POOL.emit(_f_memset(ones_f[:], 1.0))
        POOL.emit(_f_memset(mhalf[:], -0.5))
        POOL.emit(_f_memset(eps_col[:], EPS))
        pool_consts = POOL.emit(_f_memset(ss4[:], 0.0))
        for half in range(2):
            b, bw = balloc()
            tk = None
            for gg in range(4):
                g = half * 4 + gg
                tk = PE.emit(_f_tr(bank(b)[:, gg * 128:(gg + 1) * 128], wsf[:, g, :], identf[:]),
                             waits=[const_ready, i1, bw])
            tk2 = ACT.emit(_f_act(wsT[:, half * 4:(half + 1) * 4, :].rearrange("p a b -> p (a b)"), bank(b),
                                  AF.Copy), waits=[tk])
            brelease(b, tk2)
        ws_ready = ACT.tok()

        conv_i = [0]
        slot_free = [None] * NCS
        slot_conv = [None] * NCS
        conv_engs = [DVE, ACT]

        def convert(src, ncols, stores, c3=None):
            i = conv_i[0]
            conv_i[0] += 1
            s = i % NCS
            dstv = stf[s][:, 0:ncols]
            if c3 is not None:
                dstv = dstv.rearrange("p (a c) -> p a c", c=c3)
            tl = SP.dma(dstv, src, ld_sem[s], waits=[slot_conv[s]])
            eng = conv_engs[i % 2]
            if eng is ACT:
                tcv = ACT.emit(_f_act(stb[s][:, 0:ncols], stf[s][:, 0:ncols], AF.Copy), waits=[tl, slot_free[s]])
            else:
                tcv = eng.emit(_f_copy(stb[s][:, 0:ncols], stf[s][:, 0:ncols]), waits=[tl, slot_free[s]])
            slot_conv[s] = tcv
            tks = []
            for dst, vf in stores:
                tks.append(POOL.dma(dst, vf(stb[s]), stq_sem[s], waits=[tcv]))
            slot_free[s] = tks[-1]
            pend_store.append(tks[-1])

        for kc in range(8):
            r = slice(kc * 128, (kc + 1) * 128)
            convert(w_in[r, :], INC, [
                (win_bf[r, :], lambda t: t[:, 0:INC]),
                (wlat_bf[r, 0:704], lambda t: t[:, 0:704]),
                (wlat_bf[r, 704:736], lambda t: t[:, 672:704]),
                (wlat_bf[r, 736:768], lambda t: t[:, 640:672]),
            ])
        for kc in range(3):
            r = slice(kc * 128, (kc + 1) * 128)
            hv = lambda t: t[:, 0:NH * QKD].rearrange("p (h d) -> p h d", d=QKD)
            dup = []
            for half in range(2):
                o_ = half * 64
                dup.append((wuqr_d[:, kc, :, o_:o_ + 64], lambda t: hv(t)[:, :, 128:192]))
                dup.append((wuqpr_d[:, kc, :, o_:o_ + 32], lambda t: hv(t)[:, :, 160:192]))
                dup.append((wuqpr_d[:, kc, :, o_ + 32:o_ + 64], lambda t: hv(t)[:, :, 128:160]))
            convert(w_uq[r, :], NH * QKD, dup + [
                (wuq_bf[r, :], lambda t: t[:, 0:NH * QKD]),
                (wuqp_bf[r, :].rearrange("p (h d) -> p h d", d=RD)[:, :, 0:32],
                 lambda t: t[:, 0:NH * QKD].rearrange("p (h d) -> p h d", d=QKD)[:, :, 160:192]),
                (wuqp_bf[r, :].rearrange("p (h d) -> p h d", d=RD)[:, :, 32:64],
                 lambda t: t[:, 0:NH * QKD].rearrange("p (h d) -> p h d", d=QKD)[:, :, 128:160]),
            ])
        for kc in range(2):
            r = slice(kc * 128, (kc + 1) * 128)
            convert(w_ukv[r, :], NH * 256, [
                (wuk_bf[r, :].rearrange("p (h d) -> p h d", d=128),
                 lambda t: t[:, 0:NH * 256].rearrange("p (h d) -> p h d", d=256)[:, :, 0:128]),
                (wuv_bf[r, :].rearrange("p (h d) -> p h d", d=128),
                 lambda t: t[:, 0:NH * 256].rearrange("p (h d) -> p h d", d=256)[:, :, 128:256]),
            ])
        for kc in range(8):
            r = slice(kc * 128, (kc + 1) * 128)
            convert(w_o[r, :], D, [(wo_bf[r, :], lambda t: t[:, 0:D])])
        for kc in range(8):
            r = slice(kc * 128, (kc + 1) * 128)
            convert(w_ff1[r, :], DFF, [(wff1_bf[r, :], lambda t: t[:, 0:DFF])])
        for k4 in range(8):
            r = slice(k4 * 512, (k4 + 1) * 512)
            convert(w_ff2[r, :].rearrange("(a p) c -> p a c", p=128), 4 * D,
                    [(wff2_bf[r, :].rearrange("(a p) c -> p a c", p=128),
                      lambda t: t[:, 0:4 * D].rearrange("p (a c) -> p a c", c=D))], c3=D)
        wconv_done = list(slot_free)
        barrier(wconv_done + [const_ready])

        def norm_transpose(xt, x_ready, g_bc, xn2, hnT, st_tok):
            junk = st_tok["junk"]
            tks = []
            for j in range(4):
                tks.append(ACT.emit(_f_act_acc(junk, xt[:, j, :], AF.Square, ss4[:, j:j + 1]),
                                    waits=[x_ready, st_tok.get("ss_free")]))
            t1 = DVE.emit(_f_ts(t4[:], ss4[:], 1.0 / D, EPS, ALU.mult, ALU.add), waits=[tks[-1]])
            st_tok["ss_free"] = t1
            t2 = POOL.emit(_f_tt(rstd4[:], t4[:], mhalf[:, 0:4], ALU.pow), waits=[t1])
            done = []
            for j in range(4):
                s = j % 2
                tn = DVE.emit(_f_stt(xn2[s], xt[:, j, :], rstd4[:, j:j + 1], g_bc[:], ALU.mult, ALU.mult),
                              waits=[t2, st_tok["xn_free"][s]])
                b, bw = balloc()
                tk = None
                for kc in range(8):
                    tk = PE.emit(_f_tr(bank_bf(b)[:, kc * 128:(kc + 1) * 128], xn2[s][:, kc * 128:(kc + 1) * 128],
                                       ident[:]), waits=[tn, bw] if kc == 0 else (), sig=(kc == 7))
                st_tok["xn_free"][s] = tk
                te = ACT.emit(_f_act(hnT[:, :, j * 128:(j + 1) * 128],
                                     bank_bf(b).rearrange("p (k t) -> p k t", t=128), AF.Copy),
                              waits=[tk, st_tok.get("hnT_free")])
                brelease(b, te)
                done.append(te)
            return done[-1]

        def phase1(xsrc, n_mt, do_kv, do_q, cosk, sink, jb, qcol0):
            bp = Bump()
            wlat = bp.get([128, 8, 768], BF16)
            wuk = bp.get([128, 2, 1024], BF16)
            wuv = bp.get([128, 2, 1024], BF16)
            xt2 = [bp.get([128, 4, D], F32), bp.get([128, 4, D], F32)]
            xn2 = [bp.get([128, D], BF16), bp.get([128, D], BF16)]
            junk = bp.get([128, D], BF16)
            hnT2 = [bp.get([128, 8, MT], BF16), bp.get([128, 8, MT], BF16)]
            kst2 = [bp.get([128, NH, MT], BF16), bp.get([128, NH, MT], BF16)]
            vst2 = [bp.get([128, NH, 4, 128], BF16), bp.get([128, NH, 4, 128], BF16)]
            krst2 = [bp.get([64, MT], BF16, parts=64), bp.get([64, MT], BF16, parts=64)]
            ckvn2 = [bp.get([128, 2, MT], BF16), bp.get([128, 2, MT], BF16)]
            cqst2 = [bp.get([128, 3, MT], BF16), bp.get([128, 3, MT], BF16)]
            sqb = [bp.get([128, MT], BF16) for _ in range(3)]
            tf = [bp.get([128, MT], F32) for _ in range(4)]
            cs2 = [bp.get([64, 2, MT], F32), bp.get([64, 2, MT], F32)]

            wsem = new_dsem()
            SP.dma(wlat, wlat_bf.rearrange("(kc p) c -> p kc c", p=128), wsem)
            SP.dma(wuk, wuk_bf.rearrange("(kc p) c -> p kc c", p=128), wsem)
            w_ready = SP.dma(wuv, wuv_bf.rearrange("(kc p) c -> p kc c", p=128), wsem)

            x_sem = [new_dsem(), new_dsem()]
            cs_sem = [new_dsem(), new_dsem()]
            st_sem = [new_dsem("sw"), new_dsem("sw")]
            stt_ = {"junk": junk, "xn_free": [None, None]}
            xt_free = [None, None]
            cs_free = [None, None]
            hn_free = [None, None]
            stage_free = [None, None]
            ckvn_free = [None, None]
            stores = []
            A = {}

            def load(i):
                s = i % 2
                r0 = i * MT
                tx = SP.dma(xt2[s], xsrc[r0:r0 + MT, :].rearrange("(j p) d -> p j d", p=128), x_sem[s],
                            waits=[xt_free[s]])
                tc_ = None
                if do_kv:
                    SP.dma(cs2[s][:, 0, :], cosk[:, r0:r0 + MT], cs_sem[s], waits=[cs_free[s]])
                    tc_ = SP.dma(cs2[s][:, 1, :], sink[:, r0:r0 + MT], cs_sem[s])
                A[i] = dict(tx=tx, tcs=tc_)

            def stageA(i):
                s = i % 2
                a = A[i]
                stt_["hnT_free"] = hn_free[s]
                th = norm_transpose(xt2[s], a["tx"], g_mix, xn2, hnT2[s], stt_)
                xt_free[s] = DVE.tok()
                if do_q:
                    a["hn_st"] = POOL.dma(jb["hn"][qcol0 // MT + i], hnT2[s], st_sem[s], waits=[th])
                yield
                hnT = hnT2[s]
                last_pe = None
                if do_kv:
                    cb = []
                    for m in range(2):
                        b, bw = balloc()
                        tk = mm_group(bank(b), [(wlat[:, kc, QL + m * 128:QL + (m + 1) * 128], hnT[:, kc, :])
                                                for kc in range(8)], waits=[th, bw, w_ready])
                        cb.append((b, tk))
                    rb = []
                    for m in range(2):
                        b, bw = balloc()
                        tk = mm_group(bank(b)[0:64, :], [(wlat[:, kc, 640 + m * 64:640 + (m + 1) * 64], hnT[:, kc, :])
                                                          for kc in range(8)], waits=[bw])
                        rb.append((b, tk))
                    tsq = []
                    for m in range(2):
                        tsq.append(ACT.emit(_f_act(sqb[m], bank(cb[m][0]), AF.Square), waits=[cb[m][1]]))
                    yield
                    b, bw = balloc()
                    tss = mm_group(bank(b), [(ones_bf[:], sqb[m]) for m in range(2)], waits=[tsq[-1], bw])
                    tt = ACT.emit(_f_act(tf[0], bank(b), AF.Ln, scale=1.0 / KVL, bias=eps_col[:, 0:1]), waits=[tss])
                    brelease(b, tt)
                    tr_ = ACT.emit(_f_act(tf[1], tf[0], AF.Exp, scale=-0.5), waits=[tt, stt_.get("rstd_free")])
                    tcn = None
                    for m in range(2):
                        tcn = DVE.emit(_f_stt(ckvn2[s][:, m, :], bank(cb[m][0]), gkv_col[:, m:m + 1], tf[1],
                                              ALU.mult, ALU.mult), waits=[tr_, ckvn_free[s]])
                        brelease(cb[m][0], tcn)
                    a["ckvn"] = tcn
                    stt_["rstd_free"] = tcn
                    r1 = DVE.emit(_f_tt(tf[2][0:64, :], bank(rb[0][0])[0:64, :], cs2[s][:, 0, :], ALU.mult),
                                  waits=[rb[0][1], a["tcs"], stt_.get("kr_pool")])
                    brelease(rb[0][0], r1)
                    r2 = DVE.emit(_f_tt(tf[3][0:64, :], bank(rb[1][0])[0:64, :], cs2[s][:, 1, :], ALU.mult),
                                  waits=[rb[1][1]])
                    brelease(rb[1][0], r2)
                    cs_free[s] = r2
                    r3 = POOL.emit(_f_tt(krst2[s], tf[2][0:64, :], tf[3][0:64, :], ALU.add),
                                   waits=[r1, r2, stage_free[s]])
                    a["kr"] = r3
                    stt_["kr_pool"] = r3
                    last_pe = tss
                if do_q:
                    qb = []
                    for m in range(3):
                        b, bw = balloc()
                        tk = mm_group(bank(b), [(wlat[:, kc, m * 128:(m + 1) * 128], hnT[:, kc, :])
                                                for kc in range(8)], waits=[th, bw, w_ready])
                        qb.append((b, tk))
                    tsq = []
                    for m in range(3):
                        tsq.append(ACT.emit(_f_act(sqb[m], bank(qb[m][0]), AF.Square), waits=[qb[m][1], last_pe]))
                    b, bw = balloc()
                    tss = mm_group(bank(b), [(ones_bf[:], sqb[m]) for m in range(3)], waits=[tsq[-1], bw])
                    tt = ACT.emit(_f_act(tf[0], bank(b), AF.Ln, scale=1.0 / QL, bias=eps_col[:, 0:1]), waits=[tss])
                    brelease(b, tt)
                    tr_ = ACT.emit(_f_act(tf[1], tf[0], AF.Exp, scale=-0.5), waits=[tt, stt_.get("rstd_free")])
                    tcq = None
                    for m in range(3):
                        tcq = DVE.emit(_f_stt(cqst2[s][:, m, :], bank(qb[m][0]), gq_col[:, m:m + 1], tf[1],
                                              ALU.mult, ALU.mult), waits=[tr_, stage_free[s]])
                        brelease(qb[m][0], tcq)
                    a["cq"] = tcq
                    stt_["rstd_free"] = tcq
                    last_pe = tss
                hn_free[s] = [PE.tok(), a.get("hn_st")]

            def stageB(i):
                s = i % 2
                a = A[i]
                r0 = i * MT
                stks = []
                if do_kv:
                    ck = ckvn2[s]
                    evs = [ACT, DVE]
                    te = None
                    for h in range(NH):
                        b, bw = balloc()
                        tk = mm_group(bank(b), [(wuk[:, m, h * 128:(h + 1) * 128], ck[:, m, :]) for m in range(2)],
                                      waits=[a["ckvn"], bw])
                        if h % 2 == 0:
                            te = ACT.emit(_f_act(kst2[s][:, h, :], bank(b), AF.Copy), waits=[tk, stage_free[s]])
                        else:
                            te = DVE.emit(_f_copy(kst2[s][:, h, :], bank(b)), waits=[tk, stage_free[s]])
                        brelease(b, te)
                    tkA, tkD = ACT.tok(), DVE.tok()
                    stks.append(POOL.dma(jb["kT"][:, :, r0:r0 + MT].rearrange("h d t -> d h t"), kst2[s], st_sem[s],
                                         waits=[tkA, tkD]))
                    stks.append(POOL.dma(jb["krT"][:, r0:r0 + MT], krst2[s], st_sem[s], waits=[a["kr"]]))
                    yield
                    for j in range(4):
                        for half in range(2):
                            b, bw = balloc()
                            tk = mm_group(bank(b), [(ck[:, m, j * 128:(j + 1) * 128],
                                                     wuv[:, m, half * 512:(half + 1) * 512]) for m in range(2)],
                                          waits=[bw])
                            dst = vst2[s][:, half * 4:(half + 1) * 4, j, :]
                            src = bank(b).rearrange("p (h d) -> p h d", d=128)
                            if (j + half) % 2 == 0:
                                te = ACT.emit(_f_act(dst, src, AF.Copy), waits=[tk])
                            else:
                                te = DVE.emit(_f_copy(dst, src), waits=[tk])
                            brelease(b, te)
                    tkA, tkD = ACT.tok(), DVE.tok()
                    ckvn_free[s] = PE.tok()
                    c = r0 // KC
                    kb0 = (r0 % KC) // 128
                    stks.append(POOL.dma(jb["v"][:, c, :, kb0:kb0 + 4, :].rearrange("h p k d -> p h k d"), vst2[s],
                                         st_sem[s], waits=[tkA, tkD]))
                if do_q:
                    stks.append(POOL.dma(jb["cq"][:, :, qcol0 + r0:qcol0 + r0 + MT].rearrange("m p t -> p m t"),
                                         cqst2[s], st_sem[s], waits=[a["cq"]]))
                stage_free[s] = stks[-1]
                stores.append(stks[-1])
                yield

            def drive(gens):
                gens = [g for g in gens if g is not None]
                while gens:
                    for g in list(gens):
                        try:
                            next(g)
                        except StopIteration:
                            gens.remove(g)

            load(0)
            if n_mt > 1:
                load(1)
            drive([stageA(0)])
            for i in range(n_mt):
                ga = stageA(i + 1) if i + 1 < n_mt else None
                if i + 2 < n_mt:
                    load(i + 2)
                drive([ga, stageB(i)])
            barrier(stores[-2:])
            return stores[-2:]

        def phase2(jb, q0, npass):
            S = jb["S"]
            NQT = npass * NQ
            VH = [(ps_, h_) for ps_ in range(npass) for h_ in range(NH)]
            bp = Bump()
            wuq = bp.get([128, 3, NH * QKD], BF16)
            wuq_r = bp.get([128, 3, NH, 128], BF16)
            wuqp_r = bp.get([128, 3, NH, 128], BF16)
            cqn = bp.get([128, 3, NQT], BF16)
            csq = bp.get([128, 2, NQT], F32)
            qn2 = [bp.get([128, NQ], BF16), bp.get([128, NQ], BF16)]
            qr2 = [bp.get([128, NQ], BF16), bp.get([128, NQ], BF16)]
            NKV = 4
            kn = [bp.get([128, KC], BF16) for _ in range(NKV)]
            kr = [bp.get([128, KC // 2], BF16) for _ in range(NKV)]
            vv = [bp.get([128, KC // 128, 128], BF16) for _ in range(NKV)]
            NPB = 8
            pT = [bp.get([128, 512], BF16) for _ in range(NPB)]
            rec = [bp.get([128, 512], F32) for _ in range(2)]
            ocp = [[bp.get([128, 512], F32) for _ in range(2)] for _ in range(2)]
            accD = [[bp.get([128, 512], F32) for _ in range(2)] for _ in range(2)]
            pair = [bp.get([128, 512], BF16) for _ in range(2)]
            qtmp = [bp.get([128, 512], F32) for _ in range(2)]
            assert bp.off <= XT_OFF, bp.off

            wsem = new_dsem()
            SP.dma(wuq, wuq_bf.rearrange("(kc p) c -> p kc c", p=128), wsem)
            SP.dma(wuq_r.rearrange("p a b c -> p (a b c)"), wuqr_d.rearrange("p a b c -> p (a b c)"), wsem)
            SP.dma(wuqp_r.rearrange("p a b c -> p (a b c)"), wuqpr_d.rearrange("p a b c -> p (a b c)"), wsem)
            for half in range(2):
                SP.dma(csq[half * 64:(half + 1) * 64, 0, :], jb["cosq"][:, q0:q0 + NQT], wsem)
                SP.dma(csq[half * 64:(half + 1) * 64, 1, :], jb["sinq"][:, q0:q0 + NQT], wsem)
            w_ready = SP.dma(cqn, jb["cq"][:, :, q0:q0 + NQT].rearrange("m p t -> p m t"), wsem)

            kv_sem = [new_dsem() for _ in range(NKV)]
            kv_free = [None] * NKV
            nchunk = S // KC
            seq = [(h_, c) for (ps_, h_) in VH for c in range(nchunk)]
            kv_tok = {}

            def kv_load(idx):
                h, c = seq[idx]
                s = idx % NKV
                SP.dma(kn[s], jb["kT"][h, :, c * KC:(c + 1) * KC], kv_sem[s], waits=[kv_free[s]])
                krv = jb["krT"][:, c * KC:(c + 1) * KC].rearrange("r (j two t) -> r two j t", two=2, t=128)
                SP.dma(kr[s][0:64, :].rearrange("p (j t) -> p j t", t=128), krv[:, 0], kv_sem[s])
                SP.dma(kr[s][64:128, :].rearrange("p (j t) -> p j t", t=128), krv[:, 1], kv_sem[s])
                kv_tok[idx] = SP.dma(vv[s], jb["v"][h, c], kv_sem[s])

            for idx in range(min(NKV - 1, len(seq))):
                kv_load(idx)

            SB = [0, 1, 2, 3, 4, 5]
            OB = [6, 7]
            for b in range(8):
                bank_busy.discard(b)
            pT_free = [None] * NPB
            q_free = [None, None]
            o_free = [None, None]
            fin_pending = []
            fin_tok = [None, None]
            tile_ctr = [0]
            NQS = NQ // 512
            NPR = KC // 256

            def qproj_steps(vi_):
                ps_, h = VH[vi_]
                s = vi_ % 2
                res = {"toks": []}
                steps = []

                def mk(qs, kind):
                    cols = slice(qs * 512, (qs + 1) * 512)
                    gcols = slice(ps_ * NQ + qs * 512, ps_ * NQ + (qs + 1) * 512)

                    def nope():
                        b, bw = balloc(SB)
                        tk = mm_group(bank(b), [(wuq[:, m, h * QKD:h * QKD + 128], cqn[:, m, gcols])
                                                for m in range(3)], waits=[w_ready, bw])
                        te = ACT.emit(_f_act(qn2[s][:, cols], bank(b), AF.Copy), waits=[tk, q_free[s]])
                        brelease(b, te)
                        res["toks"].append(te)

                    def ropea():
                        b1, bw1 = balloc(SB)
                        tk1 = mm_group(bank(b1), [(wuq_r[:, m, h, :], cqn[:, m, gcols]) for m in range(3)],
                                       waits=[w_ready, bw1])
                        r1 = DVE.emit(_f_tt(qtmp[0], bank(b1), csq[:, 0, gcols], ALU.mult),
                                      waits=[tk1, qproj.last_pool])
                        brelease(b1, r1)
                        res["r1"] = r1

                    def ropeb():
                        b2, bw2 = balloc(SB)
                        tk2 = mm_group(bank(b2), [(wuqp_r[:, m, h, :], cqn[:, m, gcols]) for m in range(3)],
                                       waits=[w_ready, bw2])
                        r2 = DVE.emit(_f_tt(qtmp[1], bank(b2), csq[:, 1, gcols], ALU.mult),
                                      waits=[tk2, qproj.last_pool])
                        brelease(b2, r2)
                        tq = POOL.emit(_f_tt(qr2[s][:, cols], qtmp[0], qtmp[1], ALU.add),
                                       waits=[res["r1"], r2, q_free[s]])
                        qproj.last_pool = tq
                        res["toks"].append(tq)

                    return {"nope": nope, "ropea": ropea, "ropeb": ropeb}[kind]

                for qs in range(NQS):
                    for kind in ("nope", "ropea", "ropeb"):
                        steps.append(mk(qs, kind))
                return steps, res

            def qproj(vi_):
                steps, res = qproj_steps(vi_)
                for f_ in steps:
                    f_()
                return res["toks"]

            qproj.last_pool = None
            qtoks = {0: qproj(0)}

            def finalize(hh, oc0, aD, tD, oc, tO):
                for qs in range(NQS):
                    ocols = slice(oc0 + qs * 512, oc0 + (qs + 1) * 512)
                    b, bw = balloc(SB)
                    tr_ = PE.emit(_f_mm(bank(b), ones_f[:], aD[qs], True, True), waits=[tD[qs], bw])
                    t1 = DVE.emit(_f_recip(rec[qs], bank(b)), waits=[tr_])
                    brelease(b, t1)
                    fin_tok[hp_of[(hh, oc0)]] = DVE.emit(_f_tt(oaT[:, hh, ocols], oc[qs], rec[qs], ALU.mult),
                                               waits=[t1, tO[qs]])

            hp_of = {}
            for vi, (ps, h) in enumerate(VH):
                s = vi % 2
                hp = vi % 2
                oc0 = ps * NQ
                hp_of[(h, oc0)] = hp
                qt = qtoks[vi]
                items = [(c, pr, qs) for c in range(nchunk) for pr in range(NPR) for qs in range(NQS)]
                nit = len(items)
                s_tok = {}
                accD_tok = [None] * NQS
                lastpv = [None] * NQS
                qsteps, qres = [], None

                def issue_S(ii):
                    c, pr, qs = items[ii]
                    li = vi * nchunk + c
                    ks = li % NKV
                    cols = slice(qs * 512, (qs + 1) * 512)
                    bA, bwA = balloc(SB)
                    bB, bwB = balloc(SB)
                    kA, kB = 2 * pr, 2 * pr + 1
                    PE.emit(_f_mm(bank(bA), kn[ks][:, kA * 128:(kA + 1) * 128], qn2[s][:, cols], True, False),
                            waits=[kv_tok[li], qt, bwA], sig=False)
                    PE.emit(_f_mm(bank(bB), kn[ks][:, kB * 128:(kB + 1) * 128], qn2[s][:, cols], True, False),
                            waits=[bwB], sig=False)
                    PE.emit(_f_mm(bank(bA), kr[ks][0:64, pr * 128:(pr + 1) * 128], qr2[s][0:64, cols], False, True),
                            sig=False)
                    tk = PE.emit(_f_mm(bank(bB), kr[ks][64:128, pr * 128:(pr + 1) * 128], qr2[s][64:128, cols],
                                       False, True))
                    s_tok[ii] = (bA, bB, tk)

                issue_S(0)
                for ii in range(nit):
                    c, pr, qs = items[ii]
                    li = vi * nchunk + c
                    ks = li % NKV
                    if ii + 1 < nit:
                        issue_S(ii + 1)
                    if ii == 5 and fin_pending:
                        finalize(*fin_pending.pop())
                    if ii == nit // 2 and vi + 1 < len(VH):
                        qsteps, qres = qproj_steps(vi + 1)
                    if vi + 1 < len(VH) and ii >= nit // 2 and qsteps:
                        qsteps.pop(0)()
                        if not qsteps:
                            qtoks[vi + 1] = qres["toks"]
                    bA, bB, tk = s_tok.pop(ii)
                    first = (c == 0 and pr == 0)
                    last = (c == nchunk - 1) and (pr == NPR - 1)
                    pbs, tes, tps = [], [], []
                    for t_, (b, kb) in enumerate(((bA, 2 * pr), (bB, 2 * pr + 1))):
                        g = tile_ctr[0]
                        tile_ctr[0] += 1
                        pb = g % NPB
                        te = ACT.emit(_f_act(pT[pb], bank(b), AF.Exp, scale=SCALE), waits=[tk, pT_free[pb]])
                        brelease(b, te)
                        st_ = first and t_ == 0
                        en_ = last and t_ == 1
                        tp = PE.emit(_f_mm(bank(OB[qs]), vv[ks][:, kb, :], pT[pb], st_, en_),
                                     waits=[te, o_free[qs]] if st_ else [te])
                        pbs.append(pb)
                        tes.append(te)
                        tps.append(tp)
                    lastpv[qs] = tps[1]
                    tpair = DVE.emit(_f_tt(pair[qs], pT[pbs[0]], pT[pbs[1]], ALU.add),
                                     waits=[tes[0], tes[1], accD_tok[qs]])
                    if accD_tok[qs] is None:
                        ta = DVE.emit(_f_copy(accD[hp][qs], pair[qs]), waits=[tpair, fin_tok[hp]])
                    else:
                        ta = DVE.emit(_f_tt(accD[hp][qs], accD[hp][qs], pair[qs], ALU.add), waits=[tpair])
                    accD_tok[qs] = ta
                    pT_free[pbs[0]] = [tps[0], tpair]
                    pT_free[pbs[1]] = [tps[1], tpair]
                    if pr == NPR - 1 and qs == NQS - 1:
                        kv_free[ks] = tps[1]
                        nxt = li + NKV - 1
                        if nxt < len(seq) and nxt not in kv_tok:
                            kv_load(nxt)
                while qsteps:
                    qsteps.pop(0)()
                    if not qsteps:
                        qtoks[vi + 1] = qres["toks"]
                tO = []
                for qs in range(NQS):
                    c2 = ACT.emit(_f_act(ocp[hp][qs], bank(OB[qs]), AF.Copy), waits=[lastpv[qs], fin_tok[hp]])
                    o_free[qs] = c2
                    tO.append(c2)
                fin_pending.append((h, oc0, accD[hp], accD_tok, ocp[hp], tO))
                if vi == len(VH) - 1:
                    while fin_pending:
                        finalize(*fin_pending.pop(0))
                q_free[s] = PE.tok()
            barrier()

        def phase3(jb, q0, nr):
            xsrc = jb["xqr"] if jb["xqr"] is not None else jb["xk"]
            bp = Bump()
            bp.off = XT_OFF
            xt2 = [bp.get([128, 4, D], F32), bp.get([128, 4, D], F32)]
            bp.off = 0
            xn2 = [bp.get([128, D], BF16), bp.get([128, D], BF16)]
            junk = bp.get([128, D], BF16)
            hnTs = [bp.get([128, 8, MT], BF16), bp.get([128, 8, MT], BF16)]
            NW = 5
            wr = [bp.get([128, 8, 512], BF16) for _ in range(NW)]
            tf = [bp.get([128, 512], F32) for _ in range(6)]
            ma4 = bp.get([128, 4, 512], F32)
            u_off = bp.off
            vg = bp.get([128, D], F32)
            vsn = bp.get([128, 4, D], BF16)
            uT = bp.get([128, 8, MT], BF16)
            end1 = bp.off
            bp.off = u_off
            h1T = bp.get([128, 32, MT], BF16)
            bp.off = max(bp.off, end1)
            assert bp.off <= XT_OFF, bp.off

            n_mt = nr // MT
            x_sem = x_sem_p
            y_sem = y_sem_p
            w_sem = [new_dsem() for _ in range(NW)]
            w_free = [None] * NW
            xt_free = xt_free_p
            stt_ = {"junk": junk, "xn_free": [None, None], "hnT_free": None}
            hn_free = [None, None]
            ytoks = []

            def piece_list():
                L = []
                for i in range(2):
                    L.append(("v", win_bf[:, V0 + i * 512:V0 + (i + 1) * 512]))
                for i in range(2):
                    L.append(("u", win_bf[:, U0 + i * 512:U0 + (i + 1) * 512]))
                for i in range(2):
                    L.append(("ga", win_bf[:, GA0 + i * 512:GA0 + (i + 1) * 512]))
                    L.append(("gb", win_bf[:, GB0 + i * 512:GB0 + (i + 1) * 512]))
                for i in range(2):
                    L.append(("wo", wo_bf[:, i * 512:(i + 1) * 512]))
                for i in range(8):
                    L.append(("f1", wff1_bf[:, i * 512:(i + 1) * 512]))
                for half in range(2):
                    for g4 in range(4):
                        L.append(("f2", wff2_bf[g4 * 1024:(g4 + 1) * 1024, half * 512:(half + 1) * 512]))
                return L

            pieces = []
            for mt in range(n_mt):
                pieces += piece_list()
            w_tok = {}
            w_next = [0]

            def w_issue():
                i = w_next[0]
                if i >= len(pieces):
                    return
                s = i % NW
                w_tok[i] = SP.dma(wr[s], pieces[i][1].rearrange("(kc p) c -> p kc c", p=128), w_sem[s],
                                  waits=[w_free[s]])
                w_next[0] += 1

            def w_done(i, tok):
                w_free[i % NW] = tok
                w_issue()

            def xload(mt):
                s = mt % 2
                r0 = q0 + mt * MT
                return SP.dma(xt2[s], xsrc[r0:r0 + MT, :].rearrange("(j p) d -> p j d", p=128), x_sem[s],
                              waits=[xt_free[s]])

            xtok = {}
            if (jb["name"], q0) in x_pref:
                xtok = x_pref.pop((jb["name"], q0))
            if 0 not in xtok:
                xtok[0] = xload(0)
            for _ in range(NW - 1):
                w_issue()
            if n_mt > 1 and 1 not in xtok:
                xtok[1] = xload(1)
            w_issue()
            th_next = None
            hn_sem = [new_dsem(), new_dsem()]

            def hn_load(mt_):
                return SP.dma(hnTs[mt_ % 2], jb["hn"][(q0 + mt_ * MT) // MT], hn_sem[mt_ % 2],
                              waits=[hn_free[mt_ % 2]])

            def rms_rows(src3, g_bc, dst_fn, wait):
                tks = []
                for j in range(4):
                    tks.append(ACT.emit(_f_act_acc(junk, src3(j), AF.Square, ss4[:, j:j + 1]),
                                        waits=[wait, stt_.get("ss_free")]))
                t1 = DVE.emit(_f_ts(t4[:], ss4[:], 1.0 / D, EPS, ALU.mult, ALU.add), waits=[tks[-1]])
                stt_["ss_free"] = t1
                t2 = POOL.emit(_f_tt(rstd4[:], t4[:], mhalf[:, 0:4], ALU.pow), waits=[t1])
                tk = None
                for j in range(4):
                    tk = DVE.emit(_f_stt(dst_fn(j), src3(j), rstd4[:, j:j + 1], g_bc[:], ALU.mult, ALU.mult),
                                  waits=[t2])
                return tk

            tf_free = [None] * 6
            ma4_free = [None] * 4

            def gelu_from_psum(b, tk, out_ap, k):
                a1 = ACT.emit(_f_act(tf[k], bank(b), AF.Square), waits=[tk, tf_free[k]])
                d1 = DVE.emit(_f_ts(tf[k], tf[k], GC1, 1.0, ALU.mult, ALU.add), waits=[a1])
                d2 = DVE.emit(_f_tt(tf[k], tf[k], bank(b), ALU.mult), waits=[d1])
                a2 = ACT.emit(_f_act(tf[k + 1], tf[k], AF.Sigmoid, scale=GC2), waits=[d2, tf_free[k + 1]])
                d3 = DVE.emit(_f_tt(out_ap, tf[k + 1], bank(b), ALU.mult), waits=[a2])
                tf_free[k] = a2
                tf_free[k + 1] = d3
                return d3

            pi = [0]
            for mt in range(n_mt):
                s = mt % 2
                xt = xt2[s]
                qc = slice(mt * MT, (mt + 1) * MT)
                hnT = hnTs[mt % 2]
                if th_next is None:
                    th = hn_load(mt)
                else:
                    th = th_next
                    th_next = None
                pv0, pv1 = pi[0], pi[0] + 1
                pi[0] += 2
                last = None
                for j in range(4):
                    for half in range(2):
                        p_ = pv0 + half
                        b, bw = balloc()
                        tk = mm_group(bank(b), [(hnT[:, kc, j * 128:(j + 1) * 128], wr[p_ % NW][:, kc, :])
                                                for kc in range(8)], waits=[th, w_tok[p_], bw])
                        last = tk
                        d3 = ACT.emit(_f_act(vg[:, half * 512:(half + 1) * 512], bank(b), AF.Gelu_apprx_tanh),
                                      waits=[tk, stt_.get("vg_free")])
                        brelease(b, d3)
                    tsq = ACT.emit(_f_act_acc(junk, vg[:], AF.Square, ss4[:, 0:1]), waits=[d3, stt_.get("ss_free")])
                    t1 = DVE.emit(_f_ts(t4[:, 0:1], ss4[:, 0:1], 1.0 / D, EPS, ALU.mult, ALU.add), waits=[tsq])
                    stt_["ss_free"] = t1
                    t2 = POOL.emit(_f_tt(rstd4[:, 0:1], t4[:, 0:1], mhalf[:, 0:1], ALU.pow), waits=[t1])
                    tvs = DVE.emit(_f_stt(vsn[:, j, :], vg[:], rstd4[:, 0:1], g_sgu[:], ALU.mult, ALU.mult),
                                   waits=[t2])
                    stt_["vg_free"] = tvs
                w_done(pv0, last)
                w_done(pv1, last)
                for i in range(2):
                    p_ = pi[0]
                    pi[0] += 1
                    for mm in range(4):
                        m = i * 4 + mm
                        b, bw = balloc()
                        tk = mm_group(bank(b), [(wr[p_ % NW][:, kc, mm * 128:(mm + 1) * 128], hnT[:, kc, :])
                                                for kc in range(8)], waits=[w_tok[p_], bw])
                        d3 = ACT.emit(_f_act(uT[:, m, :], bank(b), AF.Gelu_apprx_tanh), waits=[tk])
                        brelease(b, d3)
                    w_done(p_, tk)
                tu = d3
                for g in range(8):
                    b, bw = balloc()
                    tk = None
                    for j in range(4):
                        tk = PE.emit(_f_mm(bank(b)[:, j * 128:(j + 1) * 128], vsn[:, j, g * 128:(g + 1) * 128],
                                           wsT[:, g, :], True, True), waits=[tvs, bw] if j == 0 else ())
                    k = 4 + (g % 2)
                    d1 = DVE.emit(_f_tt(tf[k].rearrange("p (j t) -> p j t", t=128),
                                        bank(b).rearrange("p (j t) -> p j t", t=128),
                                        bs_bc[:, g:g + 1, :].broadcast_to([128, 4, 128]), ALU.add),
                                  waits=[tk, tf_free[k]])
                    brelease(b, d1)
                    tob = DVE.emit(_f_tt(uT[:, g, :], uT[:, g, :], tf[k], ALU.mult), waits=[d1, tu])
                    tf_free[k] = tob
                for i in range(2):
                    pa, pb_ = pi[0], pi[0] + 1
                    pi[0] += 2
                    for mm in range(4):
                        m = i * 4 + mm
                        b, bw = balloc()
                        tk = mm_group(bank(b), [(wr[pa % NW][:, kc, mm * 128:(mm + 1) * 128], hnT[:, kc, :])
                                                for kc in range(8)], waits=[w_tok[pa], bw])
                        k = mm % 2
                        a1 = ACT.emit(_f_act(tf[k], bank(b), AF.Sigmoid), waits=[tk, tf_free[k]])
                        brelease(b, a1)
                        d1 = POOL.emit(_f_tt(ma4[:, mm, :], tf[k], oaT[:, m, qc], ALU.mult),
                                       waits=[a1, ma4_free[mm]])
                        tf_free[k] = d1
                        ma_tok = d1
                    w_done(pa, tk)
                    for mm in range(4):
                        m = i * 4 + mm
                        b, bw = balloc()
                        tk = mm_group(bank(b), [(wr[pb_ % NW][:, kc, mm * 128:(mm + 1) * 128], hnT[:, kc, :])
                                                for kc in range(8)], waits=[w_tok[pb_], bw])
                        k = 2 + mm % 2
                        a1 = ACT.emit(_f_act(tf[k], bank(b), AF.Sigmoid), waits=[tk, tf_free[k]])
                        brelease(b, a1)
                        d1 = DVE.emit(_f_tt(tf[k], tf[k], uT[:, m, :], ALU.mult), waits=[a1, tob])
                        tmg = DVE.emit(_f_tt(uT[:, m, :], tf[k], ma4[:, mm, :], ALU.add), waits=[d1, ma_tok])
                        tf_free[k] = tmg
                        ma4_free[mm] = tmg
                    w_done(pb_, tk)
                hn_free[mt % 2] = PE.tok()
                for half in range(2):
                    p_ = pi[0]
                    pi[0] += 1
                    for j in range(4):
                        b, bw = balloc()
                        tk = mm_group(bank(b), [(uT[:, kc, j * 128:(j + 1) * 128], wr[p_ % NW][:, kc, :])
                                                for kc in range(8)], waits=[tmg, w_tok[p_], bw])
                        tx1 = DVE.emit(_f_tt(xt[:, j, half * 512:(half + 1) * 512],
                                             xt[:, j, half * 512:(half + 1) * 512], bank(b), ALU.add),
                                       waits=[tk, xtok[mt]])
                        brelease(b, tx1)
                    w_done(p_, tk)
                stt_["hnT_free"] = hn_free[mt % 2]
                th2 = norm_transpose(xt, tx1, g_ffn, xn2, hnT, stt_)
                for i in range(8):
                    p_ = pi[0]
                    pi[0] += 1
                    for mm in range(4):
                        m = i * 4 + mm
                        b, bw = balloc()
                        tk = mm_group(bank(b), [(wr[p_ % NW][:, kc, mm * 128:(mm + 1) * 128], hnT[:, kc, :])
                                                for kc in range(8)], waits=[th2, w_tok[p_], bw])
                        k = m % 4
                        a1 = ACT.emit(_f_act(tf[k], bank(b), AF.Relu), waits=[tk, tf_free[k]])
                        brelease(b, a1)
                        th1 = POOL.emit(_f_tt(h1T[:, m, :], tf[k], tf[k], ALU.mult), waits=[a1])
                        tf_free[k] = th1
                    w_done(p_, tk)
                hn_free[mt % 2] = PE.tok()
                if mt + 2 < n_mt and (mt + 2) not in xtok:
                    pass
                if mt + 1 < n_mt:
                    th_next = hn_load(mt + 1)
                for half in range(2):
                    bks = []
                    for j in range(4):
                        b, bw = balloc()
                        bks.append((b, bw))
                    tk = None
                    for g4 in range(4):
                        p_ = pi[0]
                        pi[0] += 1
                        for j in range(4):
                            b, bw = bks[j]
                            for kc in range(8):
                                tk = PE.emit(_f_mm(bank(b), h1T[:, g4 * 8 + kc, j * 128:(j + 1) * 128],
                                                   wr[p_ % NW][:, kc, :], g4 == 0 and kc == 0, g4 == 3 and kc == 7),
                                             waits=[th1, w_tok[p_], bw] if kc == 0 else (),
                                             sig=(kc == 7))
                            if g4 == 3:
                                tx2 = DVE.emit(_f_tt(xt[:, j, half * 512:(half + 1) * 512],
                                                     xt[:, j, half * 512:(half + 1) * 512], bank(b), ALU.add),
                                               waits=[tk])
                                brelease(b, tx2)
                        w_done(p_, tk)
                tfin = rms_rows(lambda j: xt[:, j, :], g_fin, lambda j: xt[:, j, :], tx2)
                r0 = q0 + mt * MT
                ty = POOL.dma(jb["y"][r0:r0 + MT, :].rearrange("(j p) d -> p j d", p=128), xt, y_sem[s], waits=[tfin])
                xt_free[s] = ty
                ytoks.append(ty)
                if mt + 2 < n_mt:
                    xtok[mt + 2] = xload(mt + 2)
            barrier()
            return ytoks

        def prefetch_x(jb, q0, nr):
            xsrc = jb["xqr"] if jb["xqr"] is not None else jb["xk"]
            toks = {}
            for mt in range(min(2, nr // MT)):
                s_ = mt % 2
                r0 = q0 + mt * MT
                toks[mt] = SP.dma(arena_xt(s_), xsrc[r0:r0 + MT, :].rearrange("(j p) d -> p j d", p=128),
                                  x_sem_p[s_], waits=[xt_free_p[s_]])
            x_pref[(jb["name"], q0)] = toks

        def arena_xt(s_):
            bpx = Bump()
            bpx.off = XT_OFF + s_ * 4 * D * 4
            return bpx.get([128, 4, D], F32)

        all_y = []
        pending_y = []
        for jb in jobs:
            S = jb["S"]
            if pending_y:
                barrier(pending_y)
                pending_y = []
            if jb["xqr"] is None:
                phase1(jb["xk"], S // MT, True, True, jb["cosk"], jb["sink"], jb, 0)
            else:
                phase1(jb["xk"], S // MT, True, False, jb["cosk"], jb["sink"], jb, 0)
                phase1(jb["xqr"], jb["nq"] // MT, False, True, None, None, jb, 0)
            nr = min(NR, jb["nq"])
            for q0 in range(0, jb["nq"], nr):
                prefetch_x(jb, q0, nr)
                phase2(jb, q0, nr // NQ)
                ys_ = phase3(jb, q0, nr)
                all_y += ys_
                pending_y = ys_[-2:]
        barrier(all_y)

        with nc.Block() as block:
            @block.sync
            def _(e):
                SP.replay(e)

            @block.tensor
            def _(e):
                PE.replay(e)

            @block.vector
            def _(e):
                DVE.replay(e)

            @block.scalar
            def _(e):
                ACT.replay(e)

            @block.gpsimd
            def _(e):
                POOL.replay(e)
    return nc


def _rope_tables(n):
    pos = np.arange(n, dtype=np.float32)
    inv = (np.float32(ROPE_BASE) ** (-np.arange(0, RD, 2, dtype=np.float32) / np.float32(RD))).astype(np.float32)
    ang = pos[:, None] * inv[None, :]
    ang = np.concatenate([ang, ang], axis=-1)
    cos = np.cos(ang).astype(np.float32)
    sin = np.sin(ang).astype(np.float32)
    sgn = np.concatenate([-np.ones(RD // 2, np.float32), np.ones(RD // 2, np.float32)])
    return np.ascontiguousarray(cos.T), np.ascontiguousarray((sin * sgn[None, :]).T)


def run(cfg, n_cores, x_prompt, x_sample, w):
    nc = build(cfg)
    smax = max(cfg.ss, cfg.sp)
    cosT, sinT = _rope_tables(smax)
    in_maps = []
    for c in range(n_cores):
        q0 = c * cfg.nqp
        m = {
            "xs": np.ascontiguousarray(x_sample[c * cfg.nseq:(c + 1) * cfg.nseq].reshape(cfg.nseq * cfg.ss, D)),
            "xp": x_prompt,
            "xq": np.ascontiguousarray(x_prompt[q0:q0 + cfg.nqp]),
            "cos_all": cosT, "sin_all": sinT,
            "cos_q": np.ascontiguousarray(cosT[:, q0:q0 + cfg.nqp]),
            "sin_q": np.ascontiguousarray(sinT[:, q0:q0 + cfg.nqp]),
        }
        m.update(w)
        in_maps.append(m)
    res = run_bass_kernel_spmd(nc, in_maps, core_ids=list(range(n_cores)))
    ysam = np.stack([r["ys"].reshape(cfg.nseq, cfg.ss, D) for r in res.results], 0).reshape(-1, cfg.ss, D)
    yp = np.concatenate([r["yq"] for r in res.results], 0)
    return yp, ysam


def _weights(norm_mix_g, w_in, q_norm_g, w_uq, kv_norm_g, w_ukv, sgu_norm_g, w_s, b_s, w_o, norm_ffn_g,
             w_ff1, w_ff2, final_norm_g):
    f = lambda a: np.ascontiguousarray(np.asarray(a, dtype=np.float32))
    return {
        "norm_mix_g": f(norm_mix_g[0]), "w_in": f(w_in[0]), "q_norm_g": f(q_norm_g[0]), "w_uq": f(w_uq[0]),
        "kv_norm_g": f(kv_norm_g[0]), "w_ukv": f(w_ukv[0]), "sgu_norm_g": f(sgu_norm_g[0]), "w_s": f(w_s[0]),
        "b_s": f(b_s[0]).reshape(-1), "w_o": f(w_o[0]), "norm_ffn_g": f(norm_ffn_g[0]), "w_ff1": f(w_ff1[0]),
        "w_ff2": f(w_ff2[0]), "final_norm_g": f(final_norm_g),
    }


def kernel(x_prompt, x_sample, norm_mix_g, w_in, q_norm_g, w_uq, kv_norm_g, w_ukv, sgu_norm_g, w_s, b_s, w_o,
           norm_ffn_g, w_ff1, w_ff2, final_norm_g):
    w = _weights(norm_mix_g, w_in, q_norm_g, w_uq, kv_norm_g, w_ukv, sgu_norm_g, w_s, b_s, w_o, norm_ffn_g,
                 w_ff1, w_ff2, final_norm_g)
    xp = np.ascontiguousarray(np.asarray(x_prompt, dtype=np.float32)[0])
    xsam = np.ascontiguousarray(np.asarray(x_sample, dtype=np.float32))
    yp, ysam = run(FULL, 8, xp, xsam, w)
    return (yp.reshape(1, FULL.sp, D).astype(np.float32), ysam.reshape(16, FULL.ss, D).astype(np.float32))
```

```python
import numpy as np
from contextlib import ExitStack
import concourse.bass as bass
import concourse.mybir as mybir
from concourse.bass_utils import run_bass_kernel_spmd

F32 = mybir.dt.float32
BF16 = mybir.dt.bfloat16
ALU = mybir.AluOpType
AF = mybir.ActivationFunctionType

D = 1024
NH = 8
QL = 384
KVL = 256
RD = 64
DFF = 4096
INC = 4800
QKD = 192
SCALE = float(QKD ** -0.5)
EPS = 1e-6
ROPE_BASE = 10000.0
GC1 = 0.044715
GC2 = 1.5957691216057308
MT = 512
NQ = 1024
NR = 2048
KC = 1024
U0, V0, GA0, GB0 = 704, 1728, 2752, 3776


class DSem:
    def __init__(self, sem):
        self.sem = sem
        self.n = 0


class Eng:
    def __init__(self, name, sem):
        self.name = name
        self.sem = sem
        self.n = 0
        self.q = []
        self.waited = {}

    def _filter(self, waits):
        out = []
        stack = list(waits) if isinstance(waits, (list, tuple)) and not _is_tok(waits) else [waits]
        while stack:
            w = stack.pop()
            if w is None:
                continue
            if _is_tok(w):
                s, v = w
                k = id(s)
                if self.waited.get(k, 0) >= v:
                    continue
                self.waited[k] = v
                out.append((s, v))
            else:
                stack.extend(w)
        return out

    def emit(self, fn, waits=(), sig=True):
        ws = self._filter(waits)
        if sig:
            self.n += 1
        self.q.append((fn, ws, self.sem if sig else None, 1))
        return (self.sem, self.n)

    def dma(self, out, in_, dsem, waits=()):
        ws = self._filter(waits)
        dsem.n += 16
        self.q.append((_f_dma(out, in_), ws, dsem.sem, 16))
        return (dsem.sem, dsem.n)

    def wait_only(self, waits):
        ws = self._filter(waits)
        if ws:
            self.q.append((None, ws, None, 0))

    def tok(self):
        return (self.sem, self.n) if self.n > 0 else None

    def replay(self, e):
        for fn, ws, sem, inc in self.q:
            for s, v in ws:
                e.wait_ge(s, v)
            if fn is None:
                continue
            ins = fn(e)
            if sem is not None:
                ins.then_inc(sem, inc)


def _is_tok(w):
    return isinstance(w, tuple) and len(w) == 2 and isinstance(w[1], int)


def _f_dma(out, in_):
    return lambda e: e.dma_start(out=out, in_=in_)


def _f_mm(out, lhsT, rhs, start, stop):
    return lambda e: e.matmul(out, lhsT=lhsT, rhs=rhs, start=start, stop=stop)


def _f_tr(out, in_, ident):
    return lambda e: e.transpose(out=out, in_=in_, identity=ident)


def _f_act(out, in_, func, scale=1.0, bias=0.0):
    return lambda e: e.activation(out=out, in_=in_, func=func, scale=scale, bias=bias)


def _f_act_acc(out, in_, func, accum_out):
    return lambda e: e.activation(out=out, in_=in_, func=func, accum_out=accum_out)


def _f_copy(out, in_):
    return lambda e: e.tensor_copy(out=out, in_=in_)


def _f_tt(out, in0, in1, op):
    return lambda e: e.tensor_tensor(out=out, in0=in0, in1=in1, op=op)


def _f_ts(out, in0, s1, s2, op0, op1):
    return lambda e: e.tensor_scalar(out=out, in0=in0, scalar1=s1, scalar2=s2, op0=op0, op1=op1)


def _f_stt(out, in0, scalar, in1, op0, op1, accum_out=None):
    if accum_out is None:
        return lambda e: e.scalar_tensor_tensor(out=out, in0=in0, scalar=scalar, in1=in1, op0=op0, op1=op1)
    return lambda e: e.scalar_tensor_tensor(out=out, in0=in0, scalar=scalar, in1=in1, op0=op0, op1=op1,
                                            accum_out=accum_out)


def _f_recip(out, in_):
    return lambda e: e.reciprocal(out=out, in_=in_)


def _f_memset(ap, v):
    return lambda e: e.memset(ap, v)


class Cfg:
    def __init__(self, nseq, ss, sp, nqp):
        self.nseq = nseq
        self.ss = ss
        self.sp = sp
        self.nqp = nqp


FULL = Cfg(2, 4096, 16384, 2048)


def build(cfg):
    nc = bass.Bass("TRN2", target_bir_lowering=False)
    SMAX = max(cfg.ss, cfg.sp)

    def din(name, shape, dt=F32):
        return nc.dram_tensor(name, list(shape), dt, kind="ExternalInput").ap()

    def dscr(name, shape, dt=BF16):
        return nc.dram_tensor(name, list(shape), dt).ap()

    xs = din("xs", [cfg.nseq * cfg.ss, D])
    xp = din("xp", [cfg.sp, D])
    xq = din("xq", [cfg.nqp, D])
    cos_all = din("cos_all", [RD, SMAX])
    sin_all = din("sin_all", [RD, SMAX])
    cos_q = din("cos_q", [RD, cfg.nqp])
    sin_q = din("sin_q", [RD, cfg.nqp])
    norm_mix_g = din("norm_mix_g", [D])
    w_in = din("w_in", [D, INC])
    q_norm_g = din("q_norm_g", [QL])
    w_uq = din("w_uq", [QL, NH * QKD])
    kv_norm_g = din("kv_norm_g", [KVL])
    w_ukv = din("w_ukv", [KVL, NH * 256])
    sgu_norm_g = din("sgu_norm_g", [D])
    w_s = din("w_s", [8, 128, 128])
    b_s = din("b_s", [8 * 128])
    w_o = din("w_o", [D, D])
    norm_ffn_g = din("norm_ffn_g", [D])
    w_ff1 = din("w_ff1", [D, DFF])
    w_ff2 = din("w_ff2", [DFF, D])
    final_norm_g = din("final_norm_g", [D])
    ys = nc.dram_tensor("ys", [cfg.nseq * cfg.ss, D], F32, kind="ExternalOutput").ap()
    yq = nc.dram_tensor("yq", [cfg.nqp, D], F32, kind="ExternalOutput").ap()

    win_bf = dscr("win_bf", [D, INC])
    wlat_bf = dscr("wlat_bf", [D, 768])
    wuq_bf = dscr("wuq_bf", [QL, NH * QKD])
    wuqp_bf = dscr("wuqp_bf", [QL, NH * RD])
    wuqr_d = dscr("wuqr_d", [128, 3, NH, 128])
    wuqpr_d = dscr("wuqpr_d", [128, 3, NH, 128])
    wuk_bf = dscr("wuk_bf", [KVL, NH * 128])
    wuv_bf = dscr("wuv_bf", [KVL, NH * 128])
    wo_bf = dscr("wo_bf", [D, D])
    wff1_bf = dscr("wff1_bf", [D, DFF])
    wff2_bf = dscr("wff2_bf", [DFF, D])

    jobs = []
    for s in range(cfg.nseq):
        jobs.append(dict(name=f"s{s}", xk=xs[s * cfg.ss:(s + 1) * cfg.ss, :], S=cfg.ss, xqr=None, nq=cfg.ss,
                         y=ys[s * cfg.ss:(s + 1) * cfg.ss, :], cosk=cos_all, sink=sin_all,
                         cosq=cos_all, sinq=sin_all))
    jobs.append(dict(name="p", xk=xp, S=cfg.sp, xqr=xq, nq=cfg.nqp, y=yq, cosk=cos_all, sink=sin_all,
                     cosq=cos_q, sinq=sin_q))
    for jb in jobs:
        S = jb["S"]
        jb["kT"] = dscr("kT_" + jb["name"], [NH, 128, S])
        jb["krT"] = dscr("krT_" + jb["name"], [RD, S])
        jb["v"] = dscr("v_" + jb["name"], [NH, S // KC, 128, KC // 128, 128])
        jb["cq"] = dscr("cq_" + jb["name"], [3, 128, jb["nq"]])
        jb["hn"] = dscr("hn_" + jb["name"], [jb["nq"] // MT, 128, 8, MT])

    with ExitStack() as st:
        def sb(name, shape, dt):
            return st.enter_context(nc.sbuf_tensor(name, list(shape), dt))

        def mksem(name):
            return st.enter_context(nc.semaphore(name))

        PE = Eng("pe", mksem("s_pe"))
        ACT = Eng("act", mksem("s_act"))
        DVE = Eng("dve", mksem("s_dve"))
        POOL = Eng("pool", mksem("s_pool"))
        SP = Eng("sp", mksem("s_sp"))
        ENGS = [PE, ACT, DVE, POOL, SP]
        dsem_pool = {"hw": [], "sw": []}
        dsem_idx = {"hw": 0, "sw": 0}

        def new_dsem(kind="hw"):
            pool = dsem_pool[kind]
            if dsem_idx[kind] == len(pool):
                pool.append(DSem(mksem(f"d{kind}{len(pool)}")))
            d = pool[dsem_idx[kind]]
            dsem_idx[kind] += 1
            return d

        ps = st.enter_context(nc.psum_tensor("ps", [128, 8, 512], F32))

        def bank(b):
            return ps[:, b, :]

        def bank_bf(b):
            return ps[:, b, :].bitcast(BF16)

        ident = sb("ident", [128, 128], BF16)
        identf = sb("identf", [128, 128], F32)
        ones_bf = sb("ones_bf", [128, 128], BF16)
        ones_f = sb("ones_f", [128, 128], F32)
        mhalf = sb("mhalf", [128, 4], F32)
        eps_col = sb("eps_col", [128, 1], F32)
        g_mix = sb("g_mix", [128, D], F32)
        g_ffn = sb("g_ffn", [128, D], F32)
        g_fin = sb("g_fin", [128, D], F32)
        g_sgu = sb("g_sgu", [128, D], F32)
        bs_bc = sb("bs_bc", [128, 8, 128], F32)
        wsT = sb("wsT", [128, 8, 128], BF16)
        gq_col = sb("gq_col", [128, 3], F32)
        gkv_col = sb("gkv_col", [128, 2], F32)
        ss4 = sb("ss4", [128, 4], F32)
        t4 = sb("t4", [128, 4], F32)
        rstd4 = sb("rstd4", [128, 4], F32)
        oaT = sb("oaT", [128, NH, NR], BF16)
        ARENA = 147 * 1024 // 2
        arena = sb("arena", [128, ARENA], BF16)

        XT_OFF = ARENA * 2 - 2 * 4 * D * 4
        x_sem_p = [DSem(mksem("xs0")), DSem(mksem("xs1"))]
        y_sem_p = [DSem(mksem("ys0")), DSem(mksem("ys1"))]
        xt_free_p = [None, None]
        x_pref = {}

        class Bump:
            def __init__(self):
                self.off = 0

            def get(self, shape, dt, parts=128):
                n = 1
                for s_ in shape[1:]:
                    n *= s_
                nb = n * (4 if dt == F32 else 2)
                nb = (nb + 63) // 64 * 64
                a = self.off // 2
                self.off += nb
                assert self.off <= ARENA * 2, ("arena overflow", self.off)
                v = arena[0:shape[0], a:a + nb // 2]
                if dt == F32:
                    v = v.bitcast(F32)
                v = v[:, 0:n]
                if len(shape) == 3:
                    v = v.rearrange("p (a b) -> p a b", b=shape[2])
                elif len(shape) == 4:
                    v = v.rearrange("p (a b c) -> p a b c", b=shape[2], c=shape[3])
                return v

        bank_free = {b: None for b in range(8)}
        bank_order = list(range(8))
        bank_busy = set()

        def balloc(allowed=None):
            for b in bank_order:
                if b in bank_busy:
                    continue
                if allowed is not None and b not in allowed:
                    continue
                bank_order.remove(b)
                bank_order.append(b)
                bank_busy.add(b)
                return b, bank_free[b]
            raise RuntimeError("no free psum bank")

        def brelease(b, tok):
            bank_free[b] = tok
            bank_busy.discard(b)

        def barrier(extra=()):
            toks = [e.tok() for e in ENGS] + list(extra)
            for e in ENGS:
                e.wait_only(toks)
            dsem_idx["hw"] = 0
            dsem_idx["sw"] = 0

        def mm_group(out, pairs, waits=(), first_start=True, last_stop=True):
            n = len(pairs)
            tok = None
            for i, (l, r) in enumerate(pairs):
                tok = PE.emit(_f_mm(out, l, r, (i == 0) and first_start, (i == n - 1) and last_stop),
                              waits=waits if i == 0 else (), sig=(i == n - 1))
            return tok

        pend_store = []
        c_tok = []
        bp = Bump()
        NCS = 4
        stf = [bp.get([128, INC], F32) for _ in range(NCS)]
        stb = [bp.get([128, INC], BF16) for _ in range(NCS)]
        wsf = bp.get([128, 8, 128], F32)
        ld_sem = [new_dsem() for _ in range(NCS)]
        stq_sem = [new_dsem("sw") for _ in range(NCS)]
        csem = new_dsem()

        c_tok.append(SP.dma(g_mix[:], norm_mix_g.partition_broadcast(128), csem))
        c_tok.append(SP.dma(g_ffn[:], norm_ffn_g.partition_broadcast(128), csem))
        c_tok.append(SP.dma(g_fin[:], final_norm_g.partition_broadcast(128), csem))
        c_tok.append(SP.dma(g_sgu[:], sgu_norm_g.partition_broadcast(128), csem))
        c_tok.append(SP.dma(bs_bc[:].rearrange("p a b -> p (a b)"), b_s.partition_broadcast(128), csem))
        for k in range(3):
            c_tok.append(SP.dma(gq_col[:, k:k + 1], q_norm_g[k * 128:(k + 1) * 128].rearrange("(p o) -> p o", o=1),
                                csem))
        for k in range(2):
            c_tok.append(SP.dma(gkv_col[:, k:k + 1], kv_norm_g[k * 128:(k + 1) * 128].rearrange("(p o) -> p o", o=1),
                                csem))
        c_tok.append(SP.dma(wsf, w_s.rearrange("g p q -> p g q"), csem))
        const_ready = c_tok[-1]

        i0 = POOL.emit(_f_memset(identf[:], 0.0))
        i1 = POOL.emit(lambda e: e.affine_select(out=identf[:], in_=identf[:], compare_op=ALU.not_equal, fill=1.0,
                                                 base=0, pattern=[[-1, 128]], channel_multiplier=1), waits=[i0])
        i2 = POOL.emit(_f_copy(ident[:], identf[:]), waits=[i1])
        POOL.emit(_f_memset(ones_bf[:], 1.0))
        POOL.emit(_f_memset(ones_f[:], 1.0))
        POOL.emit(_f_memset(mhalf[:], -0.5))
        POOL.emit(_f_memset(eps_col[:], EPS))
        pool_consts = POOL.emit(_f_memset(ss4[:], 0.0))
        for half in range(2):
            b, bw = balloc()
            tk = None
            for gg in range(4):
                g = half * 4 + gg
                tk = PE.emit(_f_tr(bank(b)[:, gg * 128:(gg + 1) * 128], wsf[:, g, :], identf[:]),
                             waits=[const_ready, i1, bw])
            tk2 = ACT.emit(_f_act(wsT[:, half * 4:(half + 1) * 4, :].rearrange("p a b -> p (a b)"), bank(b),
                                  AF.Copy), waits=[tk])
            brelease(b, tk2)
        ws_ready = ACT.tok()

        conv_i = [0]
        slot_free = [None] * NCS
        slot_conv = [None] * NCS
        conv_engs = [DVE, ACT]

        def convert(src, ncols, stores, c3=None):
            i = conv_i[0]
            conv_i[0] += 1
            s = i % NCS
            dstv = stf[s][:, 0:ncols]
            if c3 is not None:
                dstv = dstv.rearrange("p (a c) -> p a c", c=c3)
            tl = SP.dma(dstv, src, ld_sem[s], waits=[slot_conv[s]])
            eng = conv_engs[i % 2]
            if eng is ACT:
                tcv = ACT.emit(_f_act(stb[s][:, 0:ncols], stf[s][:, 0:ncols], AF.Copy), waits=[tl, slot_free[s]])
            else:
                tcv = eng.emit(_f_copy(stb[s][:, 0:ncols], stf[s][:, 0:ncols]), waits=[tl, slot_free[s]])
            slot_conv[s] = tcv
            tks = []
            for dst, vf in stores:
                tks.append(POOL.dma(dst, vf(stb[s]), stq_sem[s], waits=[tcv]))
            slot_free[s] = tks[-1]
            pend_store.append(tks[-1])

        for kc in range(8):
            r = slice(kc * 128, (kc + 1) * 128)
            convert(w_in[r, :], INC, [
                (win_bf[r, :], lambda t: t[:, 0:INC]),
                (wlat_bf[r, 0:704], lambda t: t[:, 0:704]),
                (wlat_bf[r, 704:736], lambda t: t[:, 672:704]),
                (wlat_bf[r, 736:768], lambda t: t[:, 640:672]),
            ])
        for kc in range(3):
            r = slice(kc * 128, (kc + 1) * 128)
            hv = lambda t: t[:, 0:NH * QKD].rearrange("p (h d) -> p h d", d=QKD)
            dup = []
            for half in range(2):
                o_ = half * 64
                dup.append((wuqr_d[:, kc, :, o_:o_ + 64], lambda t: hv(t)[:, :, 128:192]))
                dup.append((wuqpr_d[:, kc, :, o_:o_ + 32], lambda t: hv(t)[:, :, 160:192]))
                dup.append((wuqpr_d[:, kc, :, o_ + 32:o_ + 64], lambda t: hv(t)[:, :, 128:160]))
            convert(w_uq[r, :], NH * QKD, dup + [
                (wuq_bf[r, :], lambda t: t[:, 0:NH * QKD]),
                (wuqp_bf[r, :].rearrange("p (h d) -> p h d", d=RD)[:, :, 0:32],
                 lambda t: t[:, 0:NH * QKD].rearrange("p (h d) -> p h d", d=QKD)[:, :, 160:192]),
                (wuqp_bf[r, :].rearrange("p (h d) -> p h d", d=RD)[:, :, 32:64],
                 lambda t: t[:, 0:NH * QKD].rearrange("p (h d) -> p h d", d=QKD)[:, :, 128:160]),
            ])
        for kc in range(2):
            r = slice(kc * 128, (kc + 1) * 128)
            convert(w_ukv[r, :], NH * 256, [
                (wuk_bf[r, :].rearrange("p (h d) -> p h d", d=128),
                 lambda t: t[:, 0:NH * 256].rearrange("p (h d) -> p h d", d=256)[:, :, 0:128]),
                (wuv_bf[r, :].rearrange("p (h d) -> p h d", d=128),
                 lambda t: t[:, 0:NH * 256].rearrange("p (h d) -> p h d", d=256)[:, :, 128:256]),
            ])
        for kc in range(8):
            r = slice(kc * 128, (kc + 1) * 128)
            convert(w_o[r, :], D, [(wo_bf[r, :], lambda t: t[:, 0:D])])
        for kc in range(8):
            r = slice(kc * 128, (kc + 1) * 128)
            convert(w_ff1[r, :], DFF, [(wff1_bf[r, :], lambda t: t[:, 0:DFF])])
        for k4 in range(8):
            r = slice(k4 * 512, (k4 + 1) * 512)
            convert(w_ff2[r, :].rearrange("(a p) c -> p a c", p=128), 4 * D,
                    [(wff2_bf[r, :].rearrange("(a p) c -> p a c", p=128),
                      lambda t: t[:, 0:4 * D].rearrange("p (a c) -> p a c", c=D))], c3=D)
        wconv_done = list(slot_free)
        barrier(wconv_done + [const_ready])

        def norm_transpose(xt, x_ready, g_bc, xn2, hnT, st_tok):
            junk = st_tok["junk"]
            tks = []
            for j in range(4):
                tks.append(ACT.emit(_f_act_acc(junk, xt[:, j, :], AF.Square, ss4[:, j:j + 1]),
                                    waits=[x_ready, st_tok.get("ss_free")]))
            t1 = DVE.emit(_f_ts(t4[:], ss4[:], 1.0 / D, EPS, ALU.mult, ALU.add), waits=[tks[-1]])
            st_tok["ss_free"] = t1
            t2 = POOL.emit(_f_tt(rstd4[:], t4[:], mhalf[:, 0:4], ALU.pow), waits=[t1])
            done = []
            for j in range(4):
                s = j % 2
                tn = DVE.emit(_f_stt(xn2[s], xt[:, j, :], rstd4[:, j:j + 1], g_bc[:], ALU.mult, ALU.mult),
                              waits=[t2, st_tok["xn_free"][s]])
                b, bw = balloc()
                tk = None
                for kc in range(8):
                    tk = PE.emit(_f_tr(bank_bf(b)[:, kc * 128:(kc + 1) * 128], xn2[s][:, kc * 128:(kc + 1) * 128],
                                       ident[:]), waits=[tn, bw] if kc == 0 else (), sig=(kc == 7))
                st_tok["xn_free"][s] = tk
                te = ACT.emit(_f_act(hnT[:, :, j * 128:(j + 1) * 128],
                                     bank_bf(b).rearrange("p (k t) -> p k t", t=128), AF.Copy),
                              waits=[tk, st_tok.get("hnT_free")])
                brelease(b, te)
                done.append(te)
            return done[-1]

        def phase1(xsrc, n_mt, do_kv, do_q, cosk, sink, jb, qcol0):
            bp = Bump()
            wlat = bp.get([128, 8, 768], BF16)
            wuk = bp.get([128, 2, 1024], BF16)
            wuv = bp.get([128, 2, 1024], BF16)
            xt2 = [bp.get([128, 4, D], F32), bp.get([128, 4, D], F32)]
            xn2 = [bp.get([128, D], BF16), bp.get([128, D], BF16)]
            junk = bp.get([128, D], BF16)
            hnT2 = [bp.get([128, 8, MT], BF16), bp.get([128, 8, MT], BF16)]
            kst2 = [bp.get([128, NH, MT], BF16), bp.get([128, NH, MT], BF16)]
            vst2 = [bp.get([128, NH, 4, 128], BF16), bp.get([128, NH, 4, 128], BF16)]
            krst2 = [bp.get([64, MT], BF16, parts=64), bp.get([64, MT], BF16, parts=64)]
            ckvn2 = [bp.get([128, 2, MT], BF16), bp.get([128, 2, MT], BF16)]
            cqst2 = [bp.get([128, 3, MT], BF16), bp.get([128, 3, MT], BF16)]
            sqb = [bp.get([128, MT], BF16) for _ in range(3)]
            tf = [bp.get([128, MT], F32) for _ in range(4)]
            cs2 = [bp.get([64, 2, MT], F32), bp.get([64, 2, MT], F32)]

            wsem = new_dsem()
            SP.dma(wlat, wlat_bf.rearrange("(kc p) c -> p kc c", p=128), wsem)
            SP.dma(wuk, wuk_bf.rearrange("(kc p) c -> p kc c", p=128), wsem)
            w_ready = SP.dma(wuv, wuv_bf.rearrange("(kc p) c -> p kc c", p=128), wsem)

            x_sem = [new_dsem(), new_dsem()]
            cs_sem = [new_dsem(), new_dsem()]
            st_sem = [new_dsem("sw"), new_dsem("sw")]
            stt_ = {"junk": junk, "xn_free": [None, None]}
            xt_free = [None, None]
            cs_free = [None, None]
            hn_free = [None, None]
            stage_free = [None, None]
            ckvn_free = [None, None]
            stores = []
            A = {}

            def load(i):
                s = i % 2
                r0 = i * MT
                tx = SP.dma(xt2[s], xsrc[r0:r0 + MT, :].rearrange("(j p) d -> p j d", p=128), x_sem[s],
                            waits=[xt_free[s]])
                tc_ = None
                if do_kv:
                    SP.dma(cs2[s][:, 0, :], cosk[:, r0:r0 + MT], cs_sem[s], waits=[cs_free[s]])
                    tc_ = SP.dma(cs2[s][:, 1, :], sink[:, r0:r0 + MT], cs_sem[s])
                A[i] = dict(tx=tx, tcs=tc_)

            def stageA(i):
                s = i % 2
                a = A[i]
                stt_["hnT_free"] = hn_free[s]
                th = norm_transpose(xt2[s], a["tx"], g_mix, xn2, hnT2[s], stt_)
                xt_free[s] = DVE.tok()
                if do_q:
                    a["hn_st"] = POOL.dma(jb["hn"][qcol0 // MT + i], hnT2[s], st_sem[s], waits=[th])
                yield
                hnT = hnT2[s]
                last_pe = None
                if do_kv:
                    cb = []
                    for m in range(2):
                        b, bw = balloc()
                        tk = mm_group(bank(b), [(wlat[:, kc, QL + m * 128:QL + (m + 1) * 128], hnT[:, kc, :])
                                                for kc in range(8)], waits=[th, bw, w_ready])
                        cb.append((b, tk))
                    rb = []
                    for m in range(2):
                        b, bw = balloc()
                        tk = mm_group(bank(b)[0:64, :], [(wlat[:, kc, 640 + m * 64:640 + (m + 1) * 64], hnT[:, kc, :])
                                                          for kc in range(8)], waits=[bw])
                        rb.append((b, tk))
                    tsq = []
                    for m in range(2):
                        tsq.append(ACT.emit(_f_act(sqb[m], bank(cb[m][0]), AF.Square), waits=[cb[m][1]]))
                    yield
                    b, bw = balloc()
                    tss = mm_group(bank(b), [(ones_bf[:], sqb[m]) for m in range(2)], waits=[tsq[-1], bw])
                    tt = ACT.emit(_f_act(tf[0], bank(b), AF.Ln, scale=1.0 / KVL, bias=eps_col[:, 0:1]), waits=[tss])
                    brelease(b, tt)
                    tr_ = ACT.emit(_f_act(tf[1], tf[0], AF.Exp, scale=-0.5), waits=[tt, stt_.get("rstd_free")])
                    tcn = None
                    for m in range(2):
                        tcn = DVE.emit(_f_stt(ckvn2[s][:, m, :], bank(cb[m][0]), gkv_col[:, m:m + 1], tf[1],
                                              ALU.mult, ALU.mult), waits=[tr_, ckvn_free[s]])
                        brelease(cb[m][0], tcn)
                    a["ckvn"] = tcn
                    stt_["rstd_free"] = tcn
                    r1 = DVE.emit(_f_tt(tf[2][0:64, :], bank(rb[0][0])[0:64, :], cs2[s][:, 0, :], ALU.mult),
                                  waits=[rb[0][1], a["tcs"], stt_.get("kr_pool")])
                    brelease(rb[0][0], r1)
                    r2 = DVE.emit(_f_tt(tf[3][0:64, :], bank(rb[1][0])[0:64, :], cs2[s][:, 1, :], ALU.mult),
                                  waits=[rb[1][1]])
                    brelease(rb[1][0], r2)
                    cs_free[s] = r2
                    r3 = POOL.emit(_f_tt(krst2[s], tf[2][0:64, :], tf[3][0:64, :], ALU.add),
                                   waits=[r1, r2, stage_free[s]])
                    a["kr"] = r3
                    stt_["kr_pool"] = r3
                    last_pe = tss
                if do_q:
                    qb = []
                    for m in range(3):
                        b, bw = balloc()
                        tk = mm_group(bank(b), [(wlat[:, kc, m * 128:(m + 1) * 128], hnT[:, kc, :])
                                                for kc in range(8)], waits=[th, bw, w_ready])
                        qb.append((b, tk))
                    tsq = []
                    for m in range(3):
                        tsq.append(ACT.emit(_f_act(sqb[m], bank(qb[m][0]), AF.Square), waits=[qb[m][1], last_pe]))
                    b, bw = balloc()
                    tss = mm_group(bank(b), [(ones_bf[:], sqb[m]) for m in range(3)], waits=[tsq[-1], bw])
                    tt = ACT.emit(_f_act(tf[0], bank(b), AF.Ln, scale=1.0 / QL, bias=eps_col[:, 0:1]), waits=[tss])
                    brelease(b, tt)
                    tr_ = ACT.emit(_f_act(tf[1], tf[0], AF.Exp, scale=-0.5), waits=[tt, stt_.get("rstd_free")])
                    tcq = None
                    for m in range(3):
                        tcq = DVE.emit(_f_stt(cqst2[s][:, m, :], bank(qb[m][0]), gq_col[:, m:m + 1], tf[1],
                                              ALU.mult, ALU.mult), waits=[tr_, stage_free[s]])
                        brelease(qb[m][0], tcq)
                    a["cq"] = tcq
                    stt_["rstd_free"] = tcq
                    last_pe = tss
                hn_free[s] = [PE.tok(), a.get("hn_st")]

            def stageB(i):
                s = i % 2
                a = A[i]
                r0 = i * MT
                stks = []
                if do_kv:
                    ck = ckvn2[s]
                    evs = [ACT, DVE]
                    te = None
                    for h in range(NH):
                        b, bw = balloc()
                        tk = mm_group(bank(b), [(wuk[:, m, h * 128:(h + 1) * 128], ck[:, m, :]) for m in range(2)],
                                      waits=[a["ckvn"], bw])
                        if h % 4 == 0:
                            te = ACT.emit(_f_act(kst2[s][:, h, :], bank(b), AF.Copy), waits=[tk, stage_free[s]])
                        else:
                            te = DVE.emit(_f_copy(kst2[s][:, h, :], bank(b)), waits=[tk, stage_free[s]])
                        brelease(b, te)
                    tkA, tkD = ACT.tok(), DVE.tok()
                    stks.append(POOL.dma(jb["kT"][:, :, r0:r0 + MT].rearrange("h d t -> d h t"), kst2[s], st_sem[s],
                                         waits=[tkA, tkD]))
                    stks.append(POOL.dma(jb["krT"][:, r0:r0 + MT], krst2[s], st_sem[s], waits=[a["kr"]]))
                    yield
                    for j in range(4):
                        for half in range(2):
                            b, bw = balloc()
                            tk = mm_group(bank(b), [(ck[:, m, j * 128:(j + 1) * 128],
                                                     wuv[:, m, half * 512:(half + 1) * 512]) for m in range(2)],
                                          waits=[bw])
                            dst = vst2[s][:, half * 4:(half + 1) * 4, j, :]
                            src = bank(b).rearrange("p (h d) -> p h d", d=128)
                            if (j + half) % 2 == 0:
                                te = ACT.emit(_f_act(dst, src, AF.Copy), waits=[tk])
                            else:
                                te = DVE.emit(_f_copy(dst, src), waits=[tk])
                            brelease(b, te)
                    tkA, tkD = ACT.tok(), DVE.tok()
                    ckvn_free[s] = PE.tok()
                    c = r0 // KC
                    kb0 = (r0 % KC) // 128
                    stks.append(POOL.dma(jb["v"][:, c, :, kb0:kb0 + 4, :].rearrange("h p k d -> p h k d"), vst2[s],
                                         st_sem[s], waits=[tkA, tkD]))
                if do_q:
                    stks.append(POOL.dma(jb["cq"][:, :, qcol0 + r0:qcol0 + r0 + MT].rearrange("m p t -> p m t"),
                                         cqst2[s], st_sem[s], waits=[a["cq"]]))
                stage_free[s] = stks[-1]
                stores.append(stks[-1])
                yield

            def drive(gens):
                gens = [g for g in gens if g is not None]
                while gens:
                    for g in list(gens):
                        try:
                            next(g)
                        except StopIteration:
                            gens.remove(g)

            load(0)
            if n_mt > 1:
                load(1)
            drive([stageA(0)])
            for i in range(n_mt):
                ga = stageA(i + 1) if i + 1 < n_mt else None
                if i + 2 < n_mt:
                    load(i + 2)
                drive([ga, stageB(i)])
            barrier(stores[-2:])
            return stores[-2:]

        def phase2(jb, q0, npass):
            S = jb["S"]
            NQT = npass * NQ
            VH = [(ps_, h_) for ps_ in range(npass) for h_ in range(NH)]
            bp = Bump()
            wuq = bp.get([128, 3, NH * QKD], BF16)
            wuq_r = bp.get([128, 3, NH, 128], BF16)
            wuqp_r = bp.get([128, 3, NH, 128], BF16)
            cqn = bp.get([128, 3, NQT], BF16)
            csq = bp.get([128, 2, NQT], F32)
            qn2 = [bp.get([128, NQ], BF16), bp.get([128, NQ], BF16)]
            qr2 = [bp.get([128, NQ], BF16), bp.get([128, NQ], BF16)]
            NKV = 4
            kn = [bp.get([128, KC], BF16) for _ in range(NKV)]
            kr = [bp.get([128, KC // 2], BF16) for _ in range(NKV)]
            vv = [bp.get([128, KC // 128, 128], BF16) for _ in range(NKV)]
            NPB = 8
            pT = [bp.get([128, 512], BF16) for _ in range(NPB)]
            rec = [bp.get([128, 512], F32) for _ in range(2)]
            ocp = [[bp.get([128, 512], F32) for _ in range(2)] for _ in range(2)]
            accD = [[bp.get([128, 512], F32) for _ in range(2)] for _ in range(2)]
            pair = [bp.get([128, 512], BF16) for _ in range(2)]
            qtmp = [bp.get([128, 512], F32) for _ in range(2)]
            assert bp.off <= XT_OFF, bp.off

            wsem = new_dsem()
            SP.dma(wuq, wuq_bf.rearrange("(kc p) c -> p kc c", p=128), wsem)
            SP.dma(wuq_r.rearrange("p a b c -> p (a b c)"), wuqr_d.rearrange("p a b c -> p (a b c)"), wsem)
            SP.dma(wuqp_r.rearrange("p a b c -> p (a b c)"), wuqpr_d.rearrange("p a b c -> p (a b c)"), wsem)
            for half in range(2):
                SP.dma(csq[half * 64:(half + 1) * 64, 0, :], jb["cosq"][:, q0:q0 + NQT], wsem)
                SP.dma(csq[half * 64:(half + 1) * 64, 1, :], jb["sinq"][:, q0:q0 + NQT], wsem)
            w_ready = SP.dma(cqn, jb["cq"][:, :, q0:q0 + NQT].rearrange("m p t -> p m t"), wsem)

            kv_sem = [new_dsem() for _ in range(NKV)]
            kv_free = [None] * NKV
            nchunk = S // KC
            seq = [(h_, c) for (ps_, h_) in VH for c in range(nchunk)]
            kv_tok = {}

            def kv_load(idx):
                h, c = seq[idx]
                s = idx % NKV
                SP.dma(kn[s], jb["kT"][h, :, c * KC:(c + 1) * KC], kv_sem[s], waits=[kv_free[s]])
                krv = jb["krT"][:, c * KC:(c + 1) * KC].rearrange("r (j two t) -> r two j t", two=2, t=128)
                SP.dma(kr[s][0:64, :].rearrange("p (j t) -> p j t", t=128), krv[:, 0], kv_sem[s])
                SP.dma(kr[s][64:128, :].rearrange("p (j t) -> p j t", t=128), krv[:, 1], kv_sem[s])
                kv_tok[idx] = SP.dma(vv[s], jb["v"][h, c], kv_sem[s])

            for idx in range(min(NKV - 1, len(seq))):
                kv_load(idx)

            SB = [0, 1, 2, 3, 4, 5]
            OB = [6, 7]
            for b in range(8):
                bank_busy.discard(b)
            pT_free = [None] * NPB
            q_free = [None, None]
            o_free = [None, None]
            fin_pending = []
            fin_tok = [None, None]
            tile_ctr = [0]
            NQS = NQ // 512
            NPR = KC // 256

            def qproj_steps(vi_):
                ps_, h = VH[vi_]
                s = vi_ % 2
                res = {"toks": []}
                steps = []

                def mk(qs, kind):
                    cols = slice(qs * 512, (qs + 1) * 512)
                    gcols = slice(ps_ * NQ + qs * 512, ps_ * NQ + (qs + 1) * 512)

                    def nope():
                        b, bw = balloc(SB)
                        tk = mm_group(bank(b), [(wuq[:, m, h * QKD:h * QKD + 128], cqn[:, m, gcols])
                                                for m in range(3)], waits=[w_ready, bw])
                        te = ACT.emit(_f_act(qn2[s][:, cols], bank(b), AF.Copy), waits=[tk, q_free[s]])
                        brelease(b, te)
                        res["toks"].append(te)

                    def ropea():
                        b1, bw1 = balloc(SB)
                        tk1 = mm_group(bank(b1), [(wuq_r[:, m, h, :], cqn[:, m, gcols]) for m in range(3)],
                                       waits=[w_ready, bw1])
                        r1 = DVE.emit(_f_tt(qtmp[0], bank(b1), csq[:, 0, gcols], ALU.mult),
                                      waits=[tk1, qproj.last_pool])
                        brelease(b1, r1)
                        res["r1"] = r1

                    def ropeb():
                        b2, bw2 = balloc(SB)
                        tk2 = mm_group(bank(b2), [(wuqp_r[:, m, h, :], cqn[:, m, gcols]) for m in range(3)],
                                       waits=[w_ready, bw2])
                        r2 = DVE.emit(_f_tt(qtmp[1], bank(b2), csq[:, 1, gcols], ALU.mult),
                                      waits=[tk2, qproj.last_pool])
                        brelease(b2, r2)
                        tq = POOL.emit(_f_tt(qr2[s][:, cols], qtmp[0], qtmp[1], ALU.add),
                                       waits=[res["r1"], r2, q_free[s]])
                        qproj.last_pool = tq
                        res["toks"].append(tq)

                    return {"nope": nope, "ropea": ropea, "ropeb": ropeb}[kind]

                for qs in range(NQS):
                    for kind in ("nope", "ropea", "ropeb"):
                        steps.append(mk(qs, kind))
                return steps, res

            def qproj(vi_):
                steps, res = qproj_steps(vi_)
                for f_ in steps:
                    f_()
                return res["toks"]

            qproj.last_pool = None
            qtoks = {0: qproj(0)}

            def finalize(hh, oc0, aD, tD, oc, tO):
                for qs in range(NQS):
                    ocols = slice(oc0 + qs * 512, oc0 + (qs + 1) * 512)
                    b, bw = balloc(SB)
                    tr_ = PE.emit(_f_mm(bank(b), ones_f[:], aD[qs], True, True), waits=[tD[qs], bw])
                    t1 = DVE.emit(_f_recip(rec[qs], bank(b)), waits=[tr_])
                    brelease(b, t1)
                    fin_tok[hp_of[(hh, oc0)]] = DVE.emit(_f_tt(oaT[:, hh, ocols], oc[qs], rec[qs], ALU.mult),
                                               waits=[t1, tO[qs]])

            hp_of = {}
            for vi, (ps, h) in enumerate(VH):
                s = vi % 2
                hp = vi % 2
                oc0 = ps * NQ
                hp_of[(h, oc0)] = hp
                qt = qtoks[vi]
                items = [(c, pr, qs) for c in range(nchunk) for pr in range(NPR) for qs in range(NQS)]
                nit = len(items)
                s_tok = {}
                accD_tok = [None] * NQS
                lastpv = [None] * NQS
                qsteps, qres = [], None

                def issue_S(ii):
                    c, pr, qs = items[ii]
                    li = vi * nchunk + c
                    ks = li % NKV
                    cols = slice(qs * 512, (qs + 1) * 512)
                    bA, bwA = balloc(SB)
                    bB, bwB = balloc(SB)
                    kA, kB = 2 * pr, 2 * pr + 1
                    PE.emit(_f_mm(bank(bA), kn[ks][:, kA * 128:(kA + 1) * 128], qn2[s][:, cols], True, False),
                            waits=[kv_tok[li], qt, bwA], sig=False)
                    PE.emit(_f_mm(bank(bB), kn[ks][:, kB * 128:(kB + 1) * 128], qn2[s][:, cols], True, False),
                            waits=[bwB], sig=False)
                    PE.emit(_f_mm(bank(bA), kr[ks][0:64, pr * 128:(pr + 1) * 128], qr2[s][0:64, cols], False, True),
                            sig=False)
                    tk = PE.emit(_f_mm(bank(bB), kr[ks][64:128, pr * 128:(pr + 1) * 128], qr2[s][64:128, cols],
                                       False, True))
                    s_tok[ii] = (bA, bB, tk)

                issue_S(0)
                for ii in range(nit):
                    c, pr, qs = items[ii]
                    li = vi * nchunk + c
                    ks = li % NKV
                    if ii + 1 < nit:
                        issue_S(ii + 1)
                    if ii == 5 and fin_pending:
                        finalize(*fin_pending.pop())
                    if ii == nit // 2 and vi + 1 < len(VH):
                        qsteps, qres = qproj_steps(vi + 1)
                    if vi + 1 < len(VH) and ii >= nit // 2 and qsteps:
                        qsteps.pop(0)()
                        if not qsteps:
                            qtoks[vi + 1] = qres["toks"]
                    bA, bB, tk = s_tok.pop(ii)
                    first = (c == 0 and pr == 0)
                    last = (c == nchunk - 1) and (pr == NPR - 1)
                    pbs, tes, tps = [], [], []
                    for t_, (b, kb) in enumerate(((bA, 2 * pr), (bB, 2 * pr + 1))):
                        g = tile_ctr[0]
                        tile_ctr[0] += 1
                        pb = g % NPB
                        te = ACT.emit(_f_act(pT[pb], bank(b), AF.Exp, scale=SCALE), waits=[tk, pT_free[pb]])
                        brelease(b, te)
                        st_ = first and t_ == 0
                        en_ = last and t_ == 1
                        tp = PE.emit(_f_mm(bank(OB[qs]), vv[ks][:, kb, :], pT[pb], st_, en_),
                                     waits=[te, o_free[qs]] if st_ else [te])
                        pbs.append(pb)
                        tes.append(te)
                        tps.append(tp)
                    lastpv[qs] = tps[1]
                    tpair = DVE.emit(_f_tt(pair[qs], pT[pbs[0]], pT[pbs[1]], ALU.add),
                                     waits=[tes[0], tes[1], accD_tok[qs]])
                    if accD_tok[qs] is None:
                        ta = DVE.emit(_f_copy(accD[hp][qs], pair[qs]), waits=[tpair, fin_tok[hp]])
                    else:
                        ta = DVE.emit(_f_tt(accD[hp][qs], accD[hp][qs], pair[qs], ALU.add), waits=[tpair])
                    accD_tok[qs] = ta
                    pT_free[pbs[0]] = [tps[0], tpair]
                    pT_free[pbs[1]] = [tps[1], tpair]
                    if pr == NPR - 1 and qs == NQS - 1:
                        kv_free[ks] = tps[1]
                        nxt = li + NKV - 1
                        if nxt < len(seq) and nxt not in kv_tok:
                            kv_load(nxt)
                while qsteps:
                    qsteps.pop(0)()
                    if not qsteps:
                        qtoks[vi + 1] = qres["toks"]
                tO = []
                for qs in range(NQS):
                    c2 = ACT.emit(_f_act(ocp[hp][qs], bank(OB[qs]), AF.Copy), waits=[lastpv[qs], fin_tok[hp]])
                    o_free[qs] = c2
                    tO.append(c2)
                fin_pending.append((h, oc0, accD[hp], accD_tok, ocp[hp], tO))
                if vi == len(VH) - 1:
                    while fin_pending:
                        finalize(*fin_pending.pop(0))
                q_free[s] = PE.tok()
            barrier()

        def phase3(jb, q0, nr):
            xsrc = jb["xqr"] if jb["xqr"] is not None else jb["xk"]
            bp = Bump()
            bp.off = XT_OFF
            xt2 = [bp.get([128, 4, D], F32), bp.get([128, 4, D], F32)]
            bp.off = 0
            xn2 = [bp.get([128, D], BF16), bp.get([128, D], BF16)]
            junk = bp.get([128, D], BF16)
            hnTs = [bp.get([128, 8, MT], BF16), bp.get([128, 8, MT], BF16)]
            NW = 5
            wr = [bp.get([128, 8, 512], BF16) for _ in range(NW)]
            tf = [bp.get([128, 512], F32) for _ in range(6)]
            ma4 = bp.get([128, 4, 512], F32)
            u_off = bp.off
            vg = bp.get([128, D], F32)
            vsn = bp.get([128, 4, D], BF16)
            uT = bp.get([128, 8, MT], BF16)
            end1 = bp.off
            bp.off = u_off
            h1T = bp.get([128, 32, MT], BF16)
            bp.off = max(bp.off, end1)
            assert bp.off <= XT_OFF, bp.off

            n_mt = nr // MT
            x_sem = x_sem_p
            y_sem = y_sem_p
            w_sem = [new_dsem() for _ in range(NW)]
            w_free = [None] * NW
            xt_free = xt_free_p
            stt_ = {"junk": junk, "xn_free": [None, None], "hnT_free": None}
            hn_free = [None, None]
            ytoks = []

            def piece_list():
                L = []
                for i in range(2):
                    L.append(("v", win_bf[:, V0 + i * 512:V0 + (i + 1) * 512]))
                for i in range(2):
                    L.append(("u", win_bf[:, U0 + i * 512:U0 + (i + 1) * 512]))
                for i in range(2):
                    L.append(("ga", win_bf[:, GA0 + i * 512:GA0 + (i + 1) * 512]))
                    L.append(("gb", win_bf[:, GB0 + i * 512:GB0 + (i + 1) * 512]))
                for i in range(2):
                    L.append(("wo", wo_bf[:, i * 512:(i + 1) * 512]))
                for i in range(8):
                    L.append(("f1", wff1_bf[:, i * 512:(i + 1) * 512]))
                for half in range(2):
                    for g4 in range(4):
                        L.append(("f2", wff2_bf[g4 * 1024:(g4 + 1) * 1024, half * 512:(half + 1) * 512]))
                return L

            pieces = []
            for mt in range(n_mt):
                pieces += piece_list()
            w_tok = {}
            w_next = [0]

            def w_issue():
                i = w_next[0]
                if i >= len(pieces):
                    return
                s = i % NW
                w_tok[i] = SP.dma(wr[s], pieces[i][1].rearrange("(kc p) c -> p kc c", p=128), w_sem[s],
                                  waits=[w_free[s]])
                w_next[0] += 1

            def w_done(i, tok):
                w_free[i % NW] = tok
                w_issue()

            def xload(mt):
                s = mt % 2
                r0 = q0 + mt * MT
                return SP.dma(xt2[s], xsrc[r0:r0 + MT, :].rearrange("(j p) d -> p j d", p=128), x_sem[s],
                              waits=[xt_free[s]])

            xtok = {}
            if (jb["name"], q0) in x_pref:
                xtok = x_pref.pop((jb["name"], q0))
            if 0 not in xtok:
                xtok[0] = xload(0)
            for _ in range(NW - 1):
                w_issue()
            if n_mt > 1 and 1 not in xtok:
                xtok[1] = xload(1)
            w_issue()
            th_next = None
            hn_sem = [new_dsem(), new_dsem()]

            def hn_load(mt_):
                return SP.dma(hnTs[mt_ % 2], jb["hn"][(q0 + mt_ * MT) // MT], hn_sem[mt_ % 2],
                              waits=[hn_free[mt_ % 2]])

            def rms_rows(src3, g_bc, dst_fn, wait):
                tks = []
                for j in range(4):
                    tks.append(ACT.emit(_f_act_acc(junk, src3(j), AF.Square, ss4[:, j:j + 1]),
                                        waits=[wait, stt_.get("ss_free")]))
                t1 = DVE.emit(_f_ts(t4[:], ss4[:], 1.0 / D, EPS, ALU.mult, ALU.add), waits=[tks[-1]])
                stt_["ss_free"] = t1
                t2 = POOL.emit(_f_tt(rstd4[:], t4[:], mhalf[:, 0:4], ALU.pow), waits=[t1])
                tk = None
                for j in range(4):
                    tk = DVE.emit(_f_stt(dst_fn(j), src3(j), rstd4[:, j:j + 1], g_bc[:], ALU.mult, ALU.mult),
                                  waits=[t2])
                return tk

            tf_free = [None] * 6
            ma4_free = [None] * 4

            def gelu_from_psum(b, tk, out_ap, k):
                a1 = ACT.emit(_f_act(tf[k], bank(b), AF.Square), waits=[tk, tf_free[k]])
                d1 = DVE.emit(_f_ts(tf[k], tf[k], GC1, 1.0, ALU.mult, ALU.add), waits=[a1])
                d2 = DVE.emit(_f_tt(tf[k], tf[k], bank(b), ALU.mult), waits=[d1])
                a2 = ACT.emit(_f_act(tf[k + 1], tf[k], AF.Sigmoid, scale=GC2), waits=[d2, tf_free[k + 1]])
                d3 = DVE.emit(_f_tt(out_ap, tf[k + 1], bank(b), ALU.mult), waits=[a2])
                tf_free[k] = a2
                tf_free[k + 1] = d3
                return d3

            pi = [0]
            for mt in range(n_mt):
                s = mt % 2
                xt = xt2[s]
                qc = slice(mt * MT, (mt + 1) * MT)
                hnT = hnTs[mt % 2]
                if th_next is None:
                    th = hn_load(mt)
                else:
                    th = th_next
                    th_next = None
                pv0, pv1 = pi[0], pi[0] + 1
                pi[0] += 2
                last = None
                for j in range(4):
                    for half in range(2):
                        p_ = pv0 + half
                        b, bw = balloc()
                        tk = mm_group(bank(b), [(hnT[:, kc, j * 128:(j + 1) * 128], wr[p_ % NW][:, kc, :])
                                                for kc in range(8)], waits=[th, w_tok[p_], bw])
                        last = tk
                        d3 = ACT.emit(_f_act(vg[:, half * 512:(half + 1) * 512], bank(b), AF.Gelu_apprx_tanh),
                                      waits=[tk, stt_.get("vg_free")])
                        brelease(b, d3)
                    tsq = ACT.emit(_f_act_acc(junk, vg[:], AF.Square, ss4[:, 0:1]), waits=[d3, stt_.get("ss_free")])
                    t1 = DVE.emit(_f_ts(t4[:, 0:1], ss4[:, 0:1], 1.0 / D, EPS, ALU.mult, ALU.add), waits=[tsq])
                    stt_["ss_free"] = t1
                    t2 = POOL.emit(_f_tt(rstd4[:, 0:1], t4[:, 0:1], mhalf[:, 0:1], ALU.pow), waits=[t1])
                    tvs = DVE.emit(_f_stt(vsn[:, j, :], vg[:], rstd4[:, 0:1], g_sgu[:], ALU.mult, ALU.mult),
                                   waits=[t2])
                    stt_["vg_free"] = tvs
                w_done(pv0, last)
                w_done(pv1, last)
                for i in range(2):
                    p_ = pi[0]
                    pi[0] += 1
                    for mm in range(4):
                        m = i * 4 + mm
                        b, bw = balloc()
                        tk = mm_group(bank(b), [(wr[p_ % NW][:, kc, mm * 128:(mm + 1) * 128], hnT[:, kc, :])
                                                for kc in range(8)], waits=[w_tok[p_], bw])
                        d3 = ACT.emit(_f_act(uT[:, m, :], bank(b), AF.Gelu_apprx_tanh), waits=[tk])
                        brelease(b, d3)
                    w_done(p_, tk)
                tu = d3
                for g in range(8):
                    b, bw = balloc()
                    tk = None
                    for j in range(4):
                        tk = PE.emit(_f_mm(bank(b)[:, j * 128:(j + 1) * 128], vsn[:, j, g * 128:(g + 1) * 128],
                                           wsT[:, g, :], True, True), waits=[tvs, bw] if j == 0 else ())
                    k = 4 + (g % 2)
                    d1 = DVE.emit(_f_tt(tf[k].rearrange("p (j t) -> p j t", t=128),
                                        bank(b).rearrange("p (j t) -> p j t", t=128),
                                        bs_bc[:, g:g + 1, :].broadcast_to([128, 4, 128]), ALU.add),
                                  waits=[tk, tf_free[k]])
                    brelease(b, d1)
                    tob = DVE.emit(_f_tt(uT[:, g, :], uT[:, g, :], tf[k], ALU.mult), waits=[d1, tu])
                    tf_free[k] = tob
                for i in range(2):
                    pa, pb_ = pi[0], pi[0] + 1
                    pi[0] += 2
                    for mm in range(4):
                        m = i * 4 + mm
                        b, bw = balloc()
                        tk = mm_group(bank(b), [(wr[pa % NW][:, kc, mm * 128:(mm + 1) * 128], hnT[:, kc, :])
                                                for kc in range(8)], waits=[w_tok[pa], bw])
                        k = mm % 2
                        a1 = ACT.emit(_f_act(tf[k], bank(b), AF.Sigmoid), waits=[tk, tf_free[k]])
                        brelease(b, a1)
                        d1 = POOL.emit(_f_tt(ma4[:, mm, :], tf[k], oaT[:, m, qc], ALU.mult),
                                       waits=[a1, ma4_free[mm]])
                        tf_free[k] = d1
                        ma_tok = d1
                    w_done(pa, tk)
                    for mm in range(4):
                        m = i * 4 + mm
                        b, bw = balloc()
                        tk = mm_group(bank(b), [(wr[pb_ % NW][:, kc, mm * 128:(mm + 1) * 128], hnT[:, kc, :])
                                                for kc in range(8)], waits=[w_tok[pb_], bw])
                        k = 2 + mm % 2
                        a1 = ACT.emit(_f_act(tf[k], bank(b), AF.Sigmoid), waits=[tk, tf_free[k]])
                        brelease(b, a1)
                        d1 = DVE.emit(_f_tt(tf[k], tf[k], uT[:, m, :], ALU.mult), waits=[a1, tob])
                        tmg = DVE.emit(_f_tt(uT[:, m, :], tf[k], ma4[:, mm, :], ALU.add), waits=[d1, ma_tok])
                        tf_free[k] = tmg
                        ma4_free[mm] = tmg
                    w_done(pb_, tk)
                hn_free[mt % 2] = PE.tok()
                for half in range(2):
                    p_ = pi[0]
                    pi[0] += 1
                    for j in range(4):
                        b, bw = balloc()
                        tk = mm_group(bank(b), [(uT[:, kc, j * 128:(j + 1) * 128], wr[p_ % NW][:, kc, :])
                                                for kc in range(8)], waits=[tmg, w_tok[p_], bw])
                        tx1 = DVE.emit(_f_tt(xt[:, j, half * 512:(half + 1) * 512],
                                             xt[:, j, half * 512:(half + 1) * 512], bank(b), ALU.add),
                                       waits=[tk, xtok[mt]])
                        brelease(b, tx1)
                    w_done(p_, tk)
                stt_["hnT_free"] = hn_free[mt % 2]
                th2 = norm_transpose(xt, tx1, g_ffn, xn2, hnT, stt_)
                for i in range(8):
                    p_ = pi[0]
                    pi[0] += 1
                    for mm in range(4):
                        m = i * 4 + mm
                        b, bw = balloc()
                        tk = mm_group(bank(b), [(wr[p_ % NW][:, kc, mm * 128:(mm + 1) * 128], hnT[:, kc, :])
                                                for kc in range(8)], waits=[th2, w_tok[p_], bw])
                        k = m % 4
                        a1 = ACT.emit(_f_act(tf[k], bank(b), AF.Relu), waits=[tk, tf_free[k]])
                        brelease(b, a1)
                        th1 = POOL.emit(_f_tt(h1T[:, m, :], tf[k], tf[k], ALU.mult), waits=[a1])
                        tf_free[k] = th1
                    w_done(p_, tk)
                hn_free[mt % 2] = PE.tok()
                if mt + 2 < n_mt and (mt + 2) not in xtok:
                    pass
                if mt + 1 < n_mt:
                    th_next = hn_load(mt + 1)
                for half in range(2):
                    bks = []
                    for j in range(4):
                        b, bw = balloc()
                        bks.append((b, bw))
                    tk = None
                    for g4 in range(4):
                        p_ = pi[0]
                        pi[0] += 1
                        for j in range(4):
                            b, bw = bks[j]
                            for kc in range(8):
                                tk = PE.emit(_f_mm(bank(b), h1T[:, g4 * 8 + kc, j * 128:(j + 1) * 128],
                                                   wr[p_ % NW][:, kc, :], g4 == 0 and kc == 0, g4 == 3 and kc == 7),
                                             waits=[th1, w_tok[p_], bw] if kc == 0 else (),
                                             sig=(kc == 7))
                            if g4 == 3:
                                tx2 = DVE.emit(_f_tt(xt[:, j, half * 512:(half + 1) * 512],
                                                     xt[:, j, half * 512:(half + 1) * 512], bank(b), ALU.add),
                                               waits=[tk])
                                brelease(b, tx2)
                        w_done(p_, tk)
                tfin = rms_rows(lambda j: xt[:, j, :], g_fin, lambda j: xt[:, j, :], tx2)
                r0 = q0 + mt * MT
                ty = POOL.dma(jb["y"][r0:r0 + MT, :].rearrange("(j p) d -> p j d", p=128), xt, y_sem[s], waits=[tfin])
                xt_free[s] = ty
                ytoks.append(ty)
                if mt + 2 < n_mt:
                    xtok[mt + 2] = xload(mt + 2)
            barrier()
            return ytoks

        def prefetch_x(jb, q0, nr):
            xsrc = jb["xqr"] if jb["xqr"] is not None else jb["xk"]
            toks = {}
            for mt in range(min(2, nr // MT)):
                s_ = mt % 2
                r0 = q0 + mt * MT
                toks[mt] = SP.dma(arena_xt(s_), xsrc[r0:r0 + MT, :].rearrange("(j p) d -> p j d", p=128),
                                  x_sem_p[s_], waits=[xt_free_p[s_]])
            x_pref[(jb["name"], q0)] = toks

        def arena_xt(s_):
            bpx = Bump()
            bpx.off = XT_OFF + s_ * 4 * D * 4
            return bpx.get([128, 4, D], F32)

        all_y = []
        pending_y = []
        for jb in jobs:
            S = jb["S"]
            if pending_y:
                barrier(pending_y)
                pending_y = []
            if jb["xqr"] is None:
                phase1(jb["xk"], S // MT, True, True, jb["cosk"], jb["sink"], jb, 0)
            else:
                phase1(jb["xk"], S // MT, True, False, jb["cosk"], jb["sink"], jb, 0)
                phase1(jb["xqr"], jb["nq"] // MT, False, True, None, None, jb, 0)
            nr = min(NR, jb["nq"])
            for q0 in range(0, jb["nq"], nr):
                prefetch_x(jb, q0, nr)
                phase2(jb, q0, nr // NQ)
                ys_ = phase3(jb, q0, nr)
                all_y += ys_
                pending_y = ys_[-2:]
        barrier(all_y)

        with nc.Block() as block:
            @block.sync
            def _(e):
                SP.replay(e)

            @block.tensor
            def _(e):
                PE.replay(e)

            @block.vector
            def _(e):
                DVE.replay(e)

            @block.scalar
            def _(e):
                ACT.replay(e)

            @block.gpsimd
            def _(e):
                POOL.replay(e)
    return nc


def _rope_tables(n):
    pos = np.arange(n, dtype=np.float32)
    inv = (np.float32(ROPE_BASE) ** (-np.arange(0, RD, 2, dtype=np.float32) / np.float32(RD))).astype(np.float32)
    ang = pos[:, None] * inv[None, :]
    ang = np.concatenate([ang, ang], axis=-1)
    cos = np.cos(ang).astype(np.float32)
    sin = np.sin(ang).astype(np.float32)
    sgn = np.concatenate([-np.ones(RD // 2, np.float32), np.ones(RD // 2, np.float32)])
    return np.ascontiguousarray(cos.T), np.ascontiguousarray((sin * sgn[None, :]).T)


def run(cfg, n_cores, x_prompt, x_sample, w):
    nc = build(cfg)
    smax = max(cfg.ss, cfg.sp)
    cosT, sinT = _rope_tables(smax)
    in_maps = []
    for c in range(n_cores):
        q0 = c * cfg.nqp
        m = {
            "xs": np.ascontiguousarray(x_sample[c * cfg.nseq:(c + 1) * cfg.nseq].reshape(cfg.nseq * cfg.ss, D)),
            "xp": x_prompt,
            "xq": np.ascontiguousarray(x_prompt[q0:q0 + cfg.nqp]),
            "cos_all": cosT, "sin_all": sinT,
            "cos_q": np.ascontiguousarray(cosT[:, q0:q0 + cfg.nqp]),
            "sin_q": np.ascontiguousarray(sinT[:, q0:q0 + cfg.nqp]),
        }
        m.update(w)
        in_maps.append(m)
    res = run_bass_kernel_spmd(nc, in_maps, core_ids=list(range(n_cores)))
    ysam = np.stack([r["ys"].reshape(cfg.nseq, cfg.ss, D) for r in res.results], 0).reshape(-1, cfg.ss, D)
    yp = np.concatenate([r["yq"] for r in res.results], 0)
    return yp, ysam


def _weights(norm_mix_g, w_in, q_norm_g, w_uq, kv_norm_g, w_ukv, sgu_norm_g, w_s, b_s, w_o, norm_ffn_g,
             w_ff1, w_ff2, final_norm_g):
    f = lambda a: np.ascontiguousarray(np.asarray(a, dtype=np.float32))
    return {
        "norm_mix_g": f(norm_mix_g[0]), "w_in": f(w_in[0]), "q_norm_g": f(q_norm_g[0]), "w_uq": f(w_uq[0]),
        "kv_norm_g": f(kv_norm_g[0]), "w_ukv": f(w_ukv[0]), "sgu_norm_g": f(sgu_norm_g[0]), "w_s": f(w_s[0]),
        "b_s": f(b_s[0]).reshape(-1), "w_o": f(w_o[0]), "norm_ffn_g": f(norm_ffn_g[0]), "w_ff1": f(w_ff1[0]),
        "w_ff2": f(w_ff2[0]), "final_norm_g": f(final_norm_g),
    }


def kernel(x_prompt, x_sample, norm_mix_g, w_in, q_norm_g, w_uq, kv_norm_g, w_ukv, sgu_norm_g, w_s, b_s, w_o,
           norm_ffn_g, w_ff1, w_ff2, final_norm_g):
    w = _weights(norm_mix_g, w_in, q_norm_g, w_uq, kv_norm_g, w_ukv, sgu_norm_g, w_s, b_s, w_o, norm_ffn_g,
                 w_ff1, w_ff2, final_norm_g)
    xp = np.ascontiguousarray(np.asarray(x_prompt, dtype=np.float32)[0])
    xsam = np.ascontiguousarray(np.asarray(x_sample, dtype=np.float32))
    yp, ysam = run(FULL, 8, xp, xsam, w)
    return (yp.reshape(1, FULL.sp, D).astype(np.float32), ysam.reshape(16, FULL.ss, D).astype(np.float32))
```
